# Optimizing a Trainium2 kernel written in Bass

```python
import jax, jax.numpy as jnp
from jax import lax
import numpy as np

D_MODEL = 2048
BATCH = 4
SEQ = 4096
DEPTH = 4

CTX_LEN = 256
GRID_W = 64
EPS = 1e-6
F_MIN = 1e-30
N_MOD = 6

HEAD_DIM = 128
ATTN_HEADS = 8
ATTN_KV_HEADS = 2
ATTN_GROUP = ATTN_HEADS // ATTN_KV_HEADS
ATTN_Q_BLOCK = 128
ROPE_THETA = 10000.0
ATTN_Q_W = ATTN_HEADS * HEAD_DIM
ATTN_KV_W = ATTN_KV_HEADS * HEAD_DIM

HG_HEADS = 4
HG_DK = 128
HG_DV = 128
HG_CHUNK = 16
HG_K_W = HG_HEADS * HG_DK
HG_V_W = HG_HEADS * HG_DV

SG_GROUPS = 4
SG_DIM = 128
SG_CHUNK = 128
SG_W = SG_GROUPS * SG_DIM

D_MIX = ATTN_Q_W + HG_V_W + SG_W
IN_SIZES = (ATTN_Q_W, ATTN_KV_W, ATTN_KV_W, HG_K_W, HG_K_W, HG_K_W, HG_V_W, HG_V_W, SG_W, SG_W)
IN_COLS = 5120

D_FF = 5632
CONV_W = 3

kernel_name = "hybrid_parallel_mixer_dit_block"


def rms_norm(x, g):
    xf = x.astype(jnp.float32)
    y = xf * lax.rsqrt(jnp.mean(xf * xf, axis=-1, keepdims=True) + EPS)
    return (y * g.astype(jnp.float32)).astype(x.dtype)


def axial_rope_tables(n_tokens):
    rows = n_tokens // GRID_W
    row = jnp.repeat(jnp.arange(rows, dtype=jnp.float32), GRID_W)
    col = jnp.tile(jnp.arange(GRID_W, dtype=jnp.float32), rows)
    n_freq = HEAD_DIM // 4
    inv = ROPE_THETA ** (-jnp.arange(n_freq, dtype=jnp.float32) / n_freq)
    ang = jnp.concatenate([row[:, None] * inv, col[:, None] * inv], axis=-1)
    return jnp.cos(ang), jnp.sin(ang)


def apply_rope(x, cos, sin):
    xf = x.astype(jnp.float32).reshape(*x.shape[:-1], HEAD_DIM // 2, 2)
    x1, x2 = xf[..., 0], xf[..., 1]
    cs, sn = cos[None, :, None, :], sin[None, :, None, :]
    out = jnp.stack([x1 * cs - x2 * sn, x1 * sn + x2 * cs], axis=-1)
    return out.reshape(x.shape).astype(x.dtype)


def gqa_softmax(q, k, v):
    s = jnp.einsum('bqkgd,bskd->bkgqs', q, k).astype(jnp.float32) * (HEAD_DIM ** -0.5)
    p = jax.nn.softmax(s, axis=-1).astype(v.dtype)
    return jnp.einsum('bkgqs,bskd->bqkgd', p, v)


def latent_attention(q, k_all, v_all):
    B, T, H, Dh = q.shape
    nb = T // ATTN_Q_BLOCK
    qb = q.reshape(B, nb, ATTN_Q_BLOCK, ATTN_KV_HEADS, ATTN_GROUP, Dh).transpose(1, 0, 2, 3, 4, 5)
    o = lax.map(lambda qblk: gqa_softmax(qblk, k_all, v_all), qb)
    return o.transpose(1, 0, 2, 3, 4, 5).reshape(B, T, H * Dh)


def context_attention(q, k, v):
    B, L, H, Dh = q.shape
    o = gqa_softmax(q.reshape(B, L, ATTN_KV_HEADS, ATTN_GROUP, Dh), k, v)
    return o.reshape(B, L, H * Dh)


def hgrn_lower_bounds(lb_param):
    p = jax.nn.softmax(lb_param.astype(jnp.float32), axis=1)
    return jnp.cumsum(p, axis=1) - p[:, :1]


def hgrn2_gates(f_raw, lb):
    z = f_raw.astype(jnp.float32)
    lb = lb.reshape(HG_HEADS, HG_DK)
    f = lb + (1.0 - lb) * jax.nn.sigmoid(z)
    log_f = jnp.log(jnp.maximum(f, F_MIN))
    k = (1.0 - lb) * jax.nn.sigmoid(-z)
    return log_f, k


def gla_chunkwise(q, k, v, log_f, s0):
    B, T, H, K = q.shape
    V = v.shape[-1]
    C = HG_CHUNK
    N = T // C
    q, k, log_f = [a.reshape(B, N, C, H, K) for a in (q, k, log_f)]
    v = v.reshape(B, N, C, H, V)
    b = jnp.cumsum(log_f, axis=2)
    tri = jnp.tril(jnp.ones((C, C), dtype=bool))[None, None, :, :, None, None]
    diff = b[:, :, :, None] - b[:, :, None, :]
    decay = jnp.where(tri, jnp.exp(jnp.where(tri, diff, 0.0)), 0.0)
    scores = jnp.einsum('bnthk,bnshk,bntshk->bnhts', q, k, decay)
    o_intra = jnp.einsum('bnhts,bnshv->bnthv', scores, v)
    b_last = b[:, :, -1]
    q_dec = q * jnp.exp(b)
    k_dec = k * jnp.exp(b_last[:, :, None] - b)

    def step(S, xs):
        qd, kd, vv, dl = xs
        o = jnp.einsum('bthk,bhkv->bthv', qd, S)
        S = S * dl[..., None] + jnp.einsum('bthk,bthv->bhkv', kd, vv)
        return S, o

    xs = tuple(jnp.moveaxis(a, 1, 0) for a in (q_dec, k_dec, v, jnp.exp(b_last)))
    s_fin, o_inter = lax.scan(step, s0, xs)
    o = o_intra + jnp.moveaxis(o_inter, 0, 1)
    return o.reshape(B, T, H, V), s_fin


def hgrn2_direction(q_l, i_l, f_l, q_c, i_c, f_c, lb):
    B = q_l.shape[0]
    s0 = jnp.zeros((B, HG_HEADS, HG_DK, HG_DV), jnp.float32)
    logf_c, k_c = hgrn2_gates(f_c, lb)
    o_c, s_ctx = gla_chunkwise(q_c, k_c, i_c, logf_c, s0)
    logf_l, k_l = hgrn2_gates(f_l, lb)
    o_l, _ = gla_chunkwise(q_l, k_l, i_l, logf_l, s_ctx)
    return o_l, o_c


def spatial_gating(u, v, norm_g, w_s, b_s):
    B, T, _ = u.shape
    N = T // SG_CHUNK
    u = jax.nn.gelu(u)
    v = rms_norm(jax.nn.gelu(v).reshape(B, T, SG_GROUPS, SG_DIM), norm_g.reshape(SG_GROUPS, SG_DIM))
    v = v.reshape(B, N, SG_CHUNK, SG_GROUPS, SG_DIM)
    mixed = jnp.einsum('gts,bnsgd->bntgd', w_s, v) + b_s.T[:, :, None]
    return u * mixed.reshape(B, T, SG_W)


def conv_ffn(h, w_up, conv_w, conv_b, w_down):
    T = h.shape[1]
    up = h @ w_up
    pad = jnp.pad(up, ((0, 0), (CONV_W // 2, CONV_W // 2), (0, 0)))
    y = conv_b + sum(pad[:, j:j + T] * conv_w[j] for j in range(CONV_W))
    gate, val = jnp.split(y, 2, axis=-1)
    return (jax.nn.silu(gate) * val) @ w_down


def token_mixers(h, hc, w_in, q_g, k_g, lb_f, lb_b, hg_g, sg_g, sg_w, sg_b, cos, sin, need_ctx):
    B, T, _ = h.shape
    L = hc.shape[1]
    splits = np.cumsum(IN_SIZES)[:-1].tolist()
    (aq, ak, av, hq, hff, hfb, hi, hgt, su, sv) = jnp.split(h @ w_in, splits, axis=-1)
    (aqc, akc, avc, hqc, hffc, hfbc, hic, hgtc, suc, svc) = jnp.split(hc @ w_in, splits, axis=-1)

    q = apply_rope(rms_norm(aq.reshape(B, T, ATTN_HEADS, HEAD_DIM), q_g), cos, sin)
    k = apply_rope(rms_norm(ak.reshape(B, T, ATTN_KV_HEADS, HEAD_DIM), k_g), cos, sin)
    v = av.reshape(B, T, ATTN_KV_HEADS, HEAD_DIM)
    kc = rms_norm(akc.reshape(B, L, ATTN_KV_HEADS, HEAD_DIM), k_g)
    vc = avc.reshape(B, L, ATTN_KV_HEADS, HEAD_DIM)
    attn = latent_attention(q, jnp.concatenate([k, kc], axis=1), jnp.concatenate([v, vc], axis=1))

    def heads(a, d):
        return a.astype(jnp.float32).reshape(a.shape[0], a.shape[1], -1, d)
    flip = lambda a: jnp.flip(a, axis=1)
    q_l, i_l = jax.nn.silu(heads(hq, HG_DK)), heads(hi, HG_DV)
    q_c, i_c = jax.nn.silu(heads(hqc, HG_DK)), heads(hic, HG_DV)
    ff_l, fb_l, ff_c, fb_c = heads(hff, HG_DK), heads(hfb, HG_DK), heads(hffc, HG_DK), heads(hfbc, HG_DK)
    o_f, oc_f = hgrn2_direction(q_l, i_l, ff_l, q_c, i_c, ff_c, lb_f)
    o_b, oc_b = hgrn2_direction(flip(q_l), flip(i_l), flip(fb_l), flip(q_c), flip(i_c), flip(fb_c), lb_b)
    hg_out = (rms_norm(o_f + flip(o_b), hg_g).reshape(B, T, HG_V_W)
              * jax.nn.silu(hgt.astype(jnp.float32))).astype(h.dtype)

    sg = spatial_gating(su, sv, sg_g, sg_w, sg_b)

    mix = jnp.concatenate([attn, hg_out, sg], axis=-1)
    if not need_ctx:
        return mix, None
    qc = rms_norm(aqc.reshape(B, L, ATTN_HEADS, HEAD_DIM), q_g)
    attn_c = context_attention(qc, kc, vc)
    hg_c = (rms_norm(oc_f + flip(oc_b), hg_g).reshape(B, L, HG_V_W)
            * jax.nn.silu(hgtc.astype(jnp.float32))).astype(hc.dtype)
    sg_c = spatial_gating(suc, svc, sg_g, sg_w, sg_b)
    mix_c = jnp.concatenate([attn_c, hg_c, sg_c], axis=-1)
    return mix, mix_c


def setup_inputs(seed: int = 0) -> dict:
    key = jax.random.key(seed)
    ks = jax.random.split(key, 24)
    nrm = lambda k, shape, s: jax.random.normal(k, shape, jnp.float32) * s
    L = DEPTH
    return {
        "x": nrm(ks[0], (BATCH, SEQ, D_MODEL), 1.0),
        "c": nrm(ks[1], (BATCH, D_MODEL), 1.0),
        "ctx": nrm(ks[2], (BATCH, CTX_LEN, D_MODEL), 1.0),
        "c_ctx": nrm(ks[3], (D_MODEL,), 1.0),
        "w_ada": nrm(ks[4], (L, D_MODEL, N_MOD * D_MODEL), 0.5 * D_MODEL ** -0.5),
        "b_ada": nrm(ks[5], (L, N_MOD * D_MODEL), 0.02),
        "norm1_g": 1.0 + nrm(ks[6], (L, D_MODEL), 0.02),
        "w_in": nrm(ks[7], (L, D_MODEL, IN_COLS), D_MODEL ** -0.5),
        "q_norm_g": 1.0 + nrm(ks[8], (L, HEAD_DIM), 0.02),
        "k_norm_g": 1.0 + nrm(ks[9], (L, HEAD_DIM), 0.02),
        "hg_lower_bounds": nrm(ks[10], (2, L, HG_K_W), 0.1),
        "hg_norm_g": 1.0 + nrm(ks[11], (L, HG_DV), 0.02),
        "sg_norm_g": 1.0 + nrm(ks[12], (L, SG_W), 0.02),
        "sg_w": nrm(ks[13], (L, SG_GROUPS, SG_CHUNK, SG_CHUNK), SG_CHUNK ** -0.5),
        "sg_b": 1.0 + nrm(ks[14], (L, SG_GROUPS, SG_CHUNK), 0.02),
        "w_out": nrm(ks[15], (L, D_MIX, D_MODEL), D_MIX ** -0.5),
        "norm2_g": 1.0 + nrm(ks[16], (L, D_MODEL), 0.02),
        "w_up": nrm(ks[17], (L, D_MODEL, 2 * D_FF), D_MODEL ** -0.5),
        "conv_w": nrm(ks[18], (L, CONV_W, 2 * D_FF), CONV_W ** -0.5),
        "conv_b": nrm(ks[19], (L, 2 * D_FF), 0.02),
        "w_down": nrm(ks[20], (L, D_FF, D_MODEL), D_FF ** -0.5),
        "final_norm_g": 1.0 + nrm(ks[21], (D_MODEL,), 0.02),
    }


def reference(x, c, ctx, c_ctx, w_ada, b_ada, norm1_g, w_in, q_norm_g, k_norm_g,
              hg_lower_bounds, hg_norm_g, sg_norm_g, sg_w, sg_b, w_out,
              norm2_g, w_up, conv_w, conv_b, w_down, final_norm_g):
    B, T, _ = x.shape
    cos, sin = axial_rope_tables(T)
    lbs = hgrn_lower_bounds(hg_lower_bounds)
    silu_c = jax.nn.silu(c)
    silu_cc = jax.nn.silu(c_ctx)
    cx = ctx
    for l in range(DEPTH):
        need_ctx = l < DEPTH - 1
        mod = (silu_c @ w_ada[l] + b_ada[l]).reshape(B, N_MOD, 1, D_MODEL)
        mod_c = (silu_cc @ w_ada[l] + b_ada[l]).reshape(N_MOD, D_MODEL)

        h = rms_norm(x, norm1_g[l]) * (1.0 + mod[:, 1]) + mod[:, 0]
        hc = rms_norm(cx, norm1_g[l]) * (1.0 + mod_c[1]) + mod_c[0]
        mix, mix_c = token_mixers(h, hc, w_in[l], q_norm_g[l], k_norm_g[l], lbs[0, l], lbs[1, l],
                                  hg_norm_g[l], sg_norm_g[l], sg_w[l], sg_b[l], cos, sin, need_ctx)
        x = x + mod[:, 2] * (mix @ w_out[l])

        h2 = rms_norm(x, norm2_g[l]) * (1.0 + mod[:, 4]) + mod[:, 3]
        x = x + mod[:, 5] * conv_ffn(h2, w_up[l], conv_w[l], conv_b[l], w_down[l])

        if need_ctx:
            cx = cx + mod_c[2] * (mix_c @ w_out[l])
            hc2 = rms_norm(cx, norm2_g[l]) * (1.0 + mod_c[4]) + mod_c[3]
            cx = cx + mod_c[5] * conv_ffn(hc2, w_up[l], conv_w[l], conv_b[l], w_down[l])
    return rms_norm(x, final_norm_g)
```

```python
import contextlib
import numpy as np
import ml_dtypes
import concourse.bass as bass
import concourse.mybir as mybir
from concourse.bass_utils import run_bass_kernel_spmd

F32 = mybir.dt.float32
BF16 = mybir.dt.bfloat16
AF = mybir.ActivationFunctionType
ALU = mybir.AluOpType
AX = mybir.AxisListType

import os
SEM_LIMIT = int(os.environ.get("SEM_LIMIT", "16000"))
D = 2048
KC = 16
T = 4096
LC = 256
U = T + LC
DFF = 5632
FC = 44
EPS = 1e-6
F_MIN = 1e-30
NB = 256
FW = 456
N_CORES = 4


class Res:
    __slots__ = ("name", "writers", "readers", "excl")

    def __init__(self, name, excl=False):
        self.name = name
        self.writers = {}
        self.readers = {}
        self.excl = excl


class Stream:
    def __init__(self, name, nslots):
        self.name = name
        self.slots = [[None, 0] for _ in range(nslots)]
        self.i = 0


def _merge(d, s, v):
    if d.get(s, 0) < v:
        d[s] = v


class Prog:
    ENG = ("pe", "act", "dve", "pool", "sp")

    def __init__(self, nc, stack):
        self.nc = nc
        self.stack = stack
        self.ops = {e: [] for e in self.ENG}
        self.esem = {e: None for e in self.ENG}
        self.ecnt = {e: 0 for e in self.ENG}
        self.known = {e: {} for e in self.ENG}
        self.nsem = 0
        self.nres = 0
        self.esems = {e: set() for e in self.ENG}
        self.streams = []
        self.nops = 0

    def new_sem(self, name):
        self.nsem += 1
        return self.stack.enter_context(self.nc.semaphore(f"{name}_{self.nsem}"))

    def res(self, name=None, excl=False):
        self.nres += 1
        return Res(name or f"r{self.nres}", excl)

    def stream(self, name, nslots):
        st = Stream(name, nslots)
        self.streams.append(st)
        return st

    def sb(self, name, shape, dtype):
        return self.stack.enter_context(self.nc.sbuf_tensor(name, list(shape), dtype))

    def ps(self, name, shape, dtype=F32):
        return self.stack.enter_context(self.nc.psum_tensor(name, list(shape), dtype))

    def _waits(self, eng, deps):
        waits = {}
        kn = self.known[eng]
        own = self.esems[eng]
        for (s, v) in deps:
            if kn.get(s, 0) >= v:
                continue
            if eng == "pe" and s in own:
                continue
            _merge(waits, s, v)
        for s, v in waits.items():
            kn[s] = v
        return list(waits.items())

    def _deps(self, reads, writes, join):
        deps = []
        for r in reads:
            deps.extend(r.writers.items())
        for r in writes:
            if join and not r.readers:
                continue
            deps.extend(r.writers.items())
            deps.extend(r.readers.items())
        return deps

    def _post(self, ev, reads, writes, join):
        s, v = ev
        for r in reads:
            _merge(r.readers, s, v)
        for r in writes:
            if join and not r.readers:
                _merge(r.writers, s, v)
            else:
                r.writers = {s: v}
                r.readers = {}

    def op(self, eng, fn, reads=(), writes=()):
        ex = [r for r in reads if r.excl]
        if ex:
            reads = [r for r in reads if not r.excl]
            writes = list(writes) + ex
        waits = self._waits(eng, self._deps(reads, writes, False))
        if self.esem[eng] is None or self.ecnt[eng] >= SEM_LIMIT:
            self.esem[eng] = self.new_sem(eng)
            self.esems[eng].add(self.esem[eng])
            self.ecnt[eng] = 0
        self.ecnt[eng] += 1
        ev = (self.esem[eng], self.ecnt[eng])
        self.ops[eng].append((waits, fn, self.esem[eng], 1))
        self._post(ev, reads, writes, False)
        self.nops += 1
        return ev

    def dma(self, eng, fn, stream, reads=(), writes=(), join=False):
        slot = stream.slots[stream.i]
        stream.i = (stream.i + 1) % len(stream.slots)
        deps = self._deps(reads, writes, join)
        if slot[0] is not None:
            deps.append((slot[0], slot[1]))
        waits = self._waits(eng, deps)
        if slot[0] is None or slot[1] + 16 > SEM_LIMIT:
            slot[0] = self.new_sem("d" + stream.name)
            slot[1] = 0
        slot[1] += 16
        ev = (slot[0], slot[1])
        self.ops[eng].append((waits, fn, slot[0], 16))
        self._post(ev, reads, writes, join)
        self.nops += 1
        return ev

    def all_events(self):
        evs = []
        for e in self.ENG:
            if self.esem[e] is not None:
                evs.append((self.esem[e], self.ecnt[e]))
        for st in self.streams:
            for sl in st.slots:
                if sl[0] is not None:
                    evs.append((sl[0], sl[1]))
        return evs

    def barrier(self, engines=None):
        evs = self.all_events()
        for e in (engines or self.ENG):
            w = self._waits(e, evs)
            if w:
                self.ops[e].append((w, None, None, 0))

    def emit(self):
        nc = self.nc
        block = self.stack.enter_context(nc.Block())
        ops = self.ops

        def run(e, name):
            for (waits, fn, sem, inc) in ops[name]:
                for (s, v) in waits:
                    e.wait_ge(s, v)
                if fn is not None:
                    fn(e).then_inc(sem, inc)

        @block.tensor
        def _(e):
            run(e, "pe")

        @block.scalar
        def _(e):
            run(e, "act")

        @block.vector
        def _(e):
            run(e, "dve")

        @block.gpsimd
        def _(e):
            run(e, "pool")

        @block.sync
        def _(e):
            run(e, "sp")


class Arena:
    def __init__(self, tensor, n):
        self.t = tensor
        self.n = n
        self.off = 0

    def reset(self):
        self.off = 0

    def take(self, *shape):
        n = int(np.prod(shape))
        assert self.off + n <= self.n, (self.off, n, self.n)
        ap = self.t[:, self.off:self.off + n]
        self.off += n
        if len(shape) == 2:
            ap = ap.rearrange("p (a b) -> p a b", a=shape[0])
        elif len(shape) == 3:
            ap = ap.rearrange("p (a b c) -> p a b c", a=shape[0], b=shape[1])
        return ap


class Builder:
    def __init__(self, NL=4, dbg=False, stop=None):
        self.NL = NL
        self.dbg = dbg
        self.stop = stop

    def mm(self, out, lhsT, rhs, st, sp, rd, wr):
        self.P.op("pe", lambda e: e.matmul(out, lhsT=lhsT, rhs=rhs, start=st, stop=sp), rd, wr)

    def act(self, out, in_, func, rd, wr, bias=None, scale=None):
        kw = {}
        if bias is not None:
            kw["bias"] = bias
        if scale is not None:
            kw["scale"] = scale
        self.P.op("act", lambda e: e.activation(out=out, in_=in_, func=func, **kw), rd, wr)

    def tt(self, eng, out, in0, in1, op, rd, wr):
        self.P.op(eng, lambda e: e.tensor_tensor(out=out, in0=in0, in1=in1, op=op), rd, wr)

    def ts(self, eng, out, in0, s1, s2, op0, op1, rd, wr):
        if op1 is None:
            self.P.op(eng, lambda e: e.tensor_scalar(out=out, in0=in0, scalar1=s1, scalar2=None, op0=op0), rd, wr)
        else:
            self.P.op(eng, lambda e: e.tensor_scalar(out=out, in0=in0, scalar1=s1, scalar2=s2, op0=op0, op1=op1), rd, wr)

    def stt(self, out, in0, scalar, in1, op0, op1, rd, wr):
        self.P.op("dve", lambda e: e.scalar_tensor_tensor(out=out, in0=in0, scalar=scalar, in1=in1, op0=op0, op1=op1), rd, wr)

    def cp(self, eng, out, in_, rd, wr):
        if eng == "act":
            self.P.op("act", lambda e: e.activation(out=out, in_=in_, func=AF.Copy), rd, wr)
        else:
            self.P.op(eng, lambda e: e.tensor_copy(out=out, in_=in_), rd, wr)

    def dma(self, eng, out, in_, stream, rd, wr, join=False, slow=False):
        if slow:
            self.P.dma(eng, lambda e: e.dma_start(out=out, in_=in_, allow_slow_non_contiguous=True), stream, rd, wr, join)
        else:
            self.P.dma(eng, lambda e: e.dma_start(out=out, in_=in_), stream, rd, wr, join)

    def PS(self, b, c0, c1, rows=128):
        ap = self.pb[b][0:rows, c0:c1]
        return ap, [self.pres[b]]

    def wload(self, eng, src, src_res, nelem):
        i = self.wi
        self.wi = (i + 1) % len(self.wbuf)
        buf = self.wbuf[i]
        r = self.wres[i]
        self.dma(eng, buf[:, 0:nelem], src, self.st_w, src_res, [r])
        return buf, r

    def wstream(self, reqs, depth=1):
        loaded = []
        for i in range(len(reqs)):
            while len(loaded) < min(len(reqs), i + 1 + depth):
                loaded.append(self.wload(*reqs[len(loaded)]))
            yield loaded[i]

    def build(self):
        NL = self.NL
        nc = bass.Bass("TRN2", target_bir_lowering=False)
        self.nc = nc

        def EI(n, s, d=F32):
            return nc.dram_tensor(n, list(s), d, kind="ExternalInput").ap()

        def SC(n, s, d=F32):
            return nc.dram_tensor(n, list(s), d, kind="Internal").ap()

        self.xT0 = EI("xT0", [KC, 128, U])
        self.cT_d = EI("cT", [128, KC, 2])
        self.w_ada = EI("w_ada", [NL, D, 6 * D])
        self.w_in = EI("w_in", [NL, D, 5120])
        self.w_out = EI("w_out", [NL, D, D])
        self.w_up = EI("w_up", [NL, D, 2 * DFF])
        self.w_down = EI("w_down", [NL, DFF, D])
        self.badaT_d = EI("badaT", [128, 4, 96])
        self.n1g_d = EI("n1g", [128, 4, KC])
        self.n2g_d = EI("n2g", [128, 4, KC])
        self.fng_d = EI("fng", [128, KC])
        self.qkg_d = EI("qkg", [128, 4, 2])
        self.hgg_d = EI("hgg", [128, 4])
        self.lbp_d = EI("lbp", [128, 2, 4, 4])
        self.sgg_d = EI("sgg", [4, 1, 512])
        self.sgb_d = EI("sgb", [4, 1, 512])
        self.sgwT_d = EI("sgwT", [4, 128, 4, 128])
        self.convw_d = EI("convw", [128, 4, 3, 88])
        self.convb_d = EI("convb", [128, 4, 88])
        self.rope_d = EI("ropeT", [2, 128, T])
        self.rotT_d = EI("rotT", [128, 128])
        self.ident_d = EI("ident", [128, 128], BF16)
        self.identf_d = EI("identf", [128, 128])
        self.masks_d = EI("masks", [128, 6, 128])
        self.seg_d = EI("seg", [128, 3, 128])
        self.outT = nc.dram_tensor("outT", [KC, 128, T], F32, kind="ExternalOutput").ap()
        if self.dbg:
            self.dbg1 = nc.dram_tensor("dbg1", [KC, 128, U], F32, kind="ExternalOutput").ap()
            self.dbg2 = nc.dram_tensor("dbg2", [KC, 128, U], F32, kind="ExternalOutput").ap()
            self.dbgm = nc.dram_tensor("dbgm", [KC, 128, U], F32, kind="ExternalOutput").ap()
            self.dbgs = nc.dram_tensor("dbgs", [128, 6600], F32, kind="ExternalOutput").ap()

        self.xTs = SC("xTs", [KC, 128, U])
        self.xMs = SC("xMs", [KC, 128, U])
        self.obT = SC("obT", [4, 128, U])
        self.modd = SC("modd", [2, 6 * D])
        self.win_s = [SC(f"win_s{l}", [10, 128, KC * 512], BF16) for l in range(NL)]
        self.wout_s = [SC(f"wout_s{l}", [4, 128, KC * 512], BF16) for l in range(NL)]
        self.wup_s = [SC(f"wup_s{l}", [22, 128, KC * 512], BF16) for l in range(NL)]
        self.wdn_s = [SC(f"wdn_s{l}", [16, 128, FC * 128], BF16) for l in range(NL)]

        with contextlib.ExitStack() as stack:
            P = Prog(nc, stack)
            self.P = P
            self.alloc()
            self.setup()
            for l in range(NL):
                self.layer(l)
            P.barrier()
            P.emit()
        return nc

    def alloc(self):
        P = self.P
        self.st_w = P.stream("w", 2)
        self.st_ld = P.stream("ld", 4)
        self.st_st = P.stream("st", 4)
        self.st_cast = P.stream("cast", 4)
        self.st_x0 = P.stream("x0", 4)
        self.pb = [P.ps(f"pb{i}", [128, 512]) for i in range(7)]
        self.pres = [P.res(f"pb{i}", excl=True) for i in range(7)]
        self.pb7 = P.ps("pb7", [128, 1024], BF16)
        self.p7res = [P.res("pb7", excl=True)] * 8
        self.wbuf = [P.sb(f"wbuf{i}", [128, 8192], BF16) for i in range(2)]
        self.wres = [P.res(f"wbuf{i}") for i in range(2)]
        self.wi = 0
        self.A32 = Arena(P.sb("A32", [128, 16384], F32), 16384)
        self.A16 = Arena(P.sb("A16", [128, 35200], BF16), 35200)
        self.C32t = P.sb("C32", [128, 6600], F32)
        C32 = Arena(self.C32t, 6600)
        C16 = Arena(P.sb("C16", [128, 1400], BF16), 1400)
        self.masks = C32.take(6, 128)
        self.seg = C32.take(3, 128)
        self.ones_f = C32.take(128)
        self.rotT = C32.take(128)
        self.ident_f = C32.take(128)
        self.sggB = C32.take(512)
        self.sgbB = C32.take(512)
        self.n1g = C32.take(4, KC)
        self.n2g = C32.take(4, KC)
        self.fng = C32.take(KC)
        self.qkg = C32.take(4, 2)
        self.hgg = C32.take(4)
        self.lbe = C32.take(2, 4, 4)
        self.lb = C32.take(2, 4, 4)
        self.oml = C32.take(2, 4, 4)
        self.lbs = C32.take(2, 4)
        self.convw = C32.take(4, 3, 88)
        self.convb = C32.take(4, 88)
        self.badaT = C32.take(4, 96)
        self.cT = C32.take(KC, 2)
        self.modT = C32.take(2, 96)
        self.G1 = C32.take(2, KC)
        self.G2 = C32.take(2, KC)
        self.modrow = [C32.take(512), C32.take(512)]
        self.state_f = C32.take(4, 128)
        self.dec = C32.take(4)
        self.ones_b = C16.take(128)
        self.ident = C16.take(128)
        self.scT = C16.take(KC, 2)
        self.sgwT = C16.take(4, 128)
        self.state_b = C16.take(4, 128)
        R = P.res
        self.r_const = R("const")
        self.r_par = R("par")
        self.r_sg = R("sgpar")
        self.r_lb = R("lb")
        self.r_mod = R("mod")
        self.r_modrow = [R("modrow0"), R("modrow1")]
        self.r_modd = R("modd")
        self.r_state = [R(f"state{h}") for h in range(4)]
        self.r_xT = [R(f"xT{j}") for j in range(U // 128)]
        self.r_xM = [R(f"xM{j}") for j in range(U // 128)]
        self.r_ob = [R(f"ob{j}") for j in range(U // 128)]
        self.r_wsc = [{m: R(f"wsc{l}{m}") for m in ("in", "out", "up", "dn")} for l in range(self.NL)]
        self.r_out = R("out")
        self.r_dbg = R("dbg")

    def setup(self):
        P = self.P
        ld = self.st_ld
        rc = [self.r_const]
        rp = [self.r_par]
        for (dst, src) in [(self.masks, self.masks_d), (self.seg, self.seg_d), (self.rotT, self.rotT_d), (self.ident_f, self.identf_d),
                           (self.ident, self.ident_d)]:
            self.dma("sp", dst, src, ld, [], rc, join=True)
        for (dst, src) in [(self.n1g, self.n1g_d), (self.n2g, self.n2g_d), (self.fng, self.fng_d),
                           (self.qkg, self.qkg_d), (self.hgg, self.hgg_d), (self.lbe, self.lbp_d),
                           (self.convw, self.convw_d), (self.convb, self.convb_d), (self.badaT, self.badaT_d),
                           (self.cT, self.cT_d)]:
            self.dma("sp", dst, src, ld, [], rp, join=True)
        P.op("dve", lambda e: e.memset(self.ones_f, 1.0), [], rc)
        P.op("dve", lambda e: e.memset(self.ones_b, 1.0), [], rc)
        self.act(self.scT, self.cT, AF.Silu, rp, [self.r_lb])
        lbe, lb, oml, lbs = self.lbe, self.lb, self.oml, self.lbs
        rl = [self.r_lb]
        self.act(lbe, lbe, AF.Exp, rp + rl, rp)
        self.tt("dve", lbs, lbe[:, :, 0, :], lbe[:, :, 1, :], ALU.add, rp, rl)
        self.tt("dve", lbs, lbs, lbe[:, :, 2, :], ALU.add, rp + rl, rl)
        self.tt("dve", lbs, lbs, lbe[:, :, 3, :], ALU.add, rp + rl, rl)
        P.op("dve", lambda e: e.reciprocal(out=lbs, in_=lbs), rl, rl)
        P.op("dve", lambda e: e.memset(lb[:, :, 0, :], 0.0), [], rl)
        self.tt("dve", lb[:, :, 1, :], lbe[:, :, 1, :], lbs, ALU.mult, rp + rl, rl)
        for j in (2, 3):
            self.tt("dve", lb[:, :, j, :], lbe[:, :, j, :], lbs, ALU.mult, rp + rl, rl)
            self.tt("dve", lb[:, :, j, :], lb[:, :, j, :], lb[:, :, j - 1, :], ALU.add, rl, rl)
        self.ts("dve", oml, lb, -1.0, 1.0, ALU.mult, ALU.add, rl, rl)
        for kc in range(KC):
            self.dma("sp", self.xTs[kc], self.xT0[kc], self.st_x0, [], self.r_xT, join=True)
        for l in range(self.NL):
            self.cast_weights(l)

    def cast_weights(self, l):
        cs = self.st_cast
        rw = self.r_wsc[l]
        for g in range(10):
            self.dma("pool", self.win_s[l][g].rearrange("p (k j) -> p k j", k=KC),
                     self.w_in[l][:, g * 512:(g + 1) * 512].rearrange("(k p) j -> p k j", p=128),
                     cs, [], [rw["in"]], join=True)
        for g in range(4):
            self.dma("pool", self.wout_s[l][g].rearrange("p (k j) -> p k j", k=KC),
                     self.w_out[l][:, g * 512:(g + 1) * 512].rearrange("(k p) j -> p k j", p=128),
                     cs, [], [rw["out"]], join=True)
        for t in range(22):
            dst = self.wup_s[l][t].rearrange("p (k j) -> p k j", k=KC)
            self.dma("pool", dst[:, :, 0:256],
                     self.w_up[l][:, t * 256:(t + 1) * 256].rearrange("(k p) j -> p k j", p=128),
                     cs, [], [rw["up"]], join=True)
            self.dma("pool", dst[:, :, 256:512],
                     self.w_up[l][:, DFF + t * 256:DFF + (t + 1) * 256].rearrange("(k p) j -> p k j", p=128),
                     cs, [], [rw["up"]], join=True)
        for m in range(16):
            self.dma("pool", self.wdn_s[l][m].rearrange("p (k j) -> p k j", k=FC),
                     self.w_down[l][:, m * 128:(m + 1) * 128].rearrange("(k p) j -> p k j", p=128),
                     cs, [], [rw["dn"]], join=True)

    def ada(self, l):
        P = self.P
        reqs = []
        for n in range(24):
            reqs.append(("pool", self.w_ada[l][:, n * 512:(n + 1) * 512].rearrange("(k p) j -> p k j", p=128), [], 8192))
        n = 0
        for (buf, r) in self.wstream_ada(reqs):
            wt = buf[:, 0:8192].rearrange("p (k j) -> p k j", k=KC)
            ps, pr = self.PS(n % 2, 0, 512, rows=2)
            for kc in range(KC):
                self.mm(ps, self.scT[:, kc, :], wt[:, kc, :], kc == 0, kc == KC - 1, [r, self.r_lb], pr)
            mr = self.modrow[n % 2]
            rr = self.r_modrow[n % 2]
            self.cp("act", mr[0:2, :], ps, pr, [rr])
            for cc in range(4):
                c = n * 4 + cc
                pm, pmr = self.PS(2, 0, 192)
                self.mm(pm[:, c * 2:c * 2 + 2], mr[0:2, cc * 128:(cc + 1) * 128], self.ident_f[0:2, 0:2], True, True,
                        [rr, self.r_const], pmr)
            n += 1
        rm = [self.r_mod]
        pm, pmr = self.PS(2, 0, 192)
        pm3 = pm.rearrange("p (c r) -> p c r", r=2)
        for r_ in range(2):
            self.cp("dve", self.modT[:, r_, :], pm3[:, :, r_], pmr, rm)
        for r_ in range(2):
            self.tt("dve", self.modT[:, r_, :], self.modT[:, r_, :], self.badaT[:, l, :], ALU.add, rm + [self.r_par], rm)
        for r_ in range(2):
            self.stt(self.G1[:, r_, :], self.modT[:, r_, 16:32], 1.0, self.n1g[:, l, :], ALU.add, ALU.mult, rm + [self.r_par], rm)
            self.stt(self.G2[:, r_, :], self.modT[:, r_, 64:80], 1.0, self.n2g[:, l, :], ALU.add, ALU.mult, rm + [self.r_par], rm)

    def wstream_ada(self, reqs):
        loaded = []
        for i in range(len(reqs)):
            while len(loaded) < min(len(reqs), i + 2):
                eng, src, sres, nelem = reqs[len(loaded)]
                k = self.wi
                self.wi = (k + 1) % len(self.wbuf)
                buf = self.wbuf[k]
                r = self.wres[k]
                self.dma(eng, buf[:, 0:nelem].rearrange("p (k j) -> p k j", k=KC), src, self.st_w, sres, [r])
                loaded.append((buf, r))
            yield loaded[i]

    def norm(self, x, rx, n, G, S, rg, dst, rdst, tmp):
        sq, rsq, tm, rtm, rstd, rrstd = tmp
        pn, pnr = self.PS(2, 0, n)
        for kc in range(KC):
            i = kc % 2
            self.act(sq[i][:, 0:n], x[:, kc, 0:n], AF.Square, [rx], [rsq[i]])
            self.mm(pn, self.ones_f, sq[i][:, 0:n], kc == 0, kc == KC - 1, [rsq[i], self.r_const], pnr)
        self.act(rstd[:, 0:n], pn, AF.Sqrt, pnr, [rrstd], bias=EPS, scale=1.0 / D)
        self.P.op("dve", lambda e: e.reciprocal(out=rstd[:, 0:n], in_=rstd[:, 0:n]), [rrstd], [rrstd])
        for kc in range(KC):
            i = kc % 2
            self.tt("dve", tm[i][:, 0:n], x[:, kc, 0:n], rstd[:, 0:n], ALU.mult, [rx, rrstd], [rtm[i]])
            if S is not None:
                self.act(dst(kc), tm[i][:, 0:n], AF.Identity, [rtm[i]] + rg, rdst, bias=S[:, kc:kc + 1], scale=G[:, kc:kc + 1])
            else:
                self.act(dst(kc), tm[i][:, 0:n], AF.Identity, [rtm[i]] + rg, rdst, scale=G[:, kc:kc + 1])

    def proj_fm(self, wt, rw, kcn, mo, rhs, rrhs, n, bank):
        ps, pr = self.PS(bank, 0, n)
        for kc in range(kcn):
            self.mm(ps, wt[:, kc, mo:mo + 128], rhs(kc), kc == 0, kc == kcn - 1, [rw] + rrhs, pr)
        return ps, pr

    def proj_tm(self, wt, rw, c0, w, hT, rh, t0, bank):
        ps, pr = self.PS(bank, 0, w)
        for kc in range(KC):
            self.mm(ps, hT[:, kc, t0:t0 + 128], wt[:, kc, c0:c0 + w], kc == 0, kc == KC - 1, [rw, rh], pr)
        return ps, pr

    def alloc_p12(self):
        A32, A16, R = self.A32, self.A16, self.P.res
        A32.reset()
        A16.reset()
        n = NB
        self.xsb = A32.take(KC, n); self.r_x = R("xsb")
        self.ntmp = ([A32.take(n), A32.take(n)], [R("sq0"), R("sq1")], [A32.take(n), A32.take(n)], [R("tm0"), R("tm1")],
                     A32.take(n), R("rstd"))
        self.qS = A32.take(4, n); self.lfS = A32.take(4, n); self.kkS = A32.take(4, n)
        self.r_qS = [R(f"qS{h}") for h in range(4)]
        self.r_lfS = [R(f"lfS{h}") for h in range(4)]
        self.r_kkS = [R(f"kkS{h}") for h in range(4)]
        self.t1 = A32.take(n); self.t2 = A32.take(n); self.r_t1 = R("t1"); self.r_t2 = R("t2")
        self.rope = A32.take(2, n); self.r_rope = R("rope")
        self.hg = {}
        for nm in ["P32", "P64", "P128", "U32", "U64", "U128", "EA", "nW32", "nW64", "XA", "YA", "XB", "YB", "XC", "YC",
                   "XD", "YD", "Sf", "tB", "tC", "oS", "sqo", "rso"]:
            self.hg[nm] = (A32.take(128), R("hg_" + nm))
        self.gv = A32.take(512); self.sqv = A32.take(512); self.vn = A32.take(512)
        self.ssum = A32.take(4); self.tsg = A32.take(128)
        self.r_gv = R("gv"); self.r_sqv = R("sqv"); self.r_vn = R("vn"); self.r_ssum = R("ssum"); self.r_tsg = R("tsg")
        self.rD = A32.take(n); self.r_rD = R("rD")
        self.obst = A32.take(4, 128); self.r_obst = R("obst")
        self.obld = A32.take(4, 128); self.r_obld = R("obld")
        self.sgm = A32.take(n); self.fS = A32.take(n); self.r_sgm = R("sgm"); self.r_fS = R("fS")
        self.hT = A16.take(KC, n); self.r_h = R("hT")
        self.KT = A16.take(2, U); self.r_KT = R("KT")
        self.V = A16.take(U // 128, 256); self.r_V = R("V")
        self.QT = A16.take(8, n); self.r_QT = [R(f"QT{h}") for h in range(8)]
        self.mixT = A16.take(KC, n); self.r_mix = [R(f"mix{c}") for c in range(KC)]
        self.pT = [A16.take(n) for _ in range(3)]; self.r_pT = [R(f"pT{i}") for i in range(3)]
        self.vS = A16.take(n // 128, 512); self.r_vS = [R(f"vS{i}") for i in range(n // 128)]
        self.vnb = A16.take(512); self.r_vnb = R("vnb")
        self.gateS = A16.take(4, n); self.uS = A16.take(4, n)
        self.r_gateS = [R(f"gateS{h}") for h in range(4)]
        self.r_uS = [R(f"uS{h}") for h in range(4)]
        self.hop = [{nm: (A16.take(128), R(f"hop{i}_{nm}")) for nm in ["qA", "kA", "qB", "kB", "qC", "kC", "qD", "kD", "Sb", "kDt"]}
                    for i in range(2)]

    def alloc_p3(self):
        A32, A16, R = self.A32, self.A16, self.P.res
        A32.reset()
        A16.reset()
        n = FW + 8
        self.xsb = A32.take(KC, n); self.r_x = R("xsb3")
        self.ntmp = ([A32.take(n), A32.take(n)], [R("sq0"), R("sq1")], [A32.take(n), A32.take(n)], [R("tm0"), R("tm1")],
                     A32.take(n), R("rstd"))
        self.accg = [A32.take(n) for _ in range(2)]; self.accv = [A32.take(n) for _ in range(2)]
        self.sil = [A32.take(n) for _ in range(2)]
        self.r_accg = [R("accg0"), R("accg1")]; self.r_accv = [R("accv0"), R("accv1")]; self.r_sil = [R("sil0"), R("sil1")]
        self.hT = A16.take(KC, n); self.r_h = R("hT3")
        self.aT = A16.take(FC, n); self.r_aT = [R(f"aT{j}") for j in range(FC)]

    def layer(self, l):
        P = self.P
        last = (l == 3)
        if self.stop == "setup":
            self.dump_c32()
            return
        self.ada(l)
        P.barrier()
        if self.stop == "ada":
            self.dump_c32()
            return
        self.alloc_p12()
        self.dma("sp", self.sggB, self.sgg_d[l].partition_broadcast(128), self.st_ld, [], [self.r_sg], join=True)
        self.dma("sp", self.sgbB, self.sgb_d[l].partition_broadcast(128), self.st_ld, [], [self.r_sg], join=True)
        self.dma("pool", self.sgwT, self.sgwT_d[l], self.st_ld, [], [self.r_sg], join=True)
        blocks = [(0, LC, True)] + [(LC + j * NB, NB, False) for j in range(T // NB)]
        self.reset_state()
        p1b = [blocks[0]] + blocks[:0:-1]
        if self.stop and self.stop.startswith("p1b"):
            p1b = p1b[:int(self.stop[3:])]
        for blk in p1b:
            self.pass1_block(l, blk)
        if self.stop and self.stop.startswith("p1"):
            return
        self.reset_state()
        p2b = blocks
        if self.stop and self.stop.startswith("p2b"):
            p2b = p2b[:int(self.stop[3:])]
        for blk in p2b:
            self.pass2_block(l, blk, last)
        if self.dbg and l == 0:
            for kc in range(KC):
                self.dma("sp", self.dbg1[kc], self.xMs[kc], self.st_x0, self.r_xM, [self.r_dbg], join=True)
        P.barrier()
        if self.stop and self.stop.startswith("p2"):
            return
        self.alloc_p3()
        wins = []
        if not last:
            wins.append((0, LC, 0, LC))
        o = 0
        while o < T:
            n = min(FW, T - o)
            wins.append((LC + o, LC + o + n, LC, U))
            o += n
        for w in wins:
            self.ffn_window(l, w, last)
        if self.dbg and l == 0:
            for kc in range(KC):
                self.dma("sp", self.dbg2[kc], self.xTs[kc], self.st_x0, self.r_xT, [self.r_dbg], join=True)
        P.barrier()

    def dump_c32(self):
        self.P.barrier()
        self.dma("sp", self.dbgs, self.C32t[:, :], self.st_st, [], [self.r_dbg])

    def reset_state(self):
        for h in range(4):
            self.P.op("dve", lambda e, h=h: e.memset(self.state_f[:, h, :], 0.0), [], [self.r_state[h]])
            self.P.op("pool", lambda e, h=h: e.memset(self.state_b[:, h, :], 0.0), [], [self.r_state[h]])

    def load_x(self, u0, n):
        rs = self.r_xT[u0 // 128:(u0 + n + 127) // 128]
        self.dma("sp", self.xsb[:, :, 0:n], self.xTs[:, :, u0:u0 + n].rearrange("k p n -> p k n"), self.st_ld, rs, [self.r_x])
        return rs

    def block_norm1(self, l, blk):
        u0, nb, isctx = blk
        r_ = 1 if isctx else 0
        self.load_x(u0, nb)
        self.norm(self.xsb, self.r_x, nb, self.G1[:, r_, :], self.modT[:, r_, 0:16], [self.r_mod],
                  lambda kc: self.hT[:, kc, 0:nb], [self.r_h], self.ntmp)

    def win_req(self, l, g):
        return ("sp", self.win_s[l][g], [self.r_wsc[l]["in"]], 8192)

    def qk_head(self, l, ps, pr, n, gcol, rope, dst, rdst):
        sq, rsq, tm, rtm, rstd, rrstd = self.ntmp
        kraw, r_kraw = tm[0], rtm[0]
        kn, r_kn = tm[1], rtm[1]
        self.cp("act", kraw[:, 0:n], ps, pr, [r_kraw])
        self.act(sq[0][:, 0:n], ps, AF.Square, pr, [rsq[0]])
        pn, pnr = self.PS(2, 0, n)
        self.mm(pn, self.ones_f, sq[0][:, 0:n], True, True, [rsq[0], self.r_const], pnr)
        self.act(sq[1][:, 0:n], pn, AF.Sqrt, pnr, [rsq[1]], bias=EPS, scale=1.0 / 128)
        self.P.op("dve", lambda e: e.reciprocal(out=sq[1][:, 0:n], in_=sq[1][:, 0:n]), [rsq[1]], [rsq[1]])
        self.stt(kn[:, 0:n], kraw[:, 0:n], gcol, sq[1][:, 0:n], ALU.mult, ALU.mult, [r_kraw, rsq[1], self.r_par], [r_kn])
        if rope:
            pro, prr = self.PS(3, 256, 256 + n)
            self.mm(pro, self.rotT, kn[:, 0:n], True, True, [r_kn, self.r_const], prr)
            self.tt("dve", self.t1[:, 0:n], kn[:, 0:n], self.rope[:, 0, 0:n], ALU.mult, [r_kn, self.r_rope], [self.r_t1])
            self.tt("dve", self.t2[:, 0:n], pro, self.rope[:, 1, 0:n], ALU.mult, prr + [self.r_rope], [self.r_t2])
            self.tt("pool", dst, self.t1[:, 0:n], self.t2[:, 0:n], ALU.add, [self.r_t1, self.r_t2], rdst)
        else:
            self.cp("act", dst, kn[:, 0:n], [r_kn], rdst)

    def load_rope(self, u0, nb):
        t0 = u0 - LC
        self.dma("sp", self.rope[:, :, 0:nb], self.rope_d[:, :, t0:t0 + nb].rearrange("a p n -> p a n"), self.st_ld,
                 [], [self.r_rope])

    def hgrn_q(self, wq, rwq, nb):
        rhs = lambda kc: self.hT[:, kc, 0:nb]
        for hd in range(4):
            ps, pr = self.proj_fm(wq, rwq, KC, hd * 128, rhs, [self.r_h], nb, hd % 2)
            self.act(self.qS[:, hd, 0:nb], ps, AF.Silu, pr, [self.r_qS[hd]])

    def hgrn_f(self, l, d, wf, rwf, nb):
        rhs = lambda kc: self.hT[:, kc, 0:nb]
        for hd in range(4):
            ps, pr = self.proj_fm(wf, rwf, KC, hd * 128, rhs, [self.r_h], nb, hd % 2)
            self.act(self.sgm[:, 0:nb], ps, AF.Sigmoid, pr, [self.r_sgm])
            self.ts("dve", self.fS[:, 0:nb], self.sgm[:, 0:nb], self.oml[:, d, l, hd:hd + 1], self.lb[:, d, l, hd:hd + 1],
                    ALU.mult, ALU.add, [self.r_sgm, self.r_lb], [self.r_fS])
            self.ts("pool", self.fS[:, 0:nb], self.fS[:, 0:nb], F_MIN, None, ALU.max, None, [self.r_fS], [self.r_fS])
            self.act(self.lfS[:, hd, 0:nb], self.fS[:, 0:nb], AF.Ln, [self.r_fS], [self.r_lfS[hd]])
            self.ts("pool", self.kkS[:, hd, 0:nb], self.fS[:, 0:nb], -1.0, 1.0, ALU.mult, ALU.add, [self.r_fS], [self.r_kkS[hd]])

    def hi_prep(self, wt, rw, nb):
        for ti in range(nb // 128):
            ps, pr = self.proj_tm(wt, rw, 0, 512, self.hT, self.r_h, ti * 128, ti % 2)
            self.cp("act", self.vS[:, ti, :], ps, pr, [self.r_vS[ti]])

    def hgrn_tile(self, l, d, hd, ti, it):
        H = self.hg
        c0 = ti * 128
        lf = self.lfS[:, hd, c0:c0 + 128]; rlf = self.r_lfS[hd]
        q = self.qS[:, hd, c0:c0 + 128]; rq = self.r_qS[hd]
        kk = self.kkS[:, hd, c0:c0 + 128]; rkk = self.r_kkS[hd]
        v = self.vS[:, ti, hd * 128:(hd + 1) * 128]; rv = self.r_vS[ti]
        rc = self.r_const
        for i, nm in enumerate(["P32", "P64", "P128"]):
            ap, r = H[nm]
            self.P.op("dve", lambda e, ap=ap, i=i: e.tensor_tensor_scan(out=ap, data0=self.seg[:, i, :], data1=lf, initial=0.0,
                                                                       op0=ALU.mult, op1=ALU.add), [rlf, rc], [r])
        if d == 1:
            Us = []
            for nm, pn in [("U32", "P32"), ("U64", "P64"), ("U128", "P128")]:
                ap, r = H[nm]
                self.tt("pool", ap, H[pn][0], lf, ALU.subtract, [H[pn][1], rlf], [r])
                Us.append((ap, r))
        else:
            Us = [H["P32"], H["P64"], H["P128"]]
        (U32, rU32), (U64, rU64), (U128, rU128) = Us
        P32, rP32 = H["P32"]; P64, rP64 = H["P64"]; P128, rP128 = H["P128"]
        mpos = 15 if d == 0 else 16
        U32v = U32.rearrange("p (a b) -> p a b", a=4)
        EA, rEA = H["EA"]
        self.tt("dve", EA.rearrange("p (a b) -> p a b", a=4), U32v, U32v[:, :, mpos:mpos + 1].to_broadcast([128, 4, 32]),
                ALU.subtract, [rU32], [rEA])
        nW32, rnW32 = H["nW32"]
        nW64, rnW64 = H["nW64"]
        P32v = P32.rearrange("p (a b) -> p a b", a=4)
        P64v = P64.rearrange("p (a b) -> p a b", a=2)
        self.tt("dve", nW32.rearrange("p (a b) -> p a b", a=4), U32v, P32v[:, :, 31:32].to_broadcast([128, 4, 32]),
                ALU.subtract, [rU32, rP32], [rnW32])
        self.tt("pool", nW64.rearrange("p (a b) -> p a b", a=2), U64.rearrange("p (a b) -> p a b", a=2),
                P64v[:, :, 63:64].to_broadcast([128, 2, 64]), ALU.subtract, [rU64, rP64], [rnW64])
        tot = P128[:, 127:128]
        ex = [("XA", EA, rEA, 1.0, None), ("YA", EA, rEA, -1.0, None),
              ("XB", U32, rU32, 1.0, None), ("YB", nW32, rnW32, -1.0, None),
              ("XC", U64, rU64, 1.0, None), ("YC", nW64, rnW64, -1.0, None),
              ("XD", U128, rU128, 1.0, None), ("YD", U128, rU128, -1.0, tot)]
        for nm, src, rs, sc, b in ex:
            ap, r = H[nm]
            if b is None:
                self.act(ap, src, AF.Exp, [rs], [r], scale=sc)
            else:
                self.act(ap, src, AF.Exp, [rs, rP128], [r], scale=sc, bias=b)
        rst = self.r_state[hd]
        self.act(self.dec[:, hd:hd + 1], tot, AF.Exp, [rP128], [rst])
        O = self.hop[it % 2]
        engs = ["dve", "pool"]
        k = 0
        for lv in "ABCD":
            eq, ek = ("X" + lv, "Y" + lv) if d == 0 else ("Y" + lv, "X" + lv)
            self.tt(engs[k % 2], O["q" + lv][0], q, H[eq][0], ALU.mult, [rq, H[eq][1]], [O["q" + lv][1]]); k += 1
            self.tt(engs[k % 2], O["k" + lv][0], kk, H[ek][0], ALU.mult, [rkk, H[ek][1]], [O["k" + lv][1]]); k += 1
        slots = [(5, 256), (5, 384), (6, 256)]
        sps = []
        for i, lv in enumerate("ABC"):
            ps, pr = self.PS(slots[i][0], slots[i][1], slots[i][1] + 128)
            self.mm(ps, O["k" + lv][0], O["q" + lv][0], True, True, [O["k" + lv][1], O["q" + lv][1]], pr)
            sps.append((ps, pr))
        Sf, rSf = H["Sf"]; tB, rtB = H["tB"]; tC, rtC = H["tC"]
        self.tt("dve", Sf, sps[0][0], self.masks[:, d * 3 + 0, :], ALU.mult, sps[0][1] + [rc], [rSf])
        self.tt("dve", tB, sps[1][0], self.masks[:, d * 3 + 1, :], ALU.mult, sps[1][1] + [rc], [rtB])
        self.tt("dve", tC, sps[2][0], self.masks[:, d * 3 + 2, :], ALU.mult, sps[2][1] + [rc], [rtC])
        self.tt("pool", Sf, Sf, tB, ALU.add, [rSf, rtB], [rSf])
        Sb, rSb = O["Sb"]
        self.tt("pool", Sb, Sf, tC, ALU.add, [rSf, rtC], [rSb])
        po, por = self.PS(6, 384, 512)
        self.mm(po, v, Sb, True, False, [rv, rSb], por)
        self.mm(po, self.state_b[:, hd, :], O["qD"][0], False, True, [rst, O["qD"][1]], por)
        slot7 = (it % 2)
        pt = self.pb7[:, slot7 * 128:(slot7 + 1) * 128]
        ptr = [self.p7res[slot7]]
        self.P.op("pe", lambda e: e.transpose(pt, O["kD"][0], self.ident), [O["kD"][1], rc], ptr)
        kDt, rkDt = O["kDt"]
        self.cp("act", kDt, pt, ptr, [rkDt])
        pst, pstr = self.PS(2, 256, 384)
        self.mm(pst, kDt, v, True, True, [rkDt, rv], pstr)
        self.stt(self.state_f[:, hd, :], self.state_f[:, hd, :], self.dec[:, hd:hd + 1], pst, ALU.mult, ALU.add, pstr + [rst], [rst])
        self.cp("act", self.state_b[:, hd, :], self.state_f[:, hd, :], [rst], [rst])
        return po, por

    def pass1_block(self, l, blk):
        u0, nb, isctx = blk
        self.block_norm1(l, blk)
        if not isctx:
            self.load_rope(u0, nb)
        rhs = lambda kc: self.hT[:, kc, 0:nb]
        reqs = [self.win_req(l, g) for g in (2, 3, 5, 6)]
        tiles = []
        for (buf, r) in self.wstream(reqs):
            tiles.append((buf[:, 0:8192].rearrange("p (k j) -> p k j", k=KC), r))
            gi = len(tiles) - 1
            wt, rw = tiles[gi]
            if gi == 0:
                for kv in range(2):
                    ps, pr = self.proj_fm(wt, rw, KC, kv * 128, rhs, [self.r_h], nb, kv % 2)
                    self.qk_head(l, ps, pr, nb, self.qkg[:, l, 1:2], not isctx, self.KT[:, kv, u0:u0 + nb], [self.r_KT])
                for ti in range(nb // 128):
                    ps, pr = self.proj_tm(wt, rw, 256, 256, self.hT, self.r_h, ti * 128, ti % 2)
                    self.cp("act", self.V[:, u0 // 128 + ti, :], ps, pr, [self.r_V])
            elif gi == 1:
                self.hgrn_q(wt, rw, nb)
            elif gi == 2:
                self.hgrn_f(l, 1, wt, rw, nb)
            elif gi == 3:
                self.hi_prep(wt, rw, nb)
        it = 0
        for ti in reversed(range(nb // 128)):
            gt = u0 // 128 + ti
            for hd in range(4):
                po, por = self.hgrn_tile(l, 1, hd, ti, it)
                it += 1
                self.cp("act", self.obst[:, hd, :], po, por, [self.r_obst])
            self.dma("sp", self.obT[:, :, gt * 128:(gt + 1) * 128].rearrange("h p n -> p h n"), self.obst, self.st_st,
                     [self.r_obst], [self.r_ob[gt]])

    def pass2_block(self, l, blk, last):
        u0, nb, isctx = blk
        skip_out = last and isctx
        self.block_norm1(l, blk)
        if not isctx:
            self.load_rope(u0, nb)
        rhs = lambda kc: self.hT[:, kc, 0:nb]
        gs = [3, 4, 6] if skip_out else [0, 1, 3, 4, 6, 7, 8, 9]
        reqs = [self.win_req(l, g) for g in gs]
        if not skip_out:
            reqs += [("sp", self.wout_s[l][g], [self.r_wsc[l]["out"]], 8192) for g in range(4)]
        ws = self.wstream(reqs)
        tiles = {}

        def nxt(g):
            buf, r = next(ws)
            tiles[g] = (buf[:, 0:8192].rearrange("p (k j) -> p k j", k=KC), r)
            return tiles[g]

        if not skip_out:
            for g in (0, 1):
                wt, rw = nxt(g)
                for hh in range(4):
                    h = g * 4 + hh
                    ps, pr = self.proj_fm(wt, rw, KC, hh * 128, rhs, [self.r_h], nb, hh % 2)
                    self.qk_head(l, ps, pr, nb, self.qkg[:, l, 0:1], not isctx, self.QT[:, h, 0:nb], [self.r_QT[h]])
            self.attention(l, blk)
        wq, rwq = nxt(3)
        self.hgrn_q(wq, rwq, nb)
        wf, rwf = nxt(4)
        self.hgrn_f(l, 0, wf, rwf, nb)
        wt, rw = nxt(6)
        self.hi_prep(wt, rw, nb)
        if not skip_out:
            wt, rw = nxt(7)
            for hd in range(4):
                ps, pr = self.proj_fm(wt, rw, KC, hd * 128, rhs, [self.r_h], nb, hd % 2)
                self.act(self.gateS[:, hd, 0:nb], ps, AF.Silu, pr, [self.r_gateS[hd]])
            wt, rw = nxt(8)
            for g in range(4):
                ps, pr = self.proj_fm(wt, rw, KC, g * 128, rhs, [self.r_h], nb, g % 2)
                self.act(self.uS[:, g, 0:nb], ps, AF.Gelu_apprx_tanh, pr, [self.r_uS[g]])
            wsv, rwsv = nxt(9)
        it = 0
        H = self.hg
        for ti in range(nb // 128):
            gt = u0 // 128 + ti
            c0 = ti * 128
            if not skip_out:
                self.dma("sp", self.obld, self.obT[:, :, gt * 128:(gt + 1) * 128].rearrange("h p n -> p h n"), self.st_ld,
                         [self.r_ob[gt]], [self.r_obld])
            for hd in range(4):
                po, por = self.hgrn_tile(l, 0, hd, ti, it)
                it += 1
                if skip_out:
                    continue
                oS, roS = H["oS"]; sqo, rsqo = H["sqo"]; rso, rrso = H["rso"]
                self.tt("dve", oS, po, self.obld[:, hd, :], ALU.add, por + [self.r_obld], [roS])
                self.act(sqo, oS, AF.Square, [roS], [rsqo])
                pn, pnr = self.PS(2, 384, 512)
                self.mm(pn, self.ones_f, sqo, True, True, [rsqo, self.r_const], pnr)
                self.act(rso, pn, AF.Sqrt, pnr, [rrso], bias=EPS, scale=1.0 / 128)
                self.P.op("dve", lambda e, rso=rso: e.reciprocal(out=rso, in_=rso), [rrso], [rrso])
                self.stt(oS, oS, self.hgg[:, l:l + 1], rso, ALU.mult, ALU.mult, [roS, rrso, self.r_par], [roS])
                self.tt("pool", self.mixT[:, 8 + hd, c0:c0 + 128], oS, self.gateS[:, hd, c0:c0 + 128], ALU.mult,
                        [roS, self.r_gateS[hd]], [self.r_mix[8 + hd]])
            if skip_out:
                continue
            ps, pr = self.proj_tm(wsv, rwsv, 0, 512, self.hT, self.r_h, c0, ti % 2)
            self.act(self.gv, ps, AF.Gelu_apprx_tanh, pr, [self.r_gv])
            self.tt("pool", self.sqv, self.gv, self.gv, ALU.mult, [self.r_gv], [self.r_sqv])
            self.P.op("dve", lambda e: e.tensor_reduce(out=self.ssum, in_=self.sqv.rearrange("p (g d) -> p g d", g=4), axis=AX.X,
                                                       op=ALU.add), [self.r_sqv], [self.r_ssum])
            self.act(self.ssum, self.ssum, AF.Sqrt, [self.r_ssum], [self.r_ssum], bias=EPS, scale=1.0 / 128)
            self.P.op("dve", lambda e: e.reciprocal(out=self.ssum, in_=self.ssum), [self.r_ssum], [self.r_ssum])
            self.tt("dve", self.vn.rearrange("p (g d) -> p g d", g=4), self.gv.rearrange("p (g d) -> p g d", g=4),
                    self.ssum.unsqueeze(2).to_broadcast([128, 4, 128]), ALU.mult, [self.r_gv, self.r_ssum], [self.r_vn])
            self.tt("pool", self.vnb, self.vn, self.sggB, ALU.mult, [self.r_vn, self.r_sg], [self.r_vnb])
            for g in range(4):
                pg, pgr = self.PS(3 + (g % 2), 256, 384)
                self.mm(pg, self.vnb[:, g * 128:(g + 1) * 128], self.sgwT[:, g, :], True, True, [self.r_vnb, self.r_sg], pgr)
                self.tt("dve", self.tsg, pg, self.sgbB[:, g * 128:(g + 1) * 128], ALU.add, pgr + [self.r_sg], [self.r_tsg])
                self.tt("pool", self.mixT[:, 12 + g, c0:c0 + 128], self.tsg, self.uS[:, g, c0:c0 + 128], ALU.mult,
                        [self.r_tsg, self.r_uS[g]], [self.r_mix[12 + g]])
        if skip_out:
            return
        if self.dbg and l == 0:
            self.dma("pool", self.dbgm[:, :, u0:u0 + nb].rearrange("k p n -> p k n"), self.mixT[:, :, 0:nb], self.st_st,
                     self.r_mix, [self.r_dbg], join=True)
        r_ = 1 if isctx else 0
        mrhs = lambda kc: self.mixT[:, kc, 0:nb]
        for g in range(4):
            wt, rw = nxt(10 + g)
            for mm_ in range(4):
                m = g * 4 + mm_
                ps, pr = self.proj_fm(wt, rw, KC, mm_ * 128, mrhs, self.r_mix, nb, mm_ % 2)
                self.stt(self.xsb[:, m, 0:nb], ps, self.modT[:, r_, 32 + m:33 + m], self.xsb[:, m, 0:nb], ALU.mult, ALU.add,
                         pr + [self.r_x, self.r_mod], [self.r_x])
        rs = self.r_xM[u0 // 128:(u0 + nb) // 128]
        self.dma("sp", self.xMs[:, :, u0:u0 + nb].rearrange("k p n -> p k n"), self.xsb[:, :, 0:nb], self.st_st, [self.r_x], rs)

    def attention(self, l, blk):
        u0, nb, isctx = blk
        keyt = [0, 1] if isctx else list(range(U // 128))
        scale = 128 ** -0.5
        cnt = 0
        for h in range(8):
            kv = h // 4
            po, por = self.PS(5, 0, nb)
            pd, pdr = self.PS(6, 0, nb)
            for ji, j in enumerate(keyt):
                ps, pr = self.PS(3 + cnt % 2, 0, nb)
                self.mm(ps, self.KT[:, kv, j * 128:(j + 1) * 128], self.QT[:, h, 0:nb], True, True, [self.r_KT, self.r_QT[h]], pr)
                pT = self.pT[cnt % 3][:, 0:nb]
                rpT = self.r_pT[cnt % 3]
                self.act(pT, ps, AF.Exp, pr, [rpT], scale=scale)
                self.mm(po, self.V[:, j, kv * 128:(kv + 1) * 128], pT, ji == 0, ji == len(keyt) - 1, [self.r_V, rpT], por)
                self.mm(pd, self.ones_b, pT, ji == 0, ji == len(keyt) - 1, [self.r_const, rpT], pdr)
                cnt += 1
            self.P.op("dve", lambda e, pd=pd: e.reciprocal(out=self.rD[:, 0:nb], in_=pd), pdr, [self.r_rD])
            self.tt("dve", self.mixT[:, h, 0:nb], po, self.rD[:, 0:nb], ALU.mult, por + [self.r_rD], [self.r_mix[h]])

    def ffn_window(self, l, win, last):
        o0, o1, s0, s1 = win
        isctx = (s0 == 0)
        r_ = 1 if isctx else 0
        c0 = max(o0 - 1, s0)
        c1 = min(o1 + 1, s1)
        n = c1 - c0
        a = o0 - c0
        b = o1 - c0
        no = o1 - o0
        rxs = self.r_xM[c0 // 128:(c1 + 127) // 128]
        self.dma("sp", self.xsb[:, :, 0:n], self.xMs[:, :, c0:c1].rearrange("k p n -> p k n"), self.st_ld, rxs, [self.r_x])
        self.norm(self.xsb, self.r_x, n, self.G2[:, r_, :], self.modT[:, r_, 48:64], [self.r_mod],
                  lambda kc: self.hT[:, kc, 0:n], [self.r_h], self.ntmp)
        rhs = lambda kc: self.hT[:, kc, 0:n]
        reqs = [("sp", self.wup_s[l][t], [self.r_wsc[l]["up"]], 8192) for t in range(22)]
        reqs += [("sp", self.wdn_s[l][m], [self.r_wsc[l]["dn"]], FC * 128) for m in range(16)]
        ws = self.wstream(reqs)
        cw = self.convw
        cb = self.convb
        rp = [self.r_par]
        for t in range(22):
            buf, rw = next(ws)
            wt = buf[:, 0:8192].rearrange("p (k j) -> p k j", k=KC)
            for i in range(2):
                j = 2 * t + i
                pg, pgr = self.proj_fm(wt, rw, KC, i * 128, rhs, [self.r_h], n, 0)
                pv, pvr = self.proj_fm(wt, rw, KC, 256 + i * 128, rhs, [self.r_h], n, 1)
                bi = j % 2
                for (ps, pr, acc, racc, fc) in [(pg, pgr, self.accg[bi], self.r_accg[bi], j), (pv, pvr, self.accv[bi], self.r_accv[bi], FC + j)]:
                    self.act(acc[:, a:b], ps[:, a:b], AF.Identity, pr + rp, [racc], bias=cb[:, l, fc:fc + 1], scale=cw[:, l, 1, fc:fc + 1])
                    lo = max(a, 1)
                    self.stt(acc[:, lo:b], ps[:, lo - 1:b - 1], cw[:, l, 0, fc:fc + 1], acc[:, lo:b], ALU.mult, ALU.add, pr + rp + [racc], [racc])
                    hi = min(b, n - 1)
                    self.stt(acc[:, a:hi], ps[:, a + 1:hi + 1], cw[:, l, 2, fc:fc + 1], acc[:, a:hi], ALU.mult, ALU.add, pr + rp + [racc], [racc])
                self.act(self.sil[bi][:, a:b], self.accg[bi][:, a:b], AF.Silu, [self.r_accg[bi]], [self.r_sil[bi]])
                self.tt("pool", self.aT[:, j, 0:no], self.sil[bi][:, a:b], self.accv[bi][:, a:b], ALU.mult,
                        [self.r_sil[bi], self.r_accv[bi]], [self.r_aT[j]])
        for m in range(16):
            buf, rw = next(ws)
            wt = buf[:, 0:FC * 128].rearrange("p (k j) -> p k j", k=FC)
            ps, pr = self.PS(m % 2, 0, no)
            for j in range(FC):
                self.mm(ps, wt[:, j, :], self.aT[:, j, 0:no], j == 0, j == FC - 1, [rw, self.r_aT[j]], pr)
            self.stt(self.xsb[:, m, a:b], ps, self.modT[:, r_, 80 + m:81 + m], self.xsb[:, m, a:b], ALU.mult, ALU.add,
                     pr + [self.r_x, self.r_mod], [self.r_x])
        ros = self.r_xT[o0 // 128:(o1 + 127) // 128]
        if not last:
            self.dma("sp", self.xTs[:, :, o0:o1].rearrange("k p n -> p k n"), self.xsb[:, :, a:b], self.st_st, [self.r_x], ros)
        else:
            sq, rsq, tm, rtm, rstd, rrstd = self.ntmp
            pn, pnr = self.PS(2, 0, no)
            for kc in range(KC):
                i = kc % 2
                self.act(sq[i][:, 0:no], self.xsb[:, kc, a:b], AF.Square, [self.r_x], [rsq[i]])
                self.mm(pn, self.ones_f, sq[i][:, 0:no], kc == 0, kc == KC - 1, [rsq[i], self.r_const], pnr)
            self.act(rstd[:, 0:no], pn, AF.Sqrt, pnr, [rrstd], bias=EPS, scale=1.0 / D)
            self.P.op("dve", lambda e: e.reciprocal(out=rstd[:, 0:no], in_=rstd[:, 0:no]), [rrstd], [rrstd])
            for kc in range(KC):
                self.stt(self.xsb[:, kc, a:b], self.xsb[:, kc, a:b], self.fng[:, kc:kc + 1], rstd[:, 0:no], ALU.mult, ALU.mult,
                         [self.r_x, rrstd, self.r_par], [self.r_x])
            self.dma("sp", self.outT[:, :, o0 - LC:o1 - LC].rearrange("k p n -> p k n"), self.xsb[:, :, a:b], self.st_st,
                     [self.r_x], [self.r_out], join=True)


def _consts():
    n_freq = 32
    inv = (np.float32(10000.0) ** (-np.arange(n_freq, dtype=np.float32) / np.float32(n_freq))).astype(np.float32)
    t = np.arange(T)
    row = (t // 64).astype(np.float32)
    col = (t % 64).astype(np.float32)
    ang = np.concatenate([row[:, None] * inv[None, :], col[:, None] * inv[None, :]], axis=-1).astype(np.float32)
    cos = np.cos(ang).astype(np.float32)
    sin = np.sin(ang).astype(np.float32)
    ropeT = np.stack([np.repeat(cos, 2, axis=1).T, np.repeat(sin, 2, axis=1).T]).astype(np.float32)
    rotT = np.zeros((128, 128), np.float32)
    for i in range(64):
        rotT[2 * i + 1, 2 * i] = -1.0
        rotT[2 * i, 2 * i + 1] = 1.0
    ident = np.eye(128).astype(ml_dtypes.bfloat16)
    s = np.arange(128)[:, None]
    tt = np.arange(128)[None, :]
    mA = ((s // 32 == tt // 32) & (s <= tt))
    mB = ((s // 64 == tt // 64) & (s % 64 < 32) & (tt % 64 >= 32))
    mC = ((s < 64) & (tt >= 64))
    masks = np.stack([mA, mB, mC, mA.T, mB.T, mC.T], axis=1).astype(np.float32)
    seg = np.ones((128, 3, 128), np.float32)
    seg[:, 0, ::32] = 0
    seg[:, 1, ::64] = 0
    seg[:, 2, 0] = 0
    return dict(ropeT=np.ascontiguousarray(ropeT), rotT=rotT, ident=ident, identf=np.eye(128, dtype=np.float32), masks=np.ascontiguousarray(masks), seg=seg)


def _layout(inputs, b, NL=4):
    f = lambda a: np.ascontiguousarray(a, dtype=np.float32)
    x = inputs["x"][b]
    ctx = inputs["ctx"][b]
    xc = np.concatenate([ctx, x], axis=0)
    m = {}
    m["xT0"] = f(xc.T.reshape(KC, 128, U))
    cc = np.stack([inputs["c"][b], inputs["c_ctx"]], axis=-1)
    m["cT"] = f(cc.reshape(KC, 128, 2).transpose(1, 0, 2))
    for k in ("w_ada", "w_in", "w_out", "w_up", "w_down"):
        m[k] = f(inputs[k][:NL])
    m["badaT"] = f(inputs["b_ada"].reshape(4, 96, 128).transpose(2, 0, 1))
    m["n1g"] = f(inputs["norm1_g"].reshape(4, KC, 128).transpose(2, 0, 1))
    m["n2g"] = f(inputs["norm2_g"].reshape(4, KC, 128).transpose(2, 0, 1))
    m["fng"] = f(inputs["final_norm_g"].reshape(KC, 128).T)
    m["qkg"] = f(np.stack([inputs["q_norm_g"], inputs["k_norm_g"]], axis=-1).transpose(1, 0, 2))
    m["hgg"] = f(inputs["hg_norm_g"].T)
    m["lbp"] = f(inputs["hg_lower_bounds"].reshape(2, 4, 4, 128).transpose(3, 0, 1, 2))
    m["sgg"] = f(inputs["sg_norm_g"].reshape(4, 1, 512))
    m["sgb"] = f(inputs["sg_b"].reshape(4, 1, 512))
    m["sgwT"] = np.ascontiguousarray(inputs["sg_w"].transpose(0, 3, 1, 2), dtype=np.float32)
    m["convw"] = f(inputs["conv_w"].reshape(4, 3, 88, 128).transpose(3, 0, 1, 2))
    m["convb"] = f(inputs["conv_b"].reshape(4, 88, 128).transpose(2, 0, 1))
    return m


_NC_CACHE = {}


def kernel(**inputs):
    inputs = {k: np.asarray(v) for k, v in inputs.items()}
    if "nc" not in _NC_CACHE:
        _NC_CACHE["nc"] = Builder(4, False).build()
    nc = _NC_CACHE["nc"]
    consts = _consts()
    in_maps = []
    for b in range(N_CORES):
        m = _layout(inputs, b)
        m.update(consts)
        in_maps.append(m)
    res = run_bass_kernel_spmd(nc, in_maps, core_ids=list(range(N_CORES)))
    out = np.empty((4, T, D), np.float32)
    for b in range(4):
        out[b] = res.results[b]["outT"].reshape(D, T).T
    return out
```

```python
import contextlib
import numpy as np
import ml_dtypes
import concourse.bass as bass
import concourse.mybir as mybir
from concourse.bass_utils import run_bass_kernel_spmd

F32 = mybir.dt.float32
BF16 = mybir.dt.bfloat16
AF = mybir.ActivationFunctionType
ALU = mybir.AluOpType
AX = mybir.AxisListType

import os
SEM_LIMIT = int(os.environ.get("SEM_LIMIT", "16000"))
D = 2048
KC = 16
T = 4096
LC = 256
U = T + LC
DFF = 5632
FC = 44
EPS = 1e-6
F_MIN = 1e-30
NB = 256
FW = 456
N_CORES = 4


class Res:
    __slots__ = ("name", "writers", "readers", "excl")

    def __init__(self, name, excl=False):
        self.name = name
        self.writers = {}
        self.readers = {}
        self.excl = excl


class Stream:
    def __init__(self, name, nslots):
        self.name = name
        self.slots = [[None, 0] for _ in range(nslots)]
        self.i = 0


def _merge(d, s, v):
    if d.get(s, 0) < v:
        d[s] = v


class Prog:
    ENG = ("pe", "act", "dve", "pool", "sp")

    def __init__(self, nc, stack):
        self.nc = nc
        self.stack = stack
        self.ops = {e: [] for e in self.ENG}
        self.esem = {e: None for e in self.ENG}
        self.ecnt = {e: 0 for e in self.ENG}
        self.known = {e: {} for e in self.ENG}
        self.nsem = 0
        self.nres = 0
        self.esems = {e: set() for e in self.ENG}
        self.streams = []
        self.nops = 0

    def new_sem(self, name):
        self.nsem += 1
        return self.stack.enter_context(self.nc.semaphore(f"{name}_{self.nsem}"))

    def res(self, name=None, excl=False):
        self.nres += 1
        return Res(name or f"r{self.nres}", excl)

    def stream(self, name, nslots):
        st = Stream(name, nslots)
        self.streams.append(st)
        return st

    def sb(self, name, shape, dtype):
        return self.stack.enter_context(self.nc.sbuf_tensor(name, list(shape), dtype))

    def ps(self, name, shape, dtype=F32):
        return self.stack.enter_context(self.nc.psum_tensor(name, list(shape), dtype))

    def _waits(self, eng, deps):
        waits = {}
        kn = self.known[eng]
        own = self.esems[eng]
        for (s, v) in deps:
            if kn.get(s, 0) >= v:
                continue
            if eng == "pe" and s in own:
                continue
            _merge(waits, s, v)
        for s, v in waits.items():
            kn[s] = v
        return list(waits.items())

    def _deps(self, reads, writes, join):
        deps = []
        for r in reads:
            deps.extend(r.writers.items())
        for r in writes:
            if join and not r.readers:
                continue
            deps.extend(r.writers.items())
            deps.extend(r.readers.items())
        return deps

    def _post(self, ev, reads, writes, join):
        s, v = ev
        for r in reads:
            _merge(r.readers, s, v)
        for r in writes:
            if join and not r.readers:
                _merge(r.writers, s, v)
            else:
                r.writers = {s: v}
                r.readers = {}

    def op(self, eng, fn, reads=(), writes=()):
        ex = [r for r in reads if r.excl]
        if ex:
            reads = [r for r in reads if not r.excl]
            writes = list(writes) + ex
        waits = self._waits(eng, self._deps(reads, writes, False))
        if self.esem[eng] is None or self.ecnt[eng] >= SEM_LIMIT:
            self.esem[eng] = self.new_sem(eng)
            self.esems[eng].add(self.esem[eng])
            self.ecnt[eng] = 0
        self.ecnt[eng] += 1
        ev = (self.esem[eng], self.ecnt[eng])
        self.ops[eng].append((waits, fn, self.esem[eng], 1))
        self._post(ev, reads, writes, False)
        self.nops += 1
        return ev

    def dma(self, eng, fn, stream, reads=(), writes=(), join=False):
        slot = stream.slots[stream.i]
        stream.i = (stream.i + 1) % len(stream.slots)
        deps = self._deps(reads, writes, join)
        if slot[0] is not None:
            deps.append((slot[0], slot[1]))
        waits = self._waits(eng, deps)
        if slot[0] is None or slot[1] + 16 > SEM_LIMIT:
            slot[0] = self.new_sem("d" + stream.name)
            slot[1] = 0
        slot[1] += 16
        ev = (slot[0], slot[1])
        self.ops[eng].append((waits, fn, slot[0], 16))
        self._post(ev, reads, writes, join)
        self.nops += 1
        return ev

    def all_events(self):
        evs = []
        for e in self.ENG:
            if self.esem[e] is not None:
                evs.append((self.esem[e], self.ecnt[e]))
        for st in self.streams:
            for sl in st.slots:
                if sl[0] is not None:
                    evs.append((sl[0], sl[1]))
        return evs

    def barrier(self, engines=None):
        evs = self.all_events()
        for e in (engines or self.ENG):
            w = self._waits(e, evs)
            if w:
                self.ops[e].append((w, None, None, 0))

    def emit(self):
        nc = self.nc
        block = self.stack.enter_context(nc.Block())
        ops = self.ops

        def run(e, name):
            for (waits, fn, sem, inc) in ops[name]:
                for (s, v) in waits:
                    e.wait_ge(s, v)
                if fn is not None:
                    fn(e).then_inc(sem, inc)

        @block.tensor
        def _(e):
            run(e, "pe")

        @block.scalar
        def _(e):
            run(e, "act")

        @block.vector
        def _(e):
            run(e, "dve")

        @block.gpsimd
        def _(e):
            run(e, "pool")

        @block.sync
        def _(e):
            run(e, "sp")


class Arena:
    def __init__(self, tensor, n):
        self.t = tensor
        self.n = n
        self.off = 0

    def reset(self):
        self.off = 0

    def take(self, *shape):
        n = int(np.prod(shape))
        assert self.off + n <= self.n, (self.off, n, self.n)
        ap = self.t[:, self.off:self.off + n]
        self.off += n
        if len(shape) == 2:
            ap = ap.rearrange("p (a b) -> p a b", a=shape[0])
        elif len(shape) == 3:
            ap = ap.rearrange("p (a b c) -> p a b c", a=shape[0], b=shape[1])
        return ap


class Builder:
    def __init__(self, NL=4, dbg=False, stop=None):
        self.NL = NL
        self.dbg = dbg
        self.stop = stop

    def mm(self, out, lhsT, rhs, st, sp, rd, wr):
        self.P.op("pe", lambda e: e.matmul(out, lhsT=lhsT, rhs=rhs, start=st, stop=sp), rd, wr)

    def act(self, out, in_, func, rd, wr, bias=None, scale=None):
        kw = {}
        if bias is not None:
            kw["bias"] = bias
        if scale is not None:
            kw["scale"] = scale
        self.P.op("act", lambda e: e.activation(out=out, in_=in_, func=func, **kw), rd, wr)

    def tt(self, eng, out, in0, in1, op, rd, wr):
        self.P.op(eng, lambda e: e.tensor_tensor(out=out, in0=in0, in1=in1, op=op), rd, wr)

    def ts(self, eng, out, in0, s1, s2, op0, op1, rd, wr):
        if op1 is None:
            self.P.op(eng, lambda e: e.tensor_scalar(out=out, in0=in0, scalar1=s1, scalar2=None, op0=op0), rd, wr)
        else:
            self.P.op(eng, lambda e: e.tensor_scalar(out=out, in0=in0, scalar1=s1, scalar2=s2, op0=op0, op1=op1), rd, wr)

    def stt(self, out, in0, scalar, in1, op0, op1, rd, wr):
        self.P.op("dve", lambda e: e.scalar_tensor_tensor(out=out, in0=in0, scalar=scalar, in1=in1, op0=op0, op1=op1), rd, wr)

    def cp(self, eng, out, in_, rd, wr):
        if eng == "act":
            self.P.op("act", lambda e: e.activation(out=out, in_=in_, func=AF.Copy), rd, wr)
        else:
            self.P.op(eng, lambda e: e.tensor_copy(out=out, in_=in_), rd, wr)

    def dma(self, eng, out, in_, stream, rd, wr, join=False, slow=False):
        if slow:
            self.P.dma(eng, lambda e: e.dma_start(out=out, in_=in_, allow_slow_non_contiguous=True), stream, rd, wr, join)
        else:
            self.P.dma(eng, lambda e: e.dma_start(out=out, in_=in_), stream, rd, wr, join)

    def PS(self, b, c0, c1, rows=128):
        ap = self.pb[b][0:rows, c0:c1]
        return ap, [self.pres[b]]

    def wload(self, eng, src, src_res, nelem):
        i = self.wi
        self.wi = (i + 1) % len(self.wbuf)
        buf = self.wbuf[i]
        r = self.wres[i]
        self.dma(eng, buf[:, 0:nelem], src, self.st_w, src_res, [r])
        return buf, r

    def wstream(self, reqs, depth=1):
        loaded = []
        for i in range(len(reqs)):
            while len(loaded) < min(len(reqs), i + 1 + depth):
                loaded.append(self.wload(*reqs[len(loaded)]))
            yield loaded[i]

    def build(self):
        NL = self.NL
        nc = bass.Bass("TRN2", target_bir_lowering=False)
        self.nc = nc

        def EI(n, s, d=F32):
            return nc.dram_tensor(n, list(s), d, kind="ExternalInput").ap()

        def SC(n, s, d=F32):
            return nc.dram_tensor(n, list(s), d, kind="Internal").ap()

        self.xT0 = EI("xT0", [KC, 128, U])
        self.cT_d = EI("cT", [128, KC, 2])
        self.w_ada = EI("w_ada", [NL, D, 6 * D])
        self.w_in = EI("w_in", [NL, D, 5120])
        self.w_out = EI("w_out", [NL, D, D])
        self.w_up = EI("w_up", [NL, D, 2 * DFF])
        self.w_down = EI("w_down", [NL, DFF, D])
        self.badaT_d = EI("badaT", [128, 4, 96])
        self.n1g_d = EI("n1g", [128, 4, KC])
        self.n2g_d = EI("n2g", [128, 4, KC])
        self.fng_d = EI("fng", [128, KC])
        self.qkg_d = EI("qkg", [128, 4, 2])
        self.hgg_d = EI("hgg", [128, 4])
        self.lbp_d = EI("lbp", [128, 2, 4, 4])
        self.sgg_d = EI("sgg", [4, 1, 512])
        self.sgb_d = EI("sgb", [4, 1, 512])
        self.sgwT_d = EI("sgwT", [4, 128, 4, 128])
        self.convw_d = EI("convw", [128, 4, 3, 88])
        self.convb_d = EI("convb", [128, 4, 88])
        self.rope_d = EI("ropeT", [2, 128, T])
        self.rotT_d = EI("rotT", [128, 128])
        self.ident_d = EI("ident", [128, 128], BF16)
        self.identf_d = EI("identf", [128, 128])
        self.masks_d = EI("masks", [128, 6, 128])
        self.seg_d = EI("seg", [128, 3, 128])
        self.outT = nc.dram_tensor("outT", [KC, 128, T], F32, kind="ExternalOutput").ap()
        if self.dbg:
            self.dbg1 = nc.dram_tensor("dbg1", [KC, 128, U], F32, kind="ExternalOutput").ap()
            self.dbg2 = nc.dram_tensor("dbg2", [KC, 128, U], F32, kind="ExternalOutput").ap()
            self.dbgm = nc.dram_tensor("dbgm", [KC, 128, U], F32, kind="ExternalOutput").ap()
            self.dbgs = nc.dram_tensor("dbgs", [128, 6600], F32, kind="ExternalOutput").ap()

        self.xTs = SC("xTs", [KC, 128, U])
        self.xMs = SC("xMs", [KC, 128, U])
        self.obT = SC("obT", [4, 128, U])
        self.modd = SC("modd", [2, 6 * D])
        self.win_s = [SC(f"win_s{l}", [10, 128, KC * 512], BF16) for l in range(NL)]
        self.wout_s = [SC(f"wout_s{l}", [4, 128, KC * 512], BF16) for l in range(NL)]
        self.wup_s = [SC(f"wup_s{l}", [22, 128, KC * 512], BF16) for l in range(NL)]
        self.wdn_s = [SC(f"wdn_s{l}", [16, 128, FC * 128], BF16) for l in range(NL)]

        with contextlib.ExitStack() as stack:
            P = Prog(nc, stack)
            self.P = P
            self.alloc()
            self.setup()
            for l in range(NL):
                self.layer(l)
            P.barrier()
            P.emit()
        return nc

    def alloc(self):
        P = self.P
        self.st_w = P.stream("w", 2)
        self.st_ld = P.stream("ld", 4)
        self.st_st = P.stream("st", 4)
        self.st_cast = P.stream("cast", 4)
        self.st_x0 = P.stream("x0", 4)
        self.pb = [P.ps(f"pb{i}", [128, 512]) for i in range(7)]
        self.pres = [P.res(f"pb{i}", excl=True) for i in range(7)]
        self.pb7 = P.ps("pb7", [128, 1024], BF16)
        self.p7res = [P.res("pb7", excl=True)] * 8
        self.wbuf = [P.sb(f"wbuf{i}", [128, 8192], BF16) for i in range(2)]
        self.wres = [P.res(f"wbuf{i}") for i in range(2)]
        self.wi = 0
        self.A32 = Arena(P.sb("A32", [128, 16384], F32), 16384)
        self.A16 = Arena(P.sb("A16", [128, 37200], BF16), 37200)
        self.C32t = P.sb("C32", [128, 6600], F32)
        C32 = Arena(self.C32t, 6600)
        C16 = Arena(P.sb("C16", [128, 1400], BF16), 1400)
        self.masks = C32.take(6, 128)
        self.seg = C32.take(3, 128)
        self.ones_f = C32.take(128)
        self.rotT = C32.take(128)
        self.ident_f = C32.take(128)
        self.sggB = C32.take(512)
        self.sgbB = C32.take(512)
        self.n1g = C32.take(4, KC)
        self.n2g = C32.take(4, KC)
        self.fng = C32.take(KC)
        self.qkg = C32.take(4, 2)
        self.hgg = C32.take(4)
        self.lbe = C32.take(2, 4, 4)
        self.lb = C32.take(2, 4, 4)
        self.oml = C32.take(2, 4, 4)
        self.lbs = C32.take(2, 4)
        self.convw = C32.take(4, 3, 88)
        self.convb = C32.take(4, 88)
        self.badaT = C32.take(4, 96)
        self.cT = C32.take(KC, 2)
        self.modT = C32.take(2, 96)
        self.G1 = C32.take(2, KC)
        self.G2 = C32.take(2, KC)
        self.modrow = [C32.take(512), C32.take(512)]
        self.state_f = C32.take(4, 128)
        self.dec = C32.take(8)
        self.ones_b = C16.take(128)
        self.ident = C16.take(128)
        self.scT = C16.take(KC, 2)
        self.sgwT = C16.take(4, 128)
        self.state_b = C16.take(4, 128)
        R = P.res
        self.r_const = R("const")
        self.r_par = R("par")
        self.r_sg = R("sgpar")
        self.r_lb = R("lb")
        self.r_mod = R("mod")
        self.r_modrow = [R("modrow0"), R("modrow1")]
        self.r_modd = R("modd")
        self.r_state = [R(f"state{h}") for h in range(4)]
        self.r_dec = [R(f"dec{h}") for h in range(8)]
        self.itc = 0
        self.acnt = 0
        self.r_xT = [R(f"xT{j}") for j in range(U // 128)]
        self.r_xM = [R(f"xM{j}") for j in range(U // 128)]
        self.r_ob = [R(f"ob{j}") for j in range(U // 128)]
        self.r_wsc = [{m: R(f"wsc{l}{m}") for m in ("in", "out", "up", "dn")} for l in range(self.NL)]
        self.r_out = R("out")
        self.r_dbg = R("dbg")

    def setup(self):
        P = self.P
        ld = self.st_ld
        rc = [self.r_const]
        rp = [self.r_par]
        for (dst, src) in [(self.masks, self.masks_d), (self.seg, self.seg_d), (self.rotT, self.rotT_d), (self.ident_f, self.identf_d),
                           (self.ident, self.ident_d)]:
            self.dma("sp", dst, src, ld, [], rc, join=True)
        for (dst, src) in [(self.n1g, self.n1g_d), (self.n2g, self.n2g_d), (self.fng, self.fng_d),
                           (self.qkg, self.qkg_d), (self.hgg, self.hgg_d), (self.lbe, self.lbp_d),
                           (self.convw, self.convw_d), (self.convb, self.convb_d), (self.badaT, self.badaT_d),
                           (self.cT, self.cT_d)]:
            self.dma("sp", dst, src, ld, [], rp, join=True)
        P.op("dve", lambda e: e.memset(self.ones_f, 1.0), [], rc)
        P.op("dve", lambda e: e.memset(self.ones_b, 1.0), [], rc)
        self.act(self.scT, self.cT, AF.Silu, rp, [self.r_lb])
        lbe, lb, oml, lbs = self.lbe, self.lb, self.oml, self.lbs
        rl = [self.r_lb]
        self.act(lbe, lbe, AF.Exp, rp + rl, rp)
        self.tt("dve", lbs, lbe[:, :, 0, :], lbe[:, :, 1, :], ALU.add, rp, rl)
        self.tt("dve", lbs, lbs, lbe[:, :, 2, :], ALU.add, rp + rl, rl)
        self.tt("dve", lbs, lbs, lbe[:, :, 3, :], ALU.add, rp + rl, rl)
        P.op("dve", lambda e: e.reciprocal(out=lbs, in_=lbs), rl, rl)
        P.op("dve", lambda e: e.memset(lb[:, :, 0, :], 0.0), [], rl)
        self.tt("dve", lb[:, :, 1, :], lbe[:, :, 1, :], lbs, ALU.mult, rp + rl, rl)
        for j in (2, 3):
            self.tt("dve", lb[:, :, j, :], lbe[:, :, j, :], lbs, ALU.mult, rp + rl, rl)
            self.tt("dve", lb[:, :, j, :], lb[:, :, j, :], lb[:, :, j - 1, :], ALU.add, rl, rl)
        self.ts("dve", oml, lb, -1.0, 1.0, ALU.mult, ALU.add, rl, rl)
        for kc in range(KC):
            self.dma("sp", self.xTs[kc], self.xT0[kc], self.st_x0, [], self.r_xT, join=True)
        for l in range(self.NL):
            self.cast_weights(l)

    def cast_weights(self, l):
        cs = self.st_cast
        rw = self.r_wsc[l]
        for g in range(10):
            self.dma("pool", self.win_s[l][g].rearrange("p (k j) -> p k j", k=KC),
                     self.w_in[l][:, g * 512:(g + 1) * 512].rearrange("(k p) j -> p k j", p=128),
                     cs, [], [rw["in"]], join=True)
        for g in range(4):
            self.dma("pool", self.wout_s[l][g].rearrange("p (k j) -> p k j", k=KC),
                     self.w_out[l][:, g * 512:(g + 1) * 512].rearrange("(k p) j -> p k j", p=128),
                     cs, [], [rw["out"]], join=True)
        for t in range(22):
            dst = self.wup_s[l][t].rearrange("p (k j) -> p k j", k=KC)
            self.dma("pool", dst[:, :, 0:256],
                     self.w_up[l][:, t * 256:(t + 1) * 256].rearrange("(k p) j -> p k j", p=128),
                     cs, [], [rw["up"]], join=True)
            self.dma("pool", dst[:, :, 256:512],
                     self.w_up[l][:, DFF + t * 256:DFF + (t + 1) * 256].rearrange("(k p) j -> p k j", p=128),
                     cs, [], [rw["up"]], join=True)
        for m in range(16):
            self.dma("pool", self.wdn_s[l][m].rearrange("p (k j) -> p k j", k=FC),
                     self.w_down[l][:, m * 128:(m + 1) * 128].rearrange("(k p) j -> p k j", p=128),
                     cs, [], [rw["dn"]], join=True)

    def ada(self, l):
        P = self.P
        reqs = []
        for n in range(24):
            reqs.append(("pool", self.w_ada[l][:, n * 512:(n + 1) * 512].rearrange("(k p) j -> p k j", p=128), [], 8192))
        n = 0
        for (buf, r) in self.wstream_ada(reqs):
            wt = buf[:, 0:8192].rearrange("p (k j) -> p k j", k=KC)
            ps, pr = self.PS(n % 2, 0, 512, rows=2)
            for kc in range(KC):
                self.mm(ps, self.scT[:, kc, :], wt[:, kc, :], kc == 0, kc == KC - 1, [r, self.r_lb], pr)
            mr = self.modrow[n % 2]
            rr = self.r_modrow[n % 2]
            self.cp("act", mr[0:2, :], ps, pr, [rr])
            for cc in range(4):
                c = n * 4 + cc
                pm, pmr = self.PS(2, 0, 192)
                self.mm(pm[:, c * 2:c * 2 + 2], mr[0:2, cc * 128:(cc + 1) * 128], self.ident_f[0:2, 0:2], True, True,
                        [rr, self.r_const], pmr)
            n += 1
        rm = [self.r_mod]
        pm, pmr = self.PS(2, 0, 192)
        pm3 = pm.rearrange("p (c r) -> p c r", r=2)
        for r_ in range(2):
            self.cp("dve", self.modT[:, r_, :], pm3[:, :, r_], pmr, rm)
        for r_ in range(2):
            self.tt("dve", self.modT[:, r_, :], self.modT[:, r_, :], self.badaT[:, l, :], ALU.add, rm + [self.r_par], rm)
        for r_ in range(2):
            self.stt(self.G1[:, r_, :], self.modT[:, r_, 16:32], 1.0, self.n1g[:, l, :], ALU.add, ALU.mult, rm + [self.r_par], rm)
            self.stt(self.G2[:, r_, :], self.modT[:, r_, 64:80], 1.0, self.n2g[:, l, :], ALU.add, ALU.mult, rm + [self.r_par], rm)

    def wstream_ada(self, reqs):
        loaded = []
        for i in range(len(reqs)):
            while len(loaded) < min(len(reqs), i + 2):
                eng, src, sres, nelem = reqs[len(loaded)]
                k = self.wi
                self.wi = (k + 1) % len(self.wbuf)
                buf = self.wbuf[k]
                r = self.wres[k]
                self.dma(eng, buf[:, 0:nelem].rearrange("p (k j) -> p k j", k=KC), src, self.st_w, sres, [r])
                loaded.append((buf, r))
            yield loaded[i]

    def norm(self, x, rx, n, G, S, rg, dst, rdst, tmp):
        sq, rsq, tm, rtm, rstd, rrstd = tmp
        pn, pnr = self.PS(2, 0, n)
        for kc in range(KC):
            i = kc % 2
            self.act(sq[i][:, 0:n], x[:, kc, 0:n], AF.Square, [rx], [rsq[i]])
            self.mm(pn, self.ones_f, sq[i][:, 0:n], kc == 0, kc == KC - 1, [rsq[i], self.r_const], pnr)
        self.act(rstd[:, 0:n], pn, AF.Sqrt, pnr, [rrstd], bias=EPS, scale=1.0 / D)
        self.P.op("dve", lambda e: e.reciprocal(out=rstd[:, 0:n], in_=rstd[:, 0:n]), [rrstd], [rrstd])
        for kc in range(KC):
            i = kc % 2
            self.tt("dve", tm[i][:, 0:n], x[:, kc, 0:n], rstd[:, 0:n], ALU.mult, [rx, rrstd], [rtm[i]])
            if S is not None:
                self.act(dst(kc), tm[i][:, 0:n], AF.Identity, [rtm[i]] + rg, rdst, bias=S[:, kc:kc + 1], scale=G[:, kc:kc + 1])
            else:
                self.act(dst(kc), tm[i][:, 0:n], AF.Identity, [rtm[i]] + rg, rdst, scale=G[:, kc:kc + 1])

    def proj_fm(self, wt, rw, kcn, mo, rhs, rrhs, n, bank):
        ps, pr = self.PS(bank, 0, n)
        for kc in range(kcn):
            self.mm(ps, wt[:, kc, mo:mo + 128], rhs(kc), kc == 0, kc == kcn - 1, [rw] + rrhs, pr)
        return ps, pr

    def proj_tm(self, wt, rw, c0, w, hT, rh, t0, bank):
        ps, pr = self.PS(bank, 0, w)
        for kc in range(KC):
            self.mm(ps, hT[:, kc, t0:t0 + 128], wt[:, kc, c0:c0 + w], kc == 0, kc == KC - 1, [rw, rh], pr)
        return ps, pr

    def alloc_p12(self):
        A32, A16, R = self.A32, self.A16, self.P.res
        A32.reset()
        A16.reset()
        n = NB
        self.xsb = A32.take(KC, n); self.r_x = R("xsb")
        self.ntmp = ([A32.take(n), A32.take(n)], [R("sq0"), R("sq1")], [A32.take(n), A32.take(n)], [R("tm0"), R("tm1")],
                     A32.take(n), R("rstd"))
        self.qS = A32.take(4, n); self.lfS = A32.take(4, n); self.kkS = A32.take(4, n)
        self.r_qS = [R(f"qS{h}") for h in range(4)]
        self.r_lfS = [R(f"lfS{h}") for h in range(4)]
        self.r_kkS = [R(f"kkS{h}") for h in range(4)]
        self.t1 = A32.take(n); self.t2 = A32.take(n); self.r_t1 = R("t1"); self.r_t2 = R("t2")
        self.rope = A32.take(2, n); self.r_rope = R("rope")
        self.hg = {}
        for nm in ["P32", "P64", "P128", "U32", "U64", "U128", "EA", "nW32", "nW64", "XA", "YA", "XB", "YB", "XC", "YC",
                   "XD", "YD", "Sf", "tB", "tC", "oS", "sqo", "rso"]:
            self.hg[nm] = (A32.take(128), R("hg_" + nm))
        self.gv = A32.take(512); self.sqv = A32.take(512); self.vn = A32.take(512)
        self.ssum = A32.take(4); self.tsg = A32.take(128)
        self.r_gv = R("gv"); self.r_sqv = R("sqv"); self.r_vn = R("vn"); self.r_ssum = R("ssum"); self.r_tsg = R("tsg")
        self.rD = A32.take(n); self.r_rD = R("rD")
        self.obst = A32.take(4, 128); self.r_obst = R("obst")
        self.obld = A32.take(4, 128); self.r_obld = R("obld")
        self.sgm = A32.take(n); self.fS = A32.take(n); self.r_sgm = R("sgm"); self.r_fS = R("fS")
        self.hT = A16.take(KC, n); self.r_h = R("hT")
        self.KT = A16.take(2, U); self.r_KT = R("KT")
        self.V = A16.take(U // 128, 256); self.r_V = R("V")
        self.QT = A16.take(8, n); self.r_QT = [R(f"QT{h}") for h in range(8)]
        self.mixT = A16.take(KC, n); self.r_mix = [R(f"mix{c}") for c in range(KC)]
        self.pT = [A16.take(n) for _ in range(3)]; self.r_pT = [R(f"pT{i}") for i in range(3)]
        self.vS = A16.take(n // 128, 512); self.r_vS = [R(f"vS{i}") for i in range(n // 128)]
        self.vnb = A16.take(512); self.r_vnb = R("vnb")
        self.gateS = A16.take(4, n); self.uS = A16.take(4, n)
        self.r_gateS = [R(f"gateS{h}") for h in range(4)]
        self.r_uS = [R(f"uS{h}") for h in range(4)]
        self.hop = [{nm: (A16.take(128), R(f"hop{i}_{nm}")) for nm in ["qA", "kA", "qB", "kB", "qC", "kC", "qD", "kD", "Sb", "kDt"]}
                    for i in range(4)]

    def alloc_p3(self):
        A32, A16, R = self.A32, self.A16, self.P.res
        A32.reset()
        A16.reset()
        n = FW + 8
        self.xsb = A32.take(KC, n); self.r_x = R("xsb3")
        self.ntmp = ([A32.take(n), A32.take(n)], [R("sq0"), R("sq1")], [A32.take(n), A32.take(n)], [R("tm0"), R("tm1")],
                     A32.take(n), R("rstd"))
        self.accg = [A32.take(n) for _ in range(2)]; self.accv = [A32.take(n) for _ in range(2)]
        self.sil = [A32.take(n) for _ in range(2)]
        self.r_accg = [R("accg0"), R("accg1")]; self.r_accv = [R("accv0"), R("accv1")]; self.r_sil = [R("sil0"), R("sil1")]
        self.hT = A16.take(KC, n); self.r_h = R("hT3")
        self.aT = A16.take(FC, n); self.r_aT = [R(f"aT{j}") for j in range(FC)]

    def layer(self, l):
        P = self.P
        last = (l == 3)
        if self.stop == "setup":
            self.dump_c32()
            return
        self.ada(l)
        P.barrier()
        if self.stop == "ada":
            self.dump_c32()
            return
        self.alloc_p12()
        self.dma("sp", self.sggB, self.sgg_d[l].partition_broadcast(128), self.st_ld, [], [self.r_sg], join=True)
        self.dma("sp", self.sgbB, self.sgb_d[l].partition_broadcast(128), self.st_ld, [], [self.r_sg], join=True)
        self.dma("pool", self.sgwT, self.sgwT_d[l], self.st_ld, [], [self.r_sg], join=True)
        blocks = [(0, LC, True)] + [(LC + j * NB, NB, False) for j in range(T // NB)]
        self.reset_state()
        p1b = [blocks[0]] + blocks[:0:-1]
        if self.stop and self.stop.startswith("p1b"):
            p1b = p1b[:int(self.stop[3:])]
        for blk in p1b:
            self.pass1_block(l, blk)
        if self.stop and self.stop.startswith("p1"):
            return
        self.reset_state()
        p2b = blocks
        if self.stop and self.stop.startswith("p2b"):
            p2b = p2b[:int(self.stop[3:])]
        for blk in p2b:
            self.pass2_block(l, blk, last)
        if self.dbg and l == 0:
            for kc in range(KC):
                self.dma("sp", self.dbg1[kc], self.xMs[kc], self.st_x0, self.r_xM, [self.r_dbg], join=True)
        P.barrier()
        if self.stop and self.stop.startswith("p2"):
            return
        self.alloc_p3()
        wins = []
        if not last:
            wins.append((0, LC, 0, LC))
        o = 0
        while o < T:
            n = min(FW, T - o)
            wins.append((LC + o, LC + o + n, LC, U))
            o += n
        for w in wins:
            self.ffn_window(l, w, last)
        if self.dbg and l == 0:
            for kc in range(KC):
                self.dma("sp", self.dbg2[kc], self.xTs[kc], self.st_x0, self.r_xT, [self.r_dbg], join=True)
        P.barrier()

    def dump_c32(self):
        self.P.barrier()
        self.dma("sp", self.dbgs, self.C32t[:, :], self.st_st, [], [self.r_dbg])

    def reset_state(self):
        for h in range(4):
            self.P.op("dve", lambda e, h=h: e.memset(self.state_f[:, h, :], 0.0), [], [self.r_state[h]])
            self.P.op("pool", lambda e, h=h: e.memset(self.state_b[:, h, :], 0.0), [], [self.r_state[h]])

    def load_x(self, u0, n):
        rs = self.r_xT[u0 // 128:(u0 + n + 127) // 128]
        self.dma("sp", self.xsb[:, :, 0:n], self.xTs[:, :, u0:u0 + n].rearrange("k p n -> p k n"), self.st_ld, rs, [self.r_x])
        return rs

    def block_norm1(self, l, blk):
        u0, nb, isctx = blk
        r_ = 1 if isctx else 0
        self.load_x(u0, nb)
        self.norm(self.xsb, self.r_x, nb, self.G1[:, r_, :], self.modT[:, r_, 0:16], [self.r_mod],
                  lambda kc: self.hT[:, kc, 0:nb], [self.r_h], self.ntmp)

    def win_req(self, l, g):
        return ("sp", self.win_s[l][g], [self.r_wsc[l]["in"]], 8192)

    def qk_head(self, l, ps, pr, n, gcol, rope, dst, rdst):
        sq, rsq, tm, rtm, rstd, rrstd = self.ntmp
        kraw, r_kraw = tm[0], rtm[0]
        kn, r_kn = tm[1], rtm[1]
        self.cp("act", kraw[:, 0:n], ps, pr, [r_kraw])
        self.tt("pool", sq[0][:, 0:n], kraw[:, 0:n], kraw[:, 0:n], ALU.mult, [r_kraw], [rsq[0]])
        pn, pnr = self.PS(2, 0, n)
        self.mm(pn, self.ones_f, sq[0][:, 0:n], True, True, [rsq[0], self.r_const], pnr)
        self.rstd_lnexp(sq[1][:, 0:n], pn, pnr, [rsq[1]], 1.0 / 128)
        self.stt(kn[:, 0:n], kraw[:, 0:n], gcol, sq[1][:, 0:n], ALU.mult, ALU.mult, [r_kraw, rsq[1], self.r_par], [r_kn])
        if rope:
            pro, prr = self.PS(3, 256, 256 + n)
            self.mm(pro, self.rotT, kn[:, 0:n], True, True, [r_kn, self.r_const], prr)
            self.tt("dve", self.t1[:, 0:n], kn[:, 0:n], self.rope[:, 0, 0:n], ALU.mult, [r_kn, self.r_rope], [self.r_t1])
            self.tt("dve", self.t2[:, 0:n], pro, self.rope[:, 1, 0:n], ALU.mult, prr + [self.r_rope], [self.r_t2])
            self.tt("pool", dst, self.t1[:, 0:n], self.t2[:, 0:n], ALU.add, [self.r_t1, self.r_t2], rdst)
        else:
            self.cp("act", dst, kn[:, 0:n], [r_kn], rdst)

    def load_rope(self, u0, nb):
        t0 = u0 - LC
        self.dma("sp", self.rope[:, :, 0:nb], self.rope_d[:, :, t0:t0 + nb].rearrange("a p n -> p a n"), self.st_ld,
                 [], [self.r_rope])

    def hgrn_q(self, wq, rwq, nb):
        rhs = lambda kc: self.hT[:, kc, 0:nb]
        for hd in range(4):
            ps, pr = self.proj_fm(wq, rwq, KC, hd * 128, rhs, [self.r_h], nb, hd % 2)
            self.act(self.qS[:, hd, 0:nb], ps, AF.Silu, pr, [self.r_qS[hd]])

    def hgrn_f(self, l, d, wf, rwf, nb):
        rhs = lambda kc: self.hT[:, kc, 0:nb]
        for hd in range(4):
            ps, pr = self.proj_fm(wf, rwf, KC, hd * 128, rhs, [self.r_h], nb, hd % 2)
            self.act(self.sgm[:, 0:nb], ps, AF.Sigmoid, pr, [self.r_sgm])
            self.ts("dve", self.fS[:, 0:nb], self.sgm[:, 0:nb], self.oml[:, d, l, hd:hd + 1], self.lb[:, d, l, hd:hd + 1],
                    ALU.mult, ALU.add, [self.r_sgm, self.r_lb], [self.r_fS])
            self.ts("pool", self.fS[:, 0:nb], self.fS[:, 0:nb], F_MIN, None, ALU.max, None, [self.r_fS], [self.r_fS])
            self.act(self.lfS[:, hd, 0:nb], self.fS[:, 0:nb], AF.Ln, [self.r_fS], [self.r_lfS[hd]])
            self.ts("pool", self.kkS[:, hd, 0:nb], self.fS[:, 0:nb], -1.0, 1.0, ALU.mult, ALU.add, [self.r_fS], [self.r_kkS[hd]])

    def hi_prep(self, wt, rw, nb):
        for ti in range(nb // 128):
            ps, pr = self.proj_tm(wt, rw, 0, 512, self.hT, self.r_h, ti * 128, ti % 2)
            self.cp("act", self.vS[:, ti, :], ps, pr, [self.r_vS[ti]])

    def rstd_lnexp(self, out, in_, rd, wr, n_inv):
        self.act(out, in_, AF.Ln, rd, wr, bias=EPS, scale=n_inv)
        self.act(out, out, AF.Exp, wr, wr, scale=-0.5)

    def hgrn_front(self, l, d, hd, ti, it):
        H = self.hg
        c0 = ti * 128
        lf = self.lfS[:, hd, c0:c0 + 128]; rlf = self.r_lfS[hd]
        q = self.qS[:, hd, c0:c0 + 128]; rq = self.r_qS[hd]
        kk = self.kkS[:, hd, c0:c0 + 128]; rkk = self.r_kkS[hd]
        rc = self.r_const
        for i, nm in enumerate(["P32", "P64", "P128"]):
            ap, r = H[nm]
            self.P.op("dve", lambda e, ap=ap, i=i: e.tensor_tensor_scan(out=ap, data0=self.seg[:, i, :], data1=lf, initial=0.0,
                                                                       op0=ALU.mult, op1=ALU.add), [rlf, rc], [r])
        if d == 1:
            Us = []
            for nm, pn in [("U32", "P32"), ("U64", "P64"), ("U128", "P128")]:
                ap, r = H[nm]
                self.tt("pool", ap, H[pn][0], lf, ALU.subtract, [H[pn][1], rlf], [r])
                Us.append((ap, r))
        else:
            Us = [H["P32"], H["P64"], H["P128"]]
        (U32, rU32), (U64, rU64), (U128, rU128) = Us
        P32, rP32 = H["P32"]; P64, rP64 = H["P64"]; P128, rP128 = H["P128"]
        mpos = 15 if d == 0 else 16
        U32v = U32.rearrange("p (a b) -> p a b", a=4)
        EA, rEA = H["EA"]
        self.tt("dve", EA.rearrange("p (a b) -> p a b", a=4), U32v, U32v[:, :, mpos:mpos + 1].to_broadcast([128, 4, 32]),
                ALU.subtract, [rU32], [rEA])
        nW32, rnW32 = H["nW32"]
        nW64, rnW64 = H["nW64"]
        P32v = P32.rearrange("p (a b) -> p a b", a=4)
        P64v = P64.rearrange("p (a b) -> p a b", a=2)
        self.tt("dve", nW32.rearrange("p (a b) -> p a b", a=4), U32v, P32v[:, :, 31:32].to_broadcast([128, 4, 32]),
                ALU.subtract, [rU32, rP32], [rnW32])
        self.tt("pool", nW64.rearrange("p (a b) -> p a b", a=2), U64.rearrange("p (a b) -> p a b", a=2),
                P64v[:, :, 63:64].to_broadcast([128, 2, 64]), ALU.subtract, [rU64, rP64], [rnW64])
        tot = P128[:, 127:128]
        ex = [("XA", EA, rEA, 1.0, None), ("YA", EA, rEA, -1.0, None),
              ("XB", U32, rU32, 1.0, None), ("YB", nW32, rnW32, -1.0, None),
              ("XC", U64, rU64, 1.0, None), ("YC", nW64, rnW64, -1.0, None),
              ("XD", U128, rU128, 1.0, None), ("YD", U128, rU128, -1.0, tot)]
        for nm, src, rs, sc, b_ in ex:
            ap, r = H[nm]
            if b_ is None:
                self.act(ap, src, AF.Exp, [rs], [r], scale=sc)
            else:
                self.act(ap, src, AF.Exp, [rs, rP128], [r], scale=sc, bias=b_)
        self.act(self.dec[:, it % 8:it % 8 + 1], tot, AF.Exp, [rP128], [self.r_dec[it % 8]])
        O = self.hop[it % 4]
        engs = ["dve", "pool"]
        k = 0
        for lv in "ABCD":
            eq, ek = ("X" + lv, "Y" + lv) if d == 0 else ("Y" + lv, "X" + lv)
            self.tt(engs[k % 2], O["q" + lv][0], q, H[eq][0], ALU.mult, [rq, H[eq][1]], [O["q" + lv][1]]); k += 1
            self.tt(engs[k % 2], O["k" + lv][0], kk, H[ek][0], ALU.mult, [rkk, H[ek][1]], [O["k" + lv][1]]); k += 1

    def hgrn_back1(self, d, hd, ti, it):
        H = self.hg
        O = self.hop[it % 4]
        rc = self.r_const
        slots = [(0, 256), (0, 384), (1, 256)]
        sps = []
        for i, lv in enumerate("ABC"):
            ps, pr = self.PS(slots[i][0], slots[i][1], slots[i][1] + 128)
            self.mm(ps, O["k" + lv][0], O["q" + lv][0], True, True, [O["k" + lv][1], O["q" + lv][1]], pr)
            sps.append((ps, pr))
        slot7 = it % 2
        pt = self.pb7[:, slot7 * 128:(slot7 + 1) * 128]
        ptr = [self.p7res[slot7]]
        self.P.op("pe", lambda e: e.transpose(pt, O["kD"][0], self.ident), [O["kD"][1], rc], ptr)
        Sf, rSf = H["Sf"]; tB, rtB = H["tB"]; tC, rtC = H["tC"]
        self.tt("dve", Sf, sps[0][0], self.masks[:, d * 3 + 0, :], ALU.mult, sps[0][1] + [rc], [rSf])
        self.tt("dve", tB, sps[1][0], self.masks[:, d * 3 + 1, :], ALU.mult, sps[1][1] + [rc], [rtB])
        self.tt("dve", tC, sps[2][0], self.masks[:, d * 3 + 2, :], ALU.mult, sps[2][1] + [rc], [rtC])
        self.tt("pool", Sf, Sf, tB, ALU.add, [rSf, rtB], [rSf])
        Sb, rSb = O["Sb"]
        self.tt("pool", Sb, Sf, tC, ALU.add, [rSf, rtC], [rSb])
        kDt, rkDt = O["kDt"]
        self.cp("act", kDt, pt, ptr, [rkDt])

    def hgrn_back2(self, d, hd, ti, it):
        O = self.hop[it % 4]
        v = self.vS[:, ti, hd * 128:(hd + 1) * 128]; rv = self.r_vS[ti]
        rst = self.r_state[hd]
        po, por = self.PS(2, 256, 384)
        self.mm(po, v, O["Sb"][0], True, False, [rv, O["Sb"][1]], por)
        self.mm(po, self.state_b[:, hd, :], O["qD"][0], False, True, [rst, O["qD"][1]], por)
        pst, pstr = self.PS(2, 384, 512)
        self.mm(pst, O["kDt"][0], v, True, True, [O["kDt"][1], rv], pstr)
        self.stt(self.state_f[:, hd, :], self.state_f[:, hd, :], self.dec[:, it % 8:it % 8 + 1], pst, ALU.mult, ALU.add,
                 pstr + [rst, self.r_dec[it % 8]], [rst])
        self.cp("act", self.state_b[:, hd, :], self.state_f[:, hd, :], [rst], [rst])
        return po, por

    def pass1_block(self, l, blk):
        u0, nb, isctx = blk
        self.block_norm1(l, blk)
        if not isctx:
            self.load_rope(u0, nb)
        rhs = lambda kc: self.hT[:, kc, 0:nb]
        reqs = [self.win_req(l, g) for g in (2, 3, 5, 6)]
        tiles = []
        for (buf, r) in self.wstream(reqs):
            tiles.append((buf[:, 0:8192].rearrange("p (k j) -> p k j", k=KC), r))
            gi = len(tiles) - 1
            wt, rw = tiles[gi]
            if gi == 0:
                for kv in range(2):
                    ps, pr = self.proj_fm(wt, rw, KC, kv * 128, rhs, [self.r_h], nb, kv % 2)
                    self.qk_head(l, ps, pr, nb, self.qkg[:, l, 1:2], not isctx, self.KT[:, kv, u0:u0 + nb], [self.r_KT])
                for ti in range(nb // 128):
                    ps, pr = self.proj_tm(wt, rw, 256, 256, self.hT, self.r_h, ti * 128, ti % 2)
                    self.cp("act", self.V[:, u0 // 128 + ti, :], ps, pr, [self.r_V])
            elif gi == 1:
                self.hgrn_q(wt, rw, nb)
            elif gi == 2:
                self.hgrn_f(l, 1, wt, rw, nb)
            elif gi == 3:
                self.hi_prep(wt, rw, nb)
        its = [(ti, hd) for ti in reversed(range(nb // 128)) for hd in range(4)]
        n_it = len(its)
        for s_ in range(n_it + 2):
            if 0 <= s_ - 2 < n_it:
                ti, hd = its[s_ - 2]
                po, por = self.hgrn_back2(1, hd, ti, self.itc + s_ - 2)
                self.cp("act", self.obst[:, hd, :], po, por, [self.r_obst])
                if hd == 3:
                    gt = u0 // 128 + ti
                    self.dma("sp", self.obT[:, :, gt * 128:(gt + 1) * 128].rearrange("h p n -> p h n"), self.obst, self.st_st,
                             [self.r_obst], [self.r_ob[gt]])
            if 0 <= s_ - 1 < n_it:
                ti, hd = its[s_ - 1]
                self.hgrn_back1(1, hd, ti, self.itc + s_ - 1)
            if s_ < n_it:
                ti, hd = its[s_]
                self.hgrn_front(l, 1, hd, ti, self.itc + s_)
        self.itc += n_it

    def pass2_block(self, l, blk, last):
        u0, nb, isctx = blk
        skip_out = last and isctx
        ntile = nb // 128
        self.block_norm1(l, blk)
        if not isctx:
            self.load_rope(u0, nb)
        rhs = lambda kc: self.hT[:, kc, 0:nb]
        gs = [3, 4, 6] if skip_out else [3, 4, 6, 7, 0, 1, 8, 9]
        reqs = [self.win_req(l, g) for g in gs]
        if not skip_out:
            reqs += [("sp", self.wout_s[l][g], [self.r_wsc[l]["out"]], 8192) for g in range(4)]
        ws = self.wstream(reqs)

        def nxt():
            buf, r = next(ws)
            return buf[:, 0:8192].rearrange("p (k j) -> p k j", k=KC), r

        wq, rwq = nxt()
        self.hgrn_q(wq, rwq, nb)
        wf, rwf = nxt()
        self.hgrn_f(l, 0, wf, rwf, nb)
        wt, rw = nxt()
        self.hi_prep(wt, rw, nb)
        oblds = [(self.obld, self.r_obld), (self.obst, self.r_obst)]
        if not skip_out:
            wt, rw = nxt()
            for hd in range(4):
                ps, pr = self.proj_fm(wt, rw, KC, hd * 128, rhs, [self.r_h], nb, hd % 2)
                self.act(self.gateS[:, hd, 0:nb], ps, AF.Silu, pr, [self.r_gateS[hd]])
            for ti in range(ntile):
                gt = u0 // 128 + ti
                self.dma("sp", oblds[ti][0], self.obT[:, :, gt * 128:(gt + 1) * 128].rearrange("h p n -> p h n"), self.st_ld,
                         [self.r_ob[gt]], [oblds[ti][1]])
            for g in (0, 1):
                wt, rw = nxt()
                for hh in range(4):
                    h = g * 4 + hh
                    ps, pr = self.proj_fm(wt, rw, KC, hh * 128, rhs, [self.r_h], nb, hh % 2)
                    self.qk_head(l, ps, pr, nb, self.qkg[:, l, 0:1], not isctx, self.QT[:, h, 0:nb], [self.r_QT[h]])
        H = self.hg
        its = [(ti, hd) for ti in range(ntile) for hd in range(4)]
        n_it = len(its)
        for s_ in range(n_it + 3):
            if not skip_out and 0 <= s_ - 3 < n_it:
                ti, hd = its[s_ - 3]
                c0 = ti * 128
                oS, roS = H["oS"]; sqo, rsqo = H["sqo"]; rso, rrso = H["rso"]
                pn, pnr = self.PS(1, 384, 512)
                self.mm(pn, self.ones_f, sqo, True, True, [rsqo, self.r_const], pnr)
                self.rstd_lnexp(rso, pn, pnr, [rrso], 1.0 / 128)
                self.stt(oS, oS, self.hgg[:, l:l + 1], rso, ALU.mult, ALU.mult, [roS, rrso, self.r_par], [roS])
                self.tt("pool", self.mixT[:, 8 + hd, c0:c0 + 128], oS, self.gateS[:, hd, c0:c0 + 128], ALU.mult,
                        [roS, self.r_gateS[hd]], [self.r_mix[8 + hd]])
            if not skip_out and s_ < 8:
                self.attention_head(l, blk, s_)
            if 0 <= s_ - 2 < n_it:
                ti, hd = its[s_ - 2]
                po, por = self.hgrn_back2(0, hd, ti, self.itc + s_ - 2)
                if not skip_out:
                    oS, roS = H["oS"]; sqo, rsqo = H["sqo"]
                    self.tt("dve", oS, po, oblds[ti][0][:, hd, :], ALU.add, por + [oblds[ti][1]], [roS])
                    self.tt("pool", sqo, oS, oS, ALU.mult, [roS], [rsqo])
            if 0 <= s_ - 1 < n_it:
                ti, hd = its[s_ - 1]
                self.hgrn_back1(0, hd, ti, self.itc + s_ - 1)
            if s_ < n_it:
                ti, hd = its[s_]
                self.hgrn_front(l, 0, hd, ti, self.itc + s_)
        self.itc += n_it
        if skip_out:
            return
        wt, rw = nxt()
        for g in range(4):
            ps, pr = self.proj_fm(wt, rw, KC, g * 128, rhs, [self.r_h], nb, g % 2)
            self.act(self.uS[:, g, 0:nb], ps, AF.Gelu_apprx_tanh, pr, [self.r_uS[g]])
        wsv, rwsv = nxt()
        for ti in range(ntile):
            c0 = ti * 128
            ps, pr = self.proj_tm(wsv, rwsv, 0, 512, self.hT, self.r_h, c0, ti % 2)
            self.act(self.gv, ps, AF.Gelu_apprx_tanh, pr, [self.r_gv])
            self.tt("pool", self.sqv, self.gv, self.gv, ALU.mult, [self.r_gv], [self.r_sqv])
            self.P.op("dve", lambda e: e.tensor_reduce(out=self.ssum, in_=self.sqv.rearrange("p (g d) -> p g d", g=4), axis=AX.X,
                                                       op=ALU.add), [self.r_sqv], [self.r_ssum])
            self.rstd_lnexp(self.ssum, self.ssum, [self.r_ssum], [self.r_ssum], 1.0 / 128)
            self.tt("dve", self.vn.rearrange("p (g d) -> p g d", g=4), self.gv.rearrange("p (g d) -> p g d", g=4),
                    self.ssum.unsqueeze(2).to_broadcast([128, 4, 128]), ALU.mult, [self.r_gv, self.r_ssum], [self.r_vn])
            self.tt("pool", self.vnb, self.vn, self.sggB, ALU.mult, [self.r_vn, self.r_sg], [self.r_vnb])
            for g in range(4):
                pg, pgr = self.PS(3 + (g % 2), 256, 384)
                self.mm(pg, self.vnb[:, g * 128:(g + 1) * 128], self.sgwT[:, g, :], True, True, [self.r_vnb, self.r_sg], pgr)
                self.tt("dve", self.tsg, pg, self.sgbB[:, g * 128:(g + 1) * 128], ALU.add, pgr + [self.r_sg], [self.r_tsg])
                self.tt("pool", self.mixT[:, 12 + g, c0:c0 + 128], self.tsg, self.uS[:, g, c0:c0 + 128], ALU.mult,
                        [self.r_tsg, self.r_uS[g]], [self.r_mix[12 + g]])
        if self.dbg and l == 0:
            self.dma("pool", self.dbgm[:, :, u0:u0 + nb].rearrange("k p n -> p k n"), self.mixT[:, :, 0:nb], self.st_st,
                     self.r_mix, [self.r_dbg], join=True)
        r_ = 1 if isctx else 0
        mrhs = lambda kc: self.mixT[:, kc, 0:nb]
        for g in range(4):
            wt, rw = nxt()
            for mm_ in range(4):
                m = g * 4 + mm_
                ps, pr = self.proj_fm(wt, rw, KC, mm_ * 128, mrhs, self.r_mix, nb, mm_ % 2)
                self.stt(self.xsb[:, m, 0:nb], ps, self.modT[:, r_, 32 + m:33 + m], self.xsb[:, m, 0:nb], ALU.mult, ALU.add,
                         pr + [self.r_x, self.r_mod], [self.r_x])
        rs = self.r_xM[u0 // 128:(u0 + nb) // 128]
        self.dma("sp", self.xMs[:, :, u0:u0 + nb].rearrange("k p n -> p k n"), self.xsb[:, :, 0:nb], self.st_st, [self.r_x], rs)

    def attention_head(self, l, blk, h):
        u0, nb, isctx = blk
        keyt = [0, 1] if isctx else list(range(U // 128))
        scale = 128 ** -0.5
        kv = h // 4
        po, por = self.PS(5, 0, nb)
        pd, pdr = self.PS(6, 0, nb)
        for ji, j in enumerate(keyt):
            cnt = self.acnt
            self.acnt += 1
            ps, pr = self.PS(3 + cnt % 2, 0, nb)
            self.mm(ps, self.KT[:, kv, j * 128:(j + 1) * 128], self.QT[:, h, 0:nb], True, True, [self.r_KT, self.r_QT[h]], pr)
            pT = self.pT[cnt % 3][:, 0:nb]
            rpT = self.r_pT[cnt % 3]
            self.act(pT, ps, AF.Exp, pr, [rpT], scale=scale)
            self.mm(po, self.V[:, j, kv * 128:(kv + 1) * 128], pT, ji == 0, ji == len(keyt) - 1, [self.r_V, rpT], por)
            self.mm(pd, self.ones_b, pT, ji == 0, ji == len(keyt) - 1, [self.r_const, rpT], pdr)
        self.P.op("dve", lambda e, pd=pd: e.reciprocal(out=self.rD[:, 0:nb], in_=pd), pdr, [self.r_rD])
        self.tt("dve", self.mixT[:, h, 0:nb], po, self.rD[:, 0:nb], ALU.mult, por + [self.r_rD], [self.r_mix[h]])

    def ffn_window(self, l, win, last):
        o0, o1, s0, s1 = win
        isctx = (s0 == 0)
        r_ = 1 if isctx else 0
        c0 = max(o0 - 1, s0)
        c1 = min(o1 + 1, s1)
        n = c1 - c0
        a = o0 - c0
        b = o1 - c0
        no = o1 - o0
        rxs = self.r_xM[c0 // 128:(c1 + 127) // 128]
        self.dma("sp", self.xsb[:, :, 0:n], self.xMs[:, :, c0:c1].rearrange("k p n -> p k n"), self.st_ld, rxs, [self.r_x])
        self.norm(self.xsb, self.r_x, n, self.G2[:, r_, :], self.modT[:, r_, 48:64], [self.r_mod],
                  lambda kc: self.hT[:, kc, 0:n], [self.r_h], self.ntmp)
        rhs = lambda kc: self.hT[:, kc, 0:n]
        reqs = [("sp", self.wup_s[l][t], [self.r_wsc[l]["up"]], 8192) for t in range(22)]
        reqs += [("sp", self.wdn_s[l][m], [self.r_wsc[l]["dn"]], FC * 128) for m in range(16)]
        ws = self.wstream(reqs)
        cw = self.convw
        cb = self.convb
        rp = [self.r_par]
        for t in range(22):
            buf, rw = next(ws)
            wt = buf[:, 0:8192].rearrange("p (k j) -> p k j", k=KC)
            for i in range(2):
                j = 2 * t + i
                pg, pgr = self.proj_fm(wt, rw, KC, i * 128, rhs, [self.r_h], n, 0)
                pv, pvr = self.proj_fm(wt, rw, KC, 256 + i * 128, rhs, [self.r_h], n, 1)
                bi = j % 2
                for (ps, pr, acc, racc, fc) in [(pg, pgr, self.accg[bi], self.r_accg[bi], j), (pv, pvr, self.accv[bi], self.r_accv[bi], FC + j)]:
                    self.act(acc[:, a:b], ps[:, a:b], AF.Identity, pr + rp, [racc], bias=cb[:, l, fc:fc + 1], scale=cw[:, l, 1, fc:fc + 1])
                    lo = max(a, 1)
                    self.stt(acc[:, lo:b], ps[:, lo - 1:b - 1], cw[:, l, 0, fc:fc + 1], acc[:, lo:b], ALU.mult, ALU.add, pr + rp + [racc], [racc])
                    hi = min(b, n - 1)
                    self.stt(acc[:, a:hi], ps[:, a + 1:hi + 1], cw[:, l, 2, fc:fc + 1], acc[:, a:hi], ALU.mult, ALU.add, pr + rp + [racc], [racc])
                self.act(self.sil[bi][:, a:b], self.accg[bi][:, a:b], AF.Silu, [self.r_accg[bi]], [self.r_sil[bi]])
                self.tt("pool", self.aT[:, j, 0:no], self.sil[bi][:, a:b], self.accv[bi][:, a:b], ALU.mult,
                        [self.r_sil[bi], self.r_accv[bi]], [self.r_aT[j]])
        for m in range(16):
            buf, rw = next(ws)
            wt = buf[:, 0:FC * 128].rearrange("p (k j) -> p k j", k=FC)
            ps, pr = self.PS(m % 2, 0, no)
            for j in range(FC):
                self.mm(ps, wt[:, j, :], self.aT[:, j, 0:no], j == 0, j == FC - 1, [rw, self.r_aT[j]], pr)
            self.stt(self.xsb[:, m, a:b], ps, self.modT[:, r_, 80 + m:81 + m], self.xsb[:, m, a:b], ALU.mult, ALU.add,
                     pr + [self.r_x, self.r_mod], [self.r_x])
        ros = self.r_xT[o0 // 128:(o1 + 127) // 128]
        if not last:
            self.dma("sp", self.xTs[:, :, o0:o1].rearrange("k p n -> p k n"), self.xsb[:, :, a:b], self.st_st, [self.r_x], ros)
        else:
            sq, rsq, tm, rtm, rstd, rrstd = self.ntmp
            pn, pnr = self.PS(2, 0, no)
            for kc in range(KC):
                i = kc % 2
                self.act(sq[i][:, 0:no], self.xsb[:, kc, a:b], AF.Square, [self.r_x], [rsq[i]])
                self.mm(pn, self.ones_f, sq[i][:, 0:no], kc == 0, kc == KC - 1, [rsq[i], self.r_const], pnr)
            self.act(rstd[:, 0:no], pn, AF.Sqrt, pnr, [rrstd], bias=EPS, scale=1.0 / D)
            self.P.op("dve", lambda e: e.reciprocal(out=rstd[:, 0:no], in_=rstd[:, 0:no]), [rrstd], [rrstd])
            for kc in range(KC):
                self.stt(self.xsb[:, kc, a:b], self.xsb[:, kc, a:b], self.fng[:, kc:kc + 1], rstd[:, 0:no], ALU.mult, ALU.mult,
                         [self.r_x, rrstd, self.r_par], [self.r_x])
            self.dma("sp", self.outT[:, :, o0 - LC:o1 - LC].rearrange("k p n -> p k n"), self.xsb[:, :, a:b], self.st_st,
                     [self.r_x], [self.r_out], join=True)


def _consts():
    n_freq = 32
    inv = (np.float32(10000.0) ** (-np.arange(n_freq, dtype=np.float32) / np.float32(n_freq))).astype(np.float32)
    t = np.arange(T)
    row = (t // 64).astype(np.float32)
    col = (t % 64).astype(np.float32)
    ang = np.concatenate([row[:, None] * inv[None, :], col[:, None] * inv[None, :]], axis=-1).astype(np.float32)
    cos = np.cos(ang).astype(np.float32)
    sin = np.sin(ang).astype(np.float32)
    ropeT = np.stack([np.repeat(cos, 2, axis=1).T, np.repeat(sin, 2, axis=1).T]).astype(np.float32)
    rotT = np.zeros((128, 128), np.float32)
    for i in range(64):
        rotT[2 * i + 1, 2 * i] = -1.0
        rotT[2 * i, 2 * i + 1] = 1.0
    ident = np.eye(128).astype(ml_dtypes.bfloat16)
    s = np.arange(128)[:, None]
    tt = np.arange(128)[None, :]
    mA = ((s // 32 == tt // 32) & (s <= tt))
    mB = ((s // 64 == tt // 64) & (s % 64 < 32) & (tt % 64 >= 32))
    mC = ((s < 64) & (tt >= 64))
    masks = np.stack([mA, mB, mC, mA.T, mB.T, mC.T], axis=1).astype(np.float32)
    seg = np.ones((128, 3, 128), np.float32)
    seg[:, 0, ::32] = 0
    seg[:, 1, ::64] = 0
    seg[:, 2, 0] = 0
    return dict(ropeT=np.ascontiguousarray(ropeT), rotT=rotT, ident=ident, identf=np.eye(128, dtype=np.float32), masks=np.ascontiguousarray(masks), seg=seg)


def _layout(inputs, b, NL=4):
    f = lambda a: np.ascontiguousarray(a, dtype=np.float32)
    x = inputs["x"][b]
    ctx = inputs["ctx"][b]
    xc = np.concatenate([ctx, x], axis=0)
    m = {}
    m["xT0"] = f(xc.T.reshape(KC, 128, U))
    cc = np.stack([inputs["c"][b], inputs["c_ctx"]], axis=-1)
    m["cT"] = f(cc.reshape(KC, 128, 2).transpose(1, 0, 2))
    for k in ("w_ada", "w_in", "w_out", "w_up", "w_down"):
        m[k] = f(inputs[k][:NL])
    m["badaT"] = f(inputs["b_ada"].reshape(4, 96, 128).transpose(2, 0, 1))
    m["n1g"] = f(inputs["norm1_g"].reshape(4, KC, 128).transpose(2, 0, 1))
    m["n2g"] = f(inputs["norm2_g"].reshape(4, KC, 128).transpose(2, 0, 1))
    m["fng"] = f(inputs["final_norm_g"].reshape(KC, 128).T)
    m["qkg"] = f(np.stack([inputs["q_norm_g"], inputs["k_norm_g"]], axis=-1).transpose(1, 0, 2))
    m["hgg"] = f(inputs["hg_norm_g"].T)
    m["lbp"] = f(inputs["hg_lower_bounds"].reshape(2, 4, 4, 128).transpose(3, 0, 1, 2))
    m["sgg"] = f(inputs["sg_norm_g"].reshape(4, 1, 512))
    m["sgb"] = f(inputs["sg_b"].reshape(4, 1, 512))
    m["sgwT"] = np.ascontiguousarray(inputs["sg_w"].transpose(0, 3, 1, 2), dtype=np.float32)
    m["convw"] = f(inputs["conv_w"].reshape(4, 3, 88, 128).transpose(3, 0, 1, 2))
    m["convb"] = f(inputs["conv_b"].reshape(4, 88, 128).transpose(2, 0, 1))
    return m


_NC_CACHE = {}


def kernel(**inputs):
    inputs = {k: np.asarray(v) for k, v in inputs.items()}
    if "nc" not in _NC_CACHE:
        _NC_CACHE["nc"] = Builder(4, False).build()
    nc = _NC_CACHE["nc"]
    consts = _consts()
    in_maps = []
    for b in range(N_CORES):
        m = _layout(inputs, b)
        m.update(consts)
        in_maps.append(m)
    res = run_bass_kernel_spmd(nc, in_maps, core_ids=list(range(N_CORES)))
    out = np.empty((4, T, D), np.float32)
    for b in range(4):
        out[b] = res.results[b]["outT"].reshape(D, T).T
    return out
```

```python
import contextlib
import numpy as np
import ml_dtypes
import concourse.bass as bass
import concourse.mybir as mybir
from concourse.bass_utils import run_bass_kernel_spmd

F32 = mybir.dt.float32
BF16 = mybir.dt.bfloat16
AF = mybir.ActivationFunctionType
ALU = mybir.AluOpType
AX = mybir.AxisListType

import os
SEM_LIMIT = int(os.environ.get("SEM_LIMIT", "16000"))
D = 2048
KC = 16
T = 4096
LC = 256
U = T + LC
DFF = 5632
FC = 44
EPS = 1e-6
F_MIN = 1e-30
NB = 256
FW = 456
N_CORES = 4


class Res:
    __slots__ = ("name", "writers", "readers", "excl")

    def __init__(self, name, excl=False):
        self.name = name
        self.writers = {}
        self.readers = {}
        self.excl = excl


class Stream:
    def __init__(self, name, nslots):
        self.name = name
        self.slots = [[None, 0] for _ in range(nslots)]
        self.i = 0


def _merge(d, s, v):
    if d.get(s, 0) < v:
        d[s] = v


class Prog:
    ENG = ("pe", "act", "dve", "pool", "sp")

    def __init__(self, nc, stack):
        self.nc = nc
        self.stack = stack
        self.ops = {e: [] for e in self.ENG}
        self.esem = {e: None for e in self.ENG}
        self.ecnt = {e: 0 for e in self.ENG}
        self.known = {e: {} for e in self.ENG}
        self.nsem = 0
        self.nres = 0
        self.esems = {e: set() for e in self.ENG}
        self.streams = []
        self.nops = 0

    def new_sem(self, name):
        self.nsem += 1
        return self.stack.enter_context(self.nc.semaphore(f"{name}_{self.nsem}"))

    def res(self, name=None, excl=False):
        self.nres += 1
        return Res(name or f"r{self.nres}", excl)

    def stream(self, name, nslots):
        st = Stream(name, nslots)
        self.streams.append(st)
        return st

    def sb(self, name, shape, dtype):
        return self.stack.enter_context(self.nc.sbuf_tensor(name, list(shape), dtype))

    def ps(self, name, shape, dtype=F32):
        return self.stack.enter_context(self.nc.psum_tensor(name, list(shape), dtype))

    def _waits(self, eng, deps):
        waits = {}
        kn = self.known[eng]
        own = self.esems[eng]
        for (s, v) in deps:
            if kn.get(s, 0) >= v:
                continue
            if eng == "pe" and s in own:
                continue
            _merge(waits, s, v)
        for s, v in waits.items():
            kn[s] = v
        return list(waits.items())

    def _deps(self, reads, writes, join):
        deps = []
        for r in reads:
            deps.extend(r.writers.items())
        for r in writes:
            if join and not r.readers:
                continue
            deps.extend(r.writers.items())
            deps.extend(r.readers.items())
        return deps

    def _post(self, ev, reads, writes, join):
        s, v = ev
        for r in reads:
            _merge(r.readers, s, v)
        for r in writes:
            if join and not r.readers:
                _merge(r.writers, s, v)
            else:
                r.writers = {s: v}
                r.readers = {}

    def op(self, eng, fn, reads=(), writes=()):
        ex = [r for r in reads if r.excl]
        if ex:
            reads = [r for r in reads if not r.excl]
            writes = list(writes) + ex
        waits = self._waits(eng, self._deps(reads, writes, False))
        if self.esem[eng] is None or self.ecnt[eng] >= SEM_LIMIT:
            self.esem[eng] = self.new_sem(eng)
            self.esems[eng].add(self.esem[eng])
            self.ecnt[eng] = 0
        self.ecnt[eng] += 1
        ev = (self.esem[eng], self.ecnt[eng])
        self.ops[eng].append((waits, fn, self.esem[eng], 1))
        self._post(ev, reads, writes, False)
        self.nops += 1
        return ev

    def dma(self, eng, fn, stream, reads=(), writes=(), join=False):
        slot = stream.slots[stream.i]
        stream.i = (stream.i + 1) % len(stream.slots)
        deps = self._deps(reads, writes, join)
        if slot[0] is not None:
            deps.append((slot[0], slot[1]))
        waits = self._waits(eng, deps)
        if slot[0] is None or slot[1] + 16 > SEM_LIMIT:
            slot[0] = self.new_sem("d" + stream.name)
            slot[1] = 0
        slot[1] += 16
        ev = (slot[0], slot[1])
        self.ops[eng].append((waits, fn, slot[0], 16))
        self._post(ev, reads, writes, join)
        self.nops += 1
        return ev

    def all_events(self):
        evs = []
        for e in self.ENG:
            if self.esem[e] is not None:
                evs.append((self.esem[e], self.ecnt[e]))
        for st in self.streams:
            for sl in st.slots:
                if sl[0] is not None:
                    evs.append((sl[0], sl[1]))
        return evs

    def barrier(self, engines=None):
        evs = self.all_events()
        for e in (engines or self.ENG):
            w = self._waits(e, evs)
            if w:
                self.ops[e].append((w, None, None, 0))

    def emit(self):
        nc = self.nc
        block = self.stack.enter_context(nc.Block())
        ops = self.ops

        def run(e, name):
            for (waits, fn, sem, inc) in ops[name]:
                for (s, v) in waits:
                    e.wait_ge(s, v)
                if fn is not None:
                    fn(e).then_inc(sem, inc)

        @block.tensor
        def _(e):
            run(e, "pe")

        @block.scalar
        def _(e):
            run(e, "act")

        @block.vector
        def _(e):
            run(e, "dve")

        @block.gpsimd
        def _(e):
            run(e, "pool")

        @block.sync
        def _(e):
            run(e, "sp")


class Arena:
    def __init__(self, tensor, n):
        self.t = tensor
        self.n = n
        self.off = 0

    def reset(self):
        self.off = 0

    def take(self, *shape):
        n = int(np.prod(shape))
        assert self.off + n <= self.n, (self.off, n, self.n)
        ap = self.t[:, self.off:self.off + n]
        self.off += n
        if len(shape) == 2:
            ap = ap.rearrange("p (a b) -> p a b", a=shape[0])
        elif len(shape) == 3:
            ap = ap.rearrange("p (a b c) -> p a b c", a=shape[0], b=shape[1])
        return ap


class Builder:
    def __init__(self, NL=4, dbg=False, stop=None):
        self.NL = NL
        self.dbg = dbg
        self.stop = stop

    def mm(self, out, lhsT, rhs, st, sp, rd, wr):
        self.P.op("pe", lambda e: e.matmul(out, lhsT=lhsT, rhs=rhs, start=st, stop=sp), rd, wr)

    def act(self, out, in_, func, rd, wr, bias=None, scale=None):
        kw = {}
        if bias is not None:
            kw["bias"] = bias
        if scale is not None:
            kw["scale"] = scale
        self.P.op("act", lambda e: e.activation(out=out, in_=in_, func=func, **kw), rd, wr)

    def tt(self, eng, out, in0, in1, op, rd, wr):
        self.P.op(eng, lambda e: e.tensor_tensor(out=out, in0=in0, in1=in1, op=op), rd, wr)

    def ts(self, eng, out, in0, s1, s2, op0, op1, rd, wr):
        if op1 is None:
            self.P.op(eng, lambda e: e.tensor_scalar(out=out, in0=in0, scalar1=s1, scalar2=None, op0=op0), rd, wr)
        else:
            self.P.op(eng, lambda e: e.tensor_scalar(out=out, in0=in0, scalar1=s1, scalar2=s2, op0=op0, op1=op1), rd, wr)

    def stt(self, out, in0, scalar, in1, op0, op1, rd, wr):
        self.P.op("dve", lambda e: e.scalar_tensor_tensor(out=out, in0=in0, scalar=scalar, in1=in1, op0=op0, op1=op1), rd, wr)

    def cp(self, eng, out, in_, rd, wr):
        if eng == "act":
            self.P.op("act", lambda e: e.activation(out=out, in_=in_, func=AF.Copy), rd, wr)
        else:
            self.P.op(eng, lambda e: e.tensor_copy(out=out, in_=in_), rd, wr)

    def dma(self, eng, out, in_, stream, rd, wr, join=False, slow=False):
        if slow:
            self.P.dma(eng, lambda e: e.dma_start(out=out, in_=in_, allow_slow_non_contiguous=True), stream, rd, wr, join)
        else:
            self.P.dma(eng, lambda e: e.dma_start(out=out, in_=in_), stream, rd, wr, join)

    def PS(self, b, c0, c1, rows=128):
        ap = self.pb[b][0:rows, c0:c1]
        return ap, [self.pres[b]]

    def wload(self, eng, src, src_res, nelem):
        i = self.wi
        self.wi = (i + 1) % len(self.wbuf)
        buf = self.wbuf[i]
        r = self.wres[i]
        self.dma(eng, buf[:, 0:nelem], src, self.st_w, src_res, [r])
        return buf, r

    def wstream(self, reqs, depth=1):
        loaded = []
        for i in range(len(reqs)):
            while len(loaded) < min(len(reqs), i + 1 + depth):
                loaded.append(self.wload(*reqs[len(loaded)]))
            yield loaded[i]

    def build(self):
        NL = self.NL
        nc = bass.Bass("TRN2", target_bir_lowering=False)
        self.nc = nc

        def EI(n, s, d=F32):
            return nc.dram_tensor(n, list(s), d, kind="ExternalInput").ap()

        def SC(n, s, d=F32):
            return nc.dram_tensor(n, list(s), d, kind="Internal").ap()

        self.xT0 = EI("xT0", [KC, 128, U])
        self.cT_d = EI("cT", [128, KC, 2])
        self.w_ada = EI("w_ada", [NL, D, 6 * D])
        self.w_in = EI("w_in", [NL, D, 5120])
        self.w_out = EI("w_out", [NL, D, D])
        self.w_up = EI("w_up", [NL, D, 2 * DFF])
        self.w_down = EI("w_down", [NL, DFF, D])
        self.badaT_d = EI("badaT", [128, 4, 96])
        self.n1g_d = EI("n1g", [128, 4, KC])
        self.n2g_d = EI("n2g", [128, 4, KC])
        self.fng_d = EI("fng", [128, KC])
        self.qkg_d = EI("qkg", [128, 4, 2])
        self.hgg_d = EI("hgg", [128, 4])
        self.lbp_d = EI("lbp", [128, 2, 4, 4])
        self.sgg_d = EI("sgg", [4, 1, 512])
        self.sgb_d = EI("sgb", [4, 1, 512])
        self.sgwT_d = EI("sgwT", [4, 128, 4, 128])
        self.convw_d = EI("convw", [128, 4, 3, 88])
        self.convb_d = EI("convb", [128, 4, 88])
        self.rope_d = EI("ropeT", [2, 128, T])
        self.rotT_d = EI("rotT", [128, 128])
        self.ident_d = EI("ident", [128, 128], BF16)
        self.identf_d = EI("identf", [128, 128])
        self.masks_d = EI("masks", [128, 6, 128])
        self.seg_d = EI("seg", [128, 3, 128])
        self.outT = nc.dram_tensor("outT", [KC, 128, T], F32, kind="ExternalOutput").ap()
        if self.dbg:
            self.dbg1 = nc.dram_tensor("dbg1", [KC, 128, U], F32, kind="ExternalOutput").ap()
            self.dbg2 = nc.dram_tensor("dbg2", [KC, 128, U], F32, kind="ExternalOutput").ap()
            self.dbgm = nc.dram_tensor("dbgm", [KC, 128, U], F32, kind="ExternalOutput").ap()
            self.dbgs = nc.dram_tensor("dbgs", [128, 6600], F32, kind="ExternalOutput").ap()

        self.xTs = SC("xTs", [KC, 128, U])
        self.xMs = SC("xMs", [KC, 128, U])
        self.obT = SC("obT", [4, 128, U])
        self.modd = SC("modd", [2, 6 * D])
        self.win_s = [SC(f"win_s{l}", [10, 128, KC * 512], BF16) for l in range(NL)]
        self.wout_s = [SC(f"wout_s{l}", [4, 128, KC * 512], BF16) for l in range(NL)]
        self.wup_s = [SC(f"wup_s{l}", [22, 128, KC * 512], BF16) for l in range(NL)]
        self.wdn_s = [SC(f"wdn_s{l}", [16, 128, FC * 128], BF16) for l in range(NL)]

        with contextlib.ExitStack() as stack:
            P = Prog(nc, stack)
            self.P = P
            self.alloc()
            self.setup()
            for l in range(NL):
                self.layer(l)
            P.barrier()
            P.emit()
        return nc

    def alloc(self):
        P = self.P
        self.st_w = P.stream("w", 2)
        self.st_ld = P.stream("ld", 4)
        self.st_st = P.stream("st", 4)
        self.st_cast = P.stream("cast", 4)
        self.st_x0 = P.stream("x0", 4)
        self.pb = [P.ps(f"pb{i}", [128, 512]) for i in range(7)]
        self.pres = [P.res(f"pb{i}", excl=True) for i in range(7)]
        self.pb7 = P.ps("pb7", [128, 1024], BF16)
        self.p7res = [P.res("pb7", excl=True)] * 8
        self.wbuf = [P.sb(f"wbuf{i}", [128, 8192], BF16) for i in range(2)]
        self.wres = [P.res(f"wbuf{i}") for i in range(2)]
        self.wi = 0
        self.A32 = Arena(P.sb("A32", [128, 16384], F32), 16384)
        self.A16 = Arena(P.sb("A16", [128, 37500], BF16), 37500)
        self.C32t = P.sb("C32", [128, 6600], F32)
        C32 = Arena(self.C32t, 6600)
        C16 = Arena(P.sb("C16", [128, 1400], BF16), 1400)
        self.masks = C32.take(6, 128)
        self.seg = C32.take(3, 128)
        self.ones_f = C32.take(128)
        self.rotT = C32.take(128)
        self.ident_f = C32.take(128)
        self.sggB = C32.take(512)
        self.sgbB = C32.take(512)
        self.n1g = C32.take(4, KC)
        self.n2g = C32.take(4, KC)
        self.fng = C32.take(KC)
        self.qkg = C32.take(4, 2)
        self.hgg = C32.take(4)
        self.lbe = C32.take(2, 4, 4)
        self.lb = C32.take(2, 4, 4)
        self.oml = C32.take(2, 4, 4)
        self.lbs = C32.take(2, 4)
        self.convw = C32.take(4, 3, 88)
        self.convb = C32.take(4, 88)
        self.badaT = C32.take(4, 96)
        self.cT = C32.take(KC, 2)
        self.modT = C32.take(2, 96)
        self.G1 = C32.take(2, KC)
        self.G2 = C32.take(2, KC)
        self.modrow = [C32.take(512), C32.take(512)]
        self.state_f = C32.take(4, 128)
        self.dec = C32.take(8)
        self.ones_b = C16.take(128)
        self.ident = C16.take(128)
        self.scT = C16.take(KC, 2)
        self.sgwT = C16.take(4, 128)
        self.state_b = C16.take(4, 128)
        R = P.res
        self.r_const = R("const")
        self.r_par = R("par")
        self.r_sg = R("sgpar")
        self.r_lb = R("lb")
        self.r_mod = R("mod")
        self.r_modrow = [R("modrow0"), R("modrow1")]
        self.r_modd = R("modd")
        self.r_state = [R(f"state{h}") for h in range(4)]
        self.r_dec = [R(f"dec{h}") for h in range(8)]
        self.itc = 0
        self.acnt = 0
        self.r_xT = [R(f"xT{j}") for j in range(U // 128)]
        self.r_xM = [R(f"xM{j}") for j in range(U // 128)]
        self.r_ob = [R(f"ob{j}") for j in range(U // 128)]
        self.r_wsc = [{m: R(f"wsc{l}{m}") for m in ("in", "out", "up", "dn")} for l in range(self.NL)]
        self.r_out = R("out")
        self.r_dbg = R("dbg")

    def setup(self):
        P = self.P
        ld = self.st_ld
        rc = [self.r_const]
        rp = [self.r_par]
        for (dst, src) in [(self.masks, self.masks_d), (self.seg, self.seg_d), (self.rotT, self.rotT_d), (self.ident_f, self.identf_d),
                           (self.ident, self.ident_d)]:
            self.dma("sp", dst, src, ld, [], rc, join=True)
        for (dst, src) in [(self.n1g, self.n1g_d), (self.n2g, self.n2g_d), (self.fng, self.fng_d),
                           (self.qkg, self.qkg_d), (self.hgg, self.hgg_d), (self.lbe, self.lbp_d),
                           (self.convw, self.convw_d), (self.convb, self.convb_d), (self.badaT, self.badaT_d),
                           (self.cT, self.cT_d)]:
            self.dma("sp", dst, src, ld, [], rp, join=True)
        P.op("dve", lambda e: e.memset(self.ones_f, 1.0), [], rc)
        P.op("dve", lambda e: e.memset(self.ones_b, 1.0), [], rc)
        self.act(self.scT, self.cT, AF.Silu, rp, [self.r_lb])
        lbe, lb, oml, lbs = self.lbe, self.lb, self.oml, self.lbs
        rl = [self.r_lb]
        self.act(lbe, lbe, AF.Exp, rp + rl, rp)
        self.tt("dve", lbs, lbe[:, :, 0, :], lbe[:, :, 1, :], ALU.add, rp, rl)
        self.tt("dve", lbs, lbs, lbe[:, :, 2, :], ALU.add, rp + rl, rl)
        self.tt("dve", lbs, lbs, lbe[:, :, 3, :], ALU.add, rp + rl, rl)
        P.op("dve", lambda e: e.reciprocal(out=lbs, in_=lbs), rl, rl)
        P.op("dve", lambda e: e.memset(lb[:, :, 0, :], 0.0), [], rl)
        self.tt("dve", lb[:, :, 1, :], lbe[:, :, 1, :], lbs, ALU.mult, rp + rl, rl)
        for j in (2, 3):
            self.tt("dve", lb[:, :, j, :], lbe[:, :, j, :], lbs, ALU.mult, rp + rl, rl)
            self.tt("dve", lb[:, :, j, :], lb[:, :, j, :], lb[:, :, j - 1, :], ALU.add, rl, rl)
        self.ts("dve", oml, lb, -1.0, 1.0, ALU.mult, ALU.add, rl, rl)
        for kc in range(KC):
            self.dma("sp", self.xTs[kc], self.xT0[kc], self.st_x0, [], self.r_xT, join=True)
        for l in range(self.NL):
            self.cast_weights(l)

    def cast_weights(self, l):
        cs = self.st_cast
        rw = self.r_wsc[l]
        for g in range(10):
            self.dma("pool", self.win_s[l][g].rearrange("p (k j) -> p k j", k=KC),
                     self.w_in[l][:, g * 512:(g + 1) * 512].rearrange("(k p) j -> p k j", p=128),
                     cs, [], [rw["in"]], join=True)
        for g in range(4):
            self.dma("pool", self.wout_s[l][g].rearrange("p (k j) -> p k j", k=KC),
                     self.w_out[l][:, g * 512:(g + 1) * 512].rearrange("(k p) j -> p k j", p=128),
                     cs, [], [rw["out"]], join=True)
        for t in range(22):
            dst = self.wup_s[l][t].rearrange("p (k j) -> p k j", k=KC)
            self.dma("pool", dst[:, :, 0:256],
                     self.w_up[l][:, t * 256:(t + 1) * 256].rearrange("(k p) j -> p k j", p=128),
                     cs, [], [rw["up"]], join=True)
            self.dma("pool", dst[:, :, 256:512],
                     self.w_up[l][:, DFF + t * 256:DFF + (t + 1) * 256].rearrange("(k p) j -> p k j", p=128),
                     cs, [], [rw["up"]], join=True)
        for m in range(16):
            self.dma("pool", self.wdn_s[l][m].rearrange("p (k j) -> p k j", k=FC),
                     self.w_down[l][:, m * 128:(m + 1) * 128].rearrange("(k p) j -> p k j", p=128),
                     cs, [], [rw["dn"]], join=True)

    def ada(self, l):
        P = self.P
        reqs = []
        for n in range(24):
            reqs.append(("pool", self.w_ada[l][:, n * 512:(n + 1) * 512].rearrange("(k p) j -> p k j", p=128), [], 8192))
        n = 0
        for (buf, r) in self.wstream_ada(reqs):
            wt = buf[:, 0:8192].rearrange("p (k j) -> p k j", k=KC)
            ps, pr = self.PS(n % 2, 0, 512, rows=2)
            for kc in range(KC):
                self.mm(ps, self.scT[:, kc, :], wt[:, kc, :], kc == 0, kc == KC - 1, [r, self.r_lb], pr)
            mr = self.modrow[n % 2]
            rr = self.r_modrow[n % 2]
            self.cp("act", mr[0:2, :], ps, pr, [rr])
            for cc in range(4):
                c = n * 4 + cc
                pm, pmr = self.PS(2, 0, 192)
                self.mm(pm[:, c * 2:c * 2 + 2], mr[0:2, cc * 128:(cc + 1) * 128], self.ident_f[0:2, 0:2], True, True,
                        [rr, self.r_const], pmr)
            n += 1
        rm = [self.r_mod]
        pm, pmr = self.PS(2, 0, 192)
        pm3 = pm.rearrange("p (c r) -> p c r", r=2)
        for r_ in range(2):
            self.cp("dve", self.modT[:, r_, :], pm3[:, :, r_], pmr, rm)
        for r_ in range(2):
            self.tt("dve", self.modT[:, r_, :], self.modT[:, r_, :], self.badaT[:, l, :], ALU.add, rm + [self.r_par], rm)
        for r_ in range(2):
            self.stt(self.G1[:, r_, :], self.modT[:, r_, 16:32], 1.0, self.n1g[:, l, :], ALU.add, ALU.mult, rm + [self.r_par], rm)
            self.stt(self.G2[:, r_, :], self.modT[:, r_, 64:80], 1.0, self.n2g[:, l, :], ALU.add, ALU.mult, rm + [self.r_par], rm)

    def wstream_ada(self, reqs):
        loaded = []
        for i in range(len(reqs)):
            while len(loaded) < min(len(reqs), i + 2):
                eng, src, sres, nelem = reqs[len(loaded)]
                k = self.wi
                self.wi = (k + 1) % len(self.wbuf)
                buf = self.wbuf[k]
                r = self.wres[k]
                self.dma(eng, buf[:, 0:nelem].rearrange("p (k j) -> p k j", k=KC), src, self.st_w, sres, [r])
                loaded.append((buf, r))
            yield loaded[i]

    def norm(self, x, rx, n, G, S, rg, dst, rdst, tmp):
        sq, rsq, tm, rtm, rstd, rrstd = tmp
        pn, pnr = self.PS(2, 0, n)
        for kc in range(KC):
            i = kc % 2
            self.act(sq[i][:, 0:n], x[:, kc, 0:n], AF.Square, [rx], [rsq[i]])
            self.mm(pn, self.ones_f, sq[i][:, 0:n], kc == 0, kc == KC - 1, [rsq[i], self.r_const], pnr)
        self.act(rstd[:, 0:n], pn, AF.Sqrt, pnr, [rrstd], bias=EPS, scale=1.0 / D)
        self.P.op("dve", lambda e: e.reciprocal(out=rstd[:, 0:n], in_=rstd[:, 0:n]), [rrstd], [rrstd])
        for kc in range(KC):
            i = kc % 2
            self.tt("dve", tm[i][:, 0:n], x[:, kc, 0:n], rstd[:, 0:n], ALU.mult, [rx, rrstd], [rtm[i]])
            if S is not None:
                self.act(dst(kc), tm[i][:, 0:n], AF.Identity, [rtm[i]] + rg, rdst, bias=S[:, kc:kc + 1], scale=G[:, kc:kc + 1])
            else:
                self.act(dst(kc), tm[i][:, 0:n], AF.Identity, [rtm[i]] + rg, rdst, scale=G[:, kc:kc + 1])

    def proj_fm(self, wt, rw, kcn, mo, rhs, rrhs, n, bank):
        ps, pr = self.PS(bank, 0, n)
        for kc in range(kcn):
            self.mm(ps, wt[:, kc, mo:mo + 128], rhs(kc), kc == 0, kc == kcn - 1, [rw] + rrhs, pr)
        return ps, pr

    def proj_tm(self, wt, rw, c0, w, hT, rh, t0, bank):
        ps, pr = self.PS(bank, 0, w)
        for kc in range(KC):
            self.mm(ps, hT[:, kc, t0:t0 + 128], wt[:, kc, c0:c0 + w], kc == 0, kc == KC - 1, [rw, rh], pr)
        return ps, pr

    def alloc_p12(self):
        A32, A16, R = self.A32, self.A16, self.P.res
        A32.reset()
        A16.reset()
        n = NB
        self.xsb = A32.take(KC, n); self.r_x = R("xsb")
        self.ntmp = ([A32.take(n), A32.take(n)], [R("sq0"), R("sq1")], [A32.take(n), A32.take(n)], [R("tm0"), R("tm1")],
                     A32.take(n), R("rstd"))
        self.qS = A32.take(4, n); self.lfS = A32.take(4, n); self.kkS = A32.take(4, n)
        self.r_qS = [R(f"qS{h}") for h in range(4)]
        self.r_lfS = [R(f"lfS{h}") for h in range(4)]
        self.r_kkS = [R(f"kkS{h}") for h in range(4)]
        self.t1 = A32.take(n); self.t2 = A32.take(n); self.r_t1 = R("t1"); self.r_t2 = R("t2")
        self.rope = A32.take(2, n); self.r_rope = R("rope")
        self.hg = {}
        for nm in ["P32", "P64", "P128", "U32", "U64", "U128", "EA", "nW32", "nW64", "XA", "YA", "XB", "YB", "XC", "YC",
                   "XD", "YD", "Sf", "tB", "tC", "oS", "sqo", "rso"]:
            self.hg[nm] = (A32.take(128), R("hg_" + nm))
        self.gv = A32.take(512); self.sqv = A32.take(512); self.vn = A32.take(512)
        self.ssum = A32.take(4); self.tsg = A32.take(128)
        self.r_gv = R("gv"); self.r_sqv = R("sqv"); self.r_vn = R("vn"); self.r_ssum = R("ssum"); self.r_tsg = R("tsg")
        self.rD = A32.take(n); self.r_rD = R("rD")
        self.obst = A32.take(4, 128); self.r_obst = R("obst")
        self.obld = A32.take(4, 128); self.r_obld = R("obld")
        self.sgm = A32.take(n); self.fS = A32.take(n); self.r_sgm = R("sgm"); self.r_fS = R("fS")
        self.hT = A16.take(KC, n); self.r_h = R("hT")
        self.KT = A16.take(2, U); self.r_KT = R("KT")
        self.V = A16.take(U // 128, 256); self.r_V = R("V")
        self.QT = A16.take(8, n); self.r_QT = [R(f"QT{h}") for h in range(8)]
        self.mixT = A16.take(KC, n); self.r_mix = [R(f"mix{c}") for c in range(KC)]
        self.pT = [A16.take(n) for _ in range(4)]; self.r_pT = [R(f"pT{i}") for i in range(4)]
        self.vS = A16.take(n // 128, 512); self.r_vS = [R(f"vS{i}") for i in range(n // 128)]
        self.vnb = A16.take(512); self.r_vnb = R("vnb")
        self.gateS = A16.take(4, n); self.uS = A16.take(4, n)
        self.r_gateS = [R(f"gateS{h}") for h in range(4)]
        self.r_uS = [R(f"uS{h}") for h in range(4)]
        self.hop = [{nm: (A16.take(128), R(f"hop{i}_{nm}")) for nm in ["qA", "kA", "qB", "kB", "qC", "kC", "qD", "kD", "Sb", "kDt"]}
                    for i in range(4)]

    def alloc_p3(self):
        A32, A16, R = self.A32, self.A16, self.P.res
        A32.reset()
        A16.reset()
        n = FW + 8
        self.xsb = A32.take(KC, n); self.r_x = R("xsb3")
        self.ntmp = ([A32.take(n), A32.take(n)], [R("sq0"), R("sq1")], [A32.take(n), A32.take(n)], [R("tm0"), R("tm1")],
                     A32.take(n), R("rstd"))
        self.accg = [A32.take(n) for _ in range(2)]; self.accv = [A32.take(n) for _ in range(2)]
        self.sil = [A32.take(n) for _ in range(2)]
        self.r_accg = [R("accg0"), R("accg1")]; self.r_accv = [R("accv0"), R("accv1")]; self.r_sil = [R("sil0"), R("sil1")]
        self.hT = A16.take(KC, n); self.r_h = R("hT3")
        self.aT = A16.take(FC, n); self.r_aT = [R(f"aT{j}") for j in range(FC)]

    def layer(self, l):
        P = self.P
        last = (l == 3)
        if self.stop == "setup":
            self.dump_c32()
            return
        self.ada(l)
        P.barrier()
        if self.stop == "ada":
            self.dump_c32()
            return
        self.alloc_p12()
        self.dma("sp", self.sggB, self.sgg_d[l].partition_broadcast(128), self.st_ld, [], [self.r_sg], join=True)
        self.dma("sp", self.sgbB, self.sgb_d[l].partition_broadcast(128), self.st_ld, [], [self.r_sg], join=True)
        self.dma("pool", self.sgwT, self.sgwT_d[l], self.st_ld, [], [self.r_sg], join=True)
        blocks = [(0, LC, True)] + [(LC + j * NB, NB, False) for j in range(T // NB)]
        self.reset_state()
        p1b = [blocks[0]] + blocks[:0:-1]
        if self.stop and self.stop.startswith("p1b"):
            p1b = p1b[:int(self.stop[3:])]
        for blk in p1b:
            self.pass1_block(l, blk)
        if self.stop and self.stop.startswith("p1"):
            return
        self.reset_state()
        p2b = blocks
        if self.stop and self.stop.startswith("p2b"):
            p2b = p2b[:int(self.stop[3:])]
        for blk in p2b:
            self.pass2_block(l, blk, last)
        if self.dbg and l == 0:
            for kc in range(KC):
                self.dma("sp", self.dbg1[kc], self.xMs[kc], self.st_x0, self.r_xM, [self.r_dbg], join=True)
        P.barrier()
        if self.stop and self.stop.startswith("p2"):
            return
        self.alloc_p3()
        wins = []
        if not last:
            wins.append((0, LC, 0, LC))
        o = 0
        while o < T:
            n = min(FW, T - o)
            wins.append((LC + o, LC + o + n, LC, U))
            o += n
        for w in wins:
            self.ffn_window(l, w, last)
        if self.dbg and l == 0:
            for kc in range(KC):
                self.dma("sp", self.dbg2[kc], self.xTs[kc], self.st_x0, self.r_xT, [self.r_dbg], join=True)
        P.barrier()

    def dump_c32(self):
        self.P.barrier()
        self.dma("sp", self.dbgs, self.C32t[:, :], self.st_st, [], [self.r_dbg])

    def reset_state(self):
        for h in range(4):
            self.P.op("dve", lambda e, h=h: e.memset(self.state_f[:, h, :], 0.0), [], [self.r_state[h]])
            self.P.op("pool", lambda e, h=h: e.memset(self.state_b[:, h, :], 0.0), [], [self.r_state[h]])

    def load_x(self, u0, n):
        rs = self.r_xT[u0 // 128:(u0 + n + 127) // 128]
        self.dma("sp", self.xsb[:, :, 0:n], self.xTs[:, :, u0:u0 + n].rearrange("k p n -> p k n"), self.st_ld, rs, [self.r_x])
        return rs

    def block_norm1(self, l, blk):
        u0, nb, isctx = blk
        r_ = 1 if isctx else 0
        self.load_x(u0, nb)
        self.norm(self.xsb, self.r_x, nb, self.G1[:, r_, :], self.modT[:, r_, 0:16], [self.r_mod],
                  lambda kc: self.hT[:, kc, 0:nb], [self.r_h], self.ntmp)

    def win_req(self, l, g):
        return ("sp", self.win_s[l][g], [self.r_wsc[l]["in"]], 8192)

    def qk_head(self, l, ps, pr, n, gcol, rope, dst, rdst):
        sq, rsq, tm, rtm, rstd, rrstd = self.ntmp
        kraw, r_kraw = tm[0], rtm[0]
        kn, r_kn = tm[1], rtm[1]
        self.cp("act", kraw[:, 0:n], ps, pr, [r_kraw])
        self.tt("pool", sq[0][:, 0:n], kraw[:, 0:n], kraw[:, 0:n], ALU.mult, [r_kraw], [rsq[0]])
        pn, pnr = self.PS(2, 0, n)
        self.mm(pn, self.ones_f, sq[0][:, 0:n], True, True, [rsq[0], self.r_const], pnr)
        self.rstd_lnexp(sq[1][:, 0:n], pn, pnr, [rsq[1]], 1.0 / 128)
        self.stt(kn[:, 0:n], kraw[:, 0:n], gcol, sq[1][:, 0:n], ALU.mult, ALU.mult, [r_kraw, rsq[1], self.r_par], [r_kn])
        if rope:
            pro, prr = self.PS(3, 256, 256 + n)
            self.mm(pro, self.rotT, kn[:, 0:n], True, True, [r_kn, self.r_const], prr)
            self.tt("dve", self.t1[:, 0:n], kn[:, 0:n], self.rope[:, 0, 0:n], ALU.mult, [r_kn, self.r_rope], [self.r_t1])
            self.tt("dve", self.t2[:, 0:n], pro, self.rope[:, 1, 0:n], ALU.mult, prr + [self.r_rope], [self.r_t2])
            self.tt("pool", dst, self.t1[:, 0:n], self.t2[:, 0:n], ALU.add, [self.r_t1, self.r_t2], rdst)
        else:
            self.cp("act", dst, kn[:, 0:n], [r_kn], rdst)

    def load_rope(self, u0, nb):
        t0 = u0 - LC
        self.dma("sp", self.rope[:, :, 0:nb], self.rope_d[:, :, t0:t0 + nb].rearrange("a p n -> p a n"), self.st_ld,
                 [], [self.r_rope])

    def hgrn_q(self, wq, rwq, nb):
        rhs = lambda kc: self.hT[:, kc, 0:nb]
        for hd in range(4):
            ps, pr = self.proj_fm(wq, rwq, KC, hd * 128, rhs, [self.r_h], nb, hd % 2)
            self.act(self.qS[:, hd, 0:nb], ps, AF.Silu, pr, [self.r_qS[hd]])

    def hgrn_f(self, l, d, wf, rwf, nb):
        rhs = lambda kc: self.hT[:, kc, 0:nb]
        for hd in range(4):
            ps, pr = self.proj_fm(wf, rwf, KC, hd * 128, rhs, [self.r_h], nb, hd % 2)
            self.act(self.sgm[:, 0:nb], ps, AF.Sigmoid, pr, [self.r_sgm])
            self.ts("dve", self.fS[:, 0:nb], self.sgm[:, 0:nb], self.oml[:, d, l, hd:hd + 1], self.lb[:, d, l, hd:hd + 1],
                    ALU.mult, ALU.add, [self.r_sgm, self.r_lb], [self.r_fS])
            self.ts("pool", self.fS[:, 0:nb], self.fS[:, 0:nb], F_MIN, None, ALU.max, None, [self.r_fS], [self.r_fS])
            self.act(self.lfS[:, hd, 0:nb], self.fS[:, 0:nb], AF.Ln, [self.r_fS], [self.r_lfS[hd]])
            self.ts("pool", self.kkS[:, hd, 0:nb], self.fS[:, 0:nb], -1.0, 1.0, ALU.mult, ALU.add, [self.r_fS], [self.r_kkS[hd]])

    def hi_prep(self, wt, rw, nb):
        for ti in range(nb // 128):
            ps, pr = self.proj_tm(wt, rw, 0, 512, self.hT, self.r_h, ti * 128, ti % 2)
            self.cp("act", self.vS[:, ti, :], ps, pr, [self.r_vS[ti]])

    def rstd_lnexp(self, out, in_, rd, wr, n_inv):
        self.act(out, in_, AF.Ln, rd, wr, bias=EPS, scale=n_inv)
        self.act(out, out, AF.Exp, wr, wr, scale=-0.5)

    def hgrn_front(self, l, d, hd, ti, it):
        H = self.hg
        c0 = ti * 128
        lf = self.lfS[:, hd, c0:c0 + 128]; rlf = self.r_lfS[hd]
        q = self.qS[:, hd, c0:c0 + 128]; rq = self.r_qS[hd]
        kk = self.kkS[:, hd, c0:c0 + 128]; rkk = self.r_kkS[hd]
        rc = self.r_const
        for i, nm in enumerate(["P32", "P64", "P128"]):
            ap, r = H[nm]
            self.P.op("dve", lambda e, ap=ap, i=i: e.tensor_tensor_scan(out=ap, data0=self.seg[:, i, :], data1=lf, initial=0.0,
                                                                       op0=ALU.mult, op1=ALU.add), [rlf, rc], [r])
        if d == 1:
            Us = []
            for nm, pn in [("U32", "P32"), ("U64", "P64"), ("U128", "P128")]:
                ap, r = H[nm]
                self.tt("pool", ap, H[pn][0], lf, ALU.subtract, [H[pn][1], rlf], [r])
                Us.append((ap, r))
        else:
            Us = [H["P32"], H["P64"], H["P128"]]
        (U32, rU32), (U64, rU64), (U128, rU128) = Us
        P32, rP32 = H["P32"]; P64, rP64 = H["P64"]; P128, rP128 = H["P128"]
        mpos = 15 if d == 0 else 16
        U32v = U32.rearrange("p (a b) -> p a b", a=4)
        EA, rEA = H["EA"]
        self.tt("dve", EA.rearrange("p (a b) -> p a b", a=4), U32v, U32v[:, :, mpos:mpos + 1].to_broadcast([128, 4, 32]),
                ALU.subtract, [rU32], [rEA])
        nW32, rnW32 = H["nW32"]
        nW64, rnW64 = H["nW64"]
        P32v = P32.rearrange("p (a b) -> p a b", a=4)
        P64v = P64.rearrange("p (a b) -> p a b", a=2)
        self.tt("dve", nW32.rearrange("p (a b) -> p a b", a=4), U32v, P32v[:, :, 31:32].to_broadcast([128, 4, 32]),
                ALU.subtract, [rU32, rP32], [rnW32])
        self.tt("pool", nW64.rearrange("p (a b) -> p a b", a=2), U64.rearrange("p (a b) -> p a b", a=2),
                P64v[:, :, 63:64].to_broadcast([128, 2, 64]), ALU.subtract, [rU64, rP64], [rnW64])
        tot = P128[:, 127:128]
        ex = [("XA", EA, rEA, 1.0, None), ("YA", EA, rEA, -1.0, None),
              ("XB", U32, rU32, 1.0, None), ("YB", nW32, rnW32, -1.0, None),
              ("XC", U64, rU64, 1.0, None), ("YC", nW64, rnW64, -1.0, None),
              ("XD", U128, rU128, 1.0, None), ("YD", U128, rU128, -1.0, tot)]
        for nm, src, rs, sc, b_ in ex:
            ap, r = H[nm]
            if b_ is None:
                self.act(ap, src, AF.Exp, [rs], [r], scale=sc)
            else:
                self.act(ap, src, AF.Exp, [rs, rP128], [r], scale=sc, bias=b_)
        self.act(self.dec[:, it % 8:it % 8 + 1], tot, AF.Exp, [rP128], [self.r_dec[it % 8]])
        O = self.hop[it % 4]
        engs = ["dve", "pool"]
        k = 0
        for lv in "ABCD":
            eq, ek = ("X" + lv, "Y" + lv) if d == 0 else ("Y" + lv, "X" + lv)
            self.tt(engs[k % 2], O["q" + lv][0], q, H[eq][0], ALU.mult, [rq, H[eq][1]], [O["q" + lv][1]]); k += 1
            self.tt(engs[k % 2], O["k" + lv][0], kk, H[ek][0], ALU.mult, [rkk, H[ek][1]], [O["k" + lv][1]]); k += 1

    def hgrn_back1(self, d, hd, ti, it):
        H = self.hg
        O = self.hop[it % 4]
        rc = self.r_const
        slots = [(0, 0), (0, 128), (0, 256)]
        sps = []
        for i, lv in enumerate("ABC"):
            ps, pr = self.PS(slots[i][0], slots[i][1], slots[i][1] + 128)
            self.mm(ps, O["k" + lv][0], O["q" + lv][0], True, True, [O["k" + lv][1], O["q" + lv][1]], pr)
            sps.append((ps, pr))
        slot7 = it % 2
        pt = self.pb7[:, slot7 * 128:(slot7 + 1) * 128]
        ptr = [self.p7res[slot7]]
        self.P.op("pe", lambda e: e.transpose(pt, O["kD"][0], self.ident), [O["kD"][1], rc], ptr)
        Sf, rSf = H["Sf"]; tB, rtB = H["tB"]; tC, rtC = H["tC"]
        self.tt("dve", Sf, sps[0][0], self.masks[:, d * 3 + 0, :], ALU.mult, sps[0][1] + [rc], [rSf])
        self.tt("dve", tB, sps[1][0], self.masks[:, d * 3 + 1, :], ALU.mult, sps[1][1] + [rc], [rtB])
        self.tt("dve", tC, sps[2][0], self.masks[:, d * 3 + 2, :], ALU.mult, sps[2][1] + [rc], [rtC])
        self.tt("pool", Sf, Sf, tB, ALU.add, [rSf, rtB], [rSf])
        Sb, rSb = O["Sb"]
        self.tt("pool", Sb, Sf, tC, ALU.add, [rSf, rtC], [rSb])
        kDt, rkDt = O["kDt"]
        self.cp("act", kDt, pt, ptr, [rkDt])

    def hgrn_back2(self, d, hd, ti, it):
        O = self.hop[it % 4]
        v = self.vS[:, ti, hd * 128:(hd + 1) * 128]; rv = self.r_vS[ti]
        rst = self.r_state[hd]
        po, por = self.PS(1, 0, 128)
        self.mm(po, v, O["Sb"][0], True, False, [rv, O["Sb"][1]], por)
        self.mm(po, self.state_b[:, hd, :], O["qD"][0], False, True, [rst, O["qD"][1]], por)
        pst, pstr = self.PS(1, 128, 256)
        self.mm(pst, O["kDt"][0], v, True, True, [O["kDt"][1], rv], pstr)
        self.stt(self.state_f[:, hd, :], self.state_f[:, hd, :], self.dec[:, it % 8:it % 8 + 1], pst, ALU.mult, ALU.add,
                 pstr + [rst, self.r_dec[it % 8]], [rst])
        self.cp("act", self.state_b[:, hd, :], self.state_f[:, hd, :], [rst], [rst])
        return po, por

    def pass1_block(self, l, blk):
        u0, nb, isctx = blk
        self.block_norm1(l, blk)
        if not isctx:
            self.load_rope(u0, nb)
        rhs = lambda kc: self.hT[:, kc, 0:nb]
        reqs = [self.win_req(l, g) for g in (2, 3, 5, 6)]
        tiles = []
        for (buf, r) in self.wstream(reqs):
            tiles.append((buf[:, 0:8192].rearrange("p (k j) -> p k j", k=KC), r))
            gi = len(tiles) - 1
            wt, rw = tiles[gi]
            if gi == 0:
                for kv in range(2):
                    ps, pr = self.proj_fm(wt, rw, KC, kv * 128, rhs, [self.r_h], nb, kv % 2)
                    self.qk_head(l, ps, pr, nb, self.qkg[:, l, 1:2], not isctx, self.KT[:, kv, u0:u0 + nb], [self.r_KT])
                for ti in range(nb // 128):
                    ps, pr = self.proj_tm(wt, rw, 256, 256, self.hT, self.r_h, ti * 128, ti % 2)
                    self.cp("act", self.V[:, u0 // 128 + ti, :], ps, pr, [self.r_V])
            elif gi == 1:
                self.hgrn_q(wt, rw, nb)
            elif gi == 2:
                self.hgrn_f(l, 1, wt, rw, nb)
            elif gi == 3:
                self.hi_prep(wt, rw, nb)
        its = [(ti, hd) for ti in reversed(range(nb // 128)) for hd in range(4)]
        n_it = len(its)
        for s_ in range(n_it + 2):
            if 0 <= s_ - 2 < n_it:
                ti, hd = its[s_ - 2]
                po, por = self.hgrn_back2(1, hd, ti, self.itc + s_ - 2)
                self.cp("act", self.obst[:, hd, :], po, por, [self.r_obst])
                if hd == 3:
                    gt = u0 // 128 + ti
                    self.dma("sp", self.obT[:, :, gt * 128:(gt + 1) * 128].rearrange("h p n -> p h n"), self.obst, self.st_st,
                             [self.r_obst], [self.r_ob[gt]])
            if 0 <= s_ - 1 < n_it:
                ti, hd = its[s_ - 1]
                self.hgrn_back1(1, hd, ti, self.itc + s_ - 1)
            if s_ < n_it:
                ti, hd = its[s_]
                self.hgrn_front(l, 1, hd, ti, self.itc + s_)
        self.itc += n_it

    def pass2_block(self, l, blk, last):
        u0, nb, isctx = blk
        skip_out = last and isctx
        ntile = nb // 128
        self.block_norm1(l, blk)
        if not isctx:
            self.load_rope(u0, nb)
        rhs = lambda kc: self.hT[:, kc, 0:nb]
        gs = [3, 4, 6] if skip_out else [3, 4, 6, 7, 0, 1, 8, 9]
        reqs = [self.win_req(l, g) for g in gs]
        if not skip_out:
            reqs += [("sp", self.wout_s[l][g], [self.r_wsc[l]["out"]], 8192) for g in range(4)]
        ws = self.wstream(reqs)

        def nxt():
            buf, r = next(ws)
            return buf[:, 0:8192].rearrange("p (k j) -> p k j", k=KC), r

        wq, rwq = nxt()
        self.hgrn_q(wq, rwq, nb)
        wf, rwf = nxt()
        self.hgrn_f(l, 0, wf, rwf, nb)
        wt, rw = nxt()
        self.hi_prep(wt, rw, nb)
        oblds = [(self.obld, self.r_obld), (self.obst, self.r_obst)]
        if not skip_out:
            wt, rw = nxt()
            for hd in range(4):
                ps, pr = self.proj_fm(wt, rw, KC, hd * 128, rhs, [self.r_h], nb, hd % 2)
                self.act(self.gateS[:, hd, 0:nb], ps, AF.Silu, pr, [self.r_gateS[hd]])
            for ti in range(ntile):
                gt = u0 // 128 + ti
                self.dma("sp", oblds[ti][0], self.obT[:, :, gt * 128:(gt + 1) * 128].rearrange("h p n -> p h n"), self.st_ld,
                         [self.r_ob[gt]], [oblds[ti][1]])
            for g in (0, 1):
                wt, rw = nxt()
                for hh in range(4):
                    h = g * 4 + hh
                    ps, pr = self.proj_fm(wt, rw, KC, hh * 128, rhs, [self.r_h], nb, hh % 2)
                    self.qk_head(l, ps, pr, nb, self.qkg[:, l, 0:1], not isctx, self.QT[:, h, 0:nb], [self.r_QT[h]])
        H = self.hg
        its = [(ti, hd) for ti in range(ntile) for hd in range(4)]
        n_it = len(its)
        for s_ in range(n_it + 3):
            if not skip_out and 0 <= s_ - 3 < n_it:
                ti, hd = its[s_ - 3]
                c0 = ti * 128
                oS, roS = H["oS"]; sqo, rsqo = H["sqo"]; rso, rrso = H["rso"]
                pn, pnr = self.PS(0, 384, 512)
                self.mm(pn, self.ones_f, sqo, True, True, [rsqo, self.r_const], pnr)
                self.rstd_lnexp(rso, pn, pnr, [rrso], 1.0 / 128)
                self.stt(oS, oS, self.hgg[:, l:l + 1], rso, ALU.mult, ALU.mult, [roS, rrso, self.r_par], [roS])
                self.tt("pool", self.mixT[:, 8 + hd, c0:c0 + 128], oS, self.gateS[:, hd, c0:c0 + 128], ALU.mult,
                        [roS, self.r_gateS[hd]], [self.r_mix[8 + hd]])
            if not skip_out and s_ < 8:
                self.attention_head(l, blk, s_)
            if 0 <= s_ - 2 < n_it:
                ti, hd = its[s_ - 2]
                po, por = self.hgrn_back2(0, hd, ti, self.itc + s_ - 2)
                if not skip_out:
                    oS, roS = H["oS"]; sqo, rsqo = H["sqo"]
                    self.tt("dve", oS, po, oblds[ti][0][:, hd, :], ALU.add, por + [oblds[ti][1]], [roS])
                    self.tt("pool", sqo, oS, oS, ALU.mult, [roS], [rsqo])
            if 0 <= s_ - 1 < n_it:
                ti, hd = its[s_ - 1]
                self.hgrn_back1(0, hd, ti, self.itc + s_ - 1)
            if s_ < n_it:
                ti, hd = its[s_]
                self.hgrn_front(l, 0, hd, ti, self.itc + s_)
        self.itc += n_it
        if skip_out:
            return
        wt, rw = nxt()
        for g in range(4):
            ps, pr = self.proj_fm(wt, rw, KC, g * 128, rhs, [self.r_h], nb, g % 2)
            self.act(self.uS[:, g, 0:nb], ps, AF.Gelu_apprx_tanh, pr, [self.r_uS[g]])
        wsv, rwsv = nxt()
        for ti in range(ntile):
            c0 = ti * 128
            ps, pr = self.proj_tm(wsv, rwsv, 0, 512, self.hT, self.r_h, c0, ti % 2)
            self.act(self.gv, ps, AF.Gelu_apprx_tanh, pr, [self.r_gv])
            self.tt("pool", self.sqv, self.gv, self.gv, ALU.mult, [self.r_gv], [self.r_sqv])
            self.P.op("dve", lambda e: e.tensor_reduce(out=self.ssum, in_=self.sqv.rearrange("p (g d) -> p g d", g=4), axis=AX.X,
                                                       op=ALU.add), [self.r_sqv], [self.r_ssum])
            self.rstd_lnexp(self.ssum, self.ssum, [self.r_ssum], [self.r_ssum], 1.0 / 128)
            self.tt("dve", self.vn.rearrange("p (g d) -> p g d", g=4), self.gv.rearrange("p (g d) -> p g d", g=4),
                    self.ssum.unsqueeze(2).to_broadcast([128, 4, 128]), ALU.mult, [self.r_gv, self.r_ssum], [self.r_vn])
            self.tt("pool", self.vnb, self.vn, self.sggB, ALU.mult, [self.r_vn, self.r_sg], [self.r_vnb])
            for g in range(4):
                pg, pgr = self.PS(3 + (g % 2), 256, 384)
                self.mm(pg, self.vnb[:, g * 128:(g + 1) * 128], self.sgwT[:, g, :], True, True, [self.r_vnb, self.r_sg], pgr)
                self.tt("dve", self.tsg, pg, self.sgbB[:, g * 128:(g + 1) * 128], ALU.add, pgr + [self.r_sg], [self.r_tsg])
                self.tt("pool", self.mixT[:, 12 + g, c0:c0 + 128], self.tsg, self.uS[:, g, c0:c0 + 128], ALU.mult,
                        [self.r_tsg, self.r_uS[g]], [self.r_mix[12 + g]])
        if self.dbg and l == 0:
            self.dma("pool", self.dbgm[:, :, u0:u0 + nb].rearrange("k p n -> p k n"), self.mixT[:, :, 0:nb], self.st_st,
                     self.r_mix, [self.r_dbg], join=True)
        r_ = 1 if isctx else 0
        mrhs = lambda kc: self.mixT[:, kc, 0:nb]
        for g in range(4):
            wt, rw = nxt()
            for mm_ in range(4):
                m = g * 4 + mm_
                ps, pr = self.proj_fm(wt, rw, KC, mm_ * 128, mrhs, self.r_mix, nb, mm_ % 2)
                self.stt(self.xsb[:, m, 0:nb], ps, self.modT[:, r_, 32 + m:33 + m], self.xsb[:, m, 0:nb], ALU.mult, ALU.add,
                         pr + [self.r_x, self.r_mod], [self.r_x])
        rs = self.r_xM[u0 // 128:(u0 + nb) // 128]
        self.dma("sp", self.xMs[:, :, u0:u0 + nb].rearrange("k p n -> p k n"), self.xsb[:, :, 0:nb], self.st_st, [self.r_x], rs)

    def attention_head(self, l, blk, h):
        u0, nb, isctx = blk
        keyt = [0, 1] if isctx else list(range(U // 128))
        scale = 128 ** -0.5
        kv = h // 4
        n = len(keyt)
        LA = 2
        banks = [2, 3, 4]
        po, por = self.PS(5, 0, nb)
        pd, pdr = self.PS(6, 0, nb)
        sts = {}
        for ji in range(n + LA):
            if ji < n:
                j = keyt[ji]
                cnt = self.acnt
                self.acnt += 1
                ps, pr = self.PS(banks[cnt % 3], 0, nb)
                self.mm(ps, self.KT[:, kv, j * 128:(j + 1) * 128], self.QT[:, h, 0:nb], True, True, [self.r_KT, self.r_QT[h]], pr)
                pT = self.pT[cnt % 4][:, 0:nb]
                rpT = self.r_pT[cnt % 4]
                self.act(pT, ps, AF.Exp, pr, [rpT], scale=scale)
                sts[ji] = (pT, rpT)
            jv = ji - LA
            if jv >= 0:
                pT, rpT = sts.pop(jv)
                j = keyt[jv]
                self.mm(po, self.V[:, j, kv * 128:(kv + 1) * 128], pT, jv == 0, jv == n - 1, [self.r_V, rpT], por)
                self.mm(pd, self.ones_b, pT, jv == 0, jv == n - 1, [self.r_const, rpT], pdr)
        self.P.op("dve", lambda e, pd=pd: e.reciprocal(out=self.rD[:, 0:nb], in_=pd), pdr, [self.r_rD])
        self.tt("dve", self.mixT[:, h, 0:nb], po, self.rD[:, 0:nb], ALU.mult, por + [self.r_rD], [self.r_mix[h]])

    def ffn_window(self, l, win, last):
        o0, o1, s0, s1 = win
        isctx = (s0 == 0)
        r_ = 1 if isctx else 0
        c0 = max(o0 - 1, s0)
        c1 = min(o1 + 1, s1)
        n = c1 - c0
        a = o0 - c0
        b = o1 - c0
        no = o1 - o0
        rxs = self.r_xM[c0 // 128:(c1 + 127) // 128]
        self.dma("sp", self.xsb[:, :, 0:n], self.xMs[:, :, c0:c1].rearrange("k p n -> p k n"), self.st_ld, rxs, [self.r_x])
        self.norm(self.xsb, self.r_x, n, self.G2[:, r_, :], self.modT[:, r_, 48:64], [self.r_mod],
                  lambda kc: self.hT[:, kc, 0:n], [self.r_h], self.ntmp)
        rhs = lambda kc: self.hT[:, kc, 0:n]
        reqs = [("sp", self.wup_s[l][t], [self.r_wsc[l]["up"]], 8192) for t in range(22)]
        reqs += [("sp", self.wdn_s[l][m], [self.r_wsc[l]["dn"]], FC * 128) for m in range(16)]
        ws = self.wstream(reqs)
        cw = self.convw
        cb = self.convb
        rp = [self.r_par]
        for t in range(22):
            buf, rw = next(ws)
            wt = buf[:, 0:8192].rearrange("p (k j) -> p k j", k=KC)
            for i in range(2):
                j = 2 * t + i
                pg, pgr = self.proj_fm(wt, rw, KC, i * 128, rhs, [self.r_h], n, 0)
                pv, pvr = self.proj_fm(wt, rw, KC, 256 + i * 128, rhs, [self.r_h], n, 1)
                bi = j % 2
                for (ps, pr, acc, racc, fc) in [(pg, pgr, self.accg[bi], self.r_accg[bi], j), (pv, pvr, self.accv[bi], self.r_accv[bi], FC + j)]:
                    self.act(acc[:, a:b], ps[:, a:b], AF.Identity, pr + rp, [racc], bias=cb[:, l, fc:fc + 1], scale=cw[:, l, 1, fc:fc + 1])
                    lo = max(a, 1)
                    self.stt(acc[:, lo:b], ps[:, lo - 1:b - 1], cw[:, l, 0, fc:fc + 1], acc[:, lo:b], ALU.mult, ALU.add, pr + rp + [racc], [racc])
                    hi = min(b, n - 1)
                    self.stt(acc[:, a:hi], ps[:, a + 1:hi + 1], cw[:, l, 2, fc:fc + 1], acc[:, a:hi], ALU.mult, ALU.add, pr + rp + [racc], [racc])
                self.act(self.sil[bi][:, a:b], self.accg[bi][:, a:b], AF.Silu, [self.r_accg[bi]], [self.r_sil[bi]])
                self.tt("pool", self.aT[:, j, 0:no], self.sil[bi][:, a:b], self.accv[bi][:, a:b], ALU.mult,
                        [self.r_sil[bi], self.r_accv[bi]], [self.r_aT[j]])
        for m in range(16):
            buf, rw = next(ws)
            wt = buf[:, 0:FC * 128].rearrange("p (k j) -> p k j", k=FC)
            ps, pr = self.PS(m % 2, 0, no)
            for j in range(FC):
                self.mm(ps, wt[:, j, :], self.aT[:, j, 0:no], j == 0, j == FC - 1, [rw, self.r_aT[j]], pr)
            self.stt(self.xsb[:, m, a:b], ps, self.modT[:, r_, 80 + m:81 + m], self.xsb[:, m, a:b], ALU.mult, ALU.add,
                     pr + [self.r_x, self.r_mod], [self.r_x])
        ros = self.r_xT[o0 // 128:(o1 + 127) // 128]
        if not last:
            self.dma("sp", self.xTs[:, :, o0:o1].rearrange("k p n -> p k n"), self.xsb[:, :, a:b], self.st_st, [self.r_x], ros)
        else:
            sq, rsq, tm, rtm, rstd, rrstd = self.ntmp
            pn, pnr = self.PS(2, 0, no)
            for kc in range(KC):
                i = kc % 2
                self.act(sq[i][:, 0:no], self.xsb[:, kc, a:b], AF.Square, [self.r_x], [rsq[i]])
                self.mm(pn, self.ones_f, sq[i][:, 0:no], kc == 0, kc == KC - 1, [rsq[i], self.r_const], pnr)
            self.act(rstd[:, 0:no], pn, AF.Sqrt, pnr, [rrstd], bias=EPS, scale=1.0 / D)
            self.P.op("dve", lambda e: e.reciprocal(out=rstd[:, 0:no], in_=rstd[:, 0:no]), [rrstd], [rrstd])
            for kc in range(KC):
                self.stt(self.xsb[:, kc, a:b], self.xsb[:, kc, a:b], self.fng[:, kc:kc + 1], rstd[:, 0:no], ALU.mult, ALU.mult,
                         [self.r_x, rrstd, self.r_par], [self.r_x])
            self.dma("sp", self.outT[:, :, o0 - LC:o1 - LC].rearrange("k p n -> p k n"), self.xsb[:, :, a:b], self.st_st,
                     [self.r_x], [self.r_out], join=True)


def _consts():
    n_freq = 32
    inv = (np.float32(10000.0) ** (-np.arange(n_freq, dtype=np.float32) / np.float32(n_freq))).astype(np.float32)
    t = np.arange(T)
    row = (t // 64).astype(np.float32)
    col = (t % 64).astype(np.float32)
    ang = np.concatenate([row[:, None] * inv[None, :], col[:, None] * inv[None, :]], axis=-1).astype(np.float32)
    cos = np.cos(ang).astype(np.float32)
    sin = np.sin(ang).astype(np.float32)
    ropeT = np.stack([np.repeat(cos, 2, axis=1).T, np.repeat(sin, 2, axis=1).T]).astype(np.float32)
    rotT = np.zeros((128, 128), np.float32)
    for i in range(64):
        rotT[2 * i + 1, 2 * i] = -1.0
        rotT[2 * i, 2 * i + 1] = 1.0
    ident = np.eye(128).astype(ml_dtypes.bfloat16)
    s = np.arange(128)[:, None]
    tt = np.arange(128)[None, :]
    mA = ((s // 32 == tt // 32) & (s <= tt))
    mB = ((s // 64 == tt // 64) & (s % 64 < 32) & (tt % 64 >= 32))
    mC = ((s < 64) & (tt >= 64))
    masks = np.stack([mA, mB, mC, mA.T, mB.T, mC.T], axis=1).astype(np.float32)
    seg = np.ones((128, 3, 128), np.float32)
    seg[:, 0, ::32] = 0
    seg[:, 1, ::64] = 0
    seg[:, 2, 0] = 0
    return dict(ropeT=np.ascontiguousarray(ropeT), rotT=rotT, ident=ident, identf=np.eye(128, dtype=np.float32), masks=np.ascontiguousarray(masks), seg=seg)


def _layout(inputs, b, NL=4):
    f = lambda a: np.ascontiguousarray(a, dtype=np.float32)
    x = inputs["x"][b]
    ctx = inputs["ctx"][b]
    xc = np.concatenate([ctx, x], axis=0)
    m = {}
    m["xT0"] = f(xc.T.reshape(KC, 128, U))
    cc = np.stack([inputs["c"][b], inputs["c_ctx"]], axis=-1)
    m["cT"] = f(cc.reshape(KC, 128, 2).transpose(1, 0, 2))
    for k in ("w_ada", "w_in", "w_out", "w_up", "w_down"):
        m[k] = f(inputs[k][:NL])
    m["badaT"] = f(inputs["b_ada"].reshape(4, 96, 128).transpose(2, 0, 1))
    m["n1g"] = f(inputs["norm1_g"].reshape(4, KC, 128).transpose(2, 0, 1))
    m["n2g"] = f(inputs["norm2_g"].reshape(4, KC, 128).transpose(2, 0, 1))
    m["fng"] = f(inputs["final_norm_g"].reshape(KC, 128).T)
    m["qkg"] = f(np.stack([inputs["q_norm_g"], inputs["k_norm_g"]], axis=-1).transpose(1, 0, 2))
    m["hgg"] = f(inputs["hg_norm_g"].T)
    m["lbp"] = f(inputs["hg_lower_bounds"].reshape(2, 4, 4, 128).transpose(3, 0, 1, 2))
    m["sgg"] = f(inputs["sg_norm_g"].reshape(4, 1, 512))
    m["sgb"] = f(inputs["sg_b"].reshape(4, 1, 512))
    m["sgwT"] = np.ascontiguousarray(inputs["sg_w"].transpose(0, 3, 1, 2), dtype=np.float32)
    m["convw"] = f(inputs["conv_w"].reshape(4, 3, 88, 128).transpose(3, 0, 1, 2))
    m["convb"] = f(inputs["conv_b"].reshape(4, 88, 128).transpose(2, 0, 1))
    return m


_NC_CACHE = {}


def kernel(**inputs):
    inputs = {k: np.asarray(v) for k, v in inputs.items()}
    if "nc" not in _NC_CACHE:
        _NC_CACHE["nc"] = Builder(4, False).build()
    nc = _NC_CACHE["nc"]
    consts = _consts()
    in_maps = []
    for b in range(N_CORES):
        m = _layout(inputs, b)
        m.update(consts)
        in_maps.append(m)
    res = run_bass_kernel_spmd(nc, in_maps, core_ids=list(range(N_CORES)))
    out = np.empty((4, T, D), np.float32)
    for b in range(4):
        out[b] = res.results[b]["outT"].reshape(D, T).T
    return out
```

```python
import contextlib
import numpy as np
import ml_dtypes
import concourse.bass as bass
import concourse.mybir as mybir
from concourse.bass_utils import run_bass_kernel_spmd

F32 = mybir.dt.float32
BF16 = mybir.dt.bfloat16
AF = mybir.ActivationFunctionType
ALU = mybir.AluOpType
AX = mybir.AxisListType

import os
SEM_LIMIT = int(os.environ.get("SEM_LIMIT", "16000"))
D = 2048
KC = 16
T = 4096
LC = 256
U = T + LC
DFF = 5632
FC = 44
EPS = 1e-6
F_MIN = 1e-30
NB = 256
FW = 456
N_CORES = 4


class Res:
    __slots__ = ("name", "writers", "readers", "excl")

    def __init__(self, name, excl=False):
        self.name = name
        self.writers = {}
        self.readers = {}
        self.excl = excl


class Stream:
    def __init__(self, name, nslots):
        self.name = name
        self.slots = [[None, 0] for _ in range(nslots)]
        self.i = 0


def _merge(d, s, v):
    if d.get(s, 0) < v:
        d[s] = v


class Prog:
    ENG = ("pe", "act", "dve", "pool", "sp")

    def __init__(self, nc, stack):
        self.nc = nc
        self.stack = stack
        self.ops = {e: [] for e in self.ENG}
        self.esem = {e: None for e in self.ENG}
        self.ecnt = {e: 0 for e in self.ENG}
        self.known = {e: {} for e in self.ENG}
        self.nsem = 0
        self.nres = 0
        self.esems = {e: set() for e in self.ENG}
        self.streams = []
        self.nops = 0

    def new_sem(self, name):
        self.nsem += 1
        return self.stack.enter_context(self.nc.semaphore(f"{name}_{self.nsem}"))

    def res(self, name=None, excl=False):
        self.nres += 1
        return Res(name or f"r{self.nres}", excl)

    def stream(self, name, nslots):
        st = Stream(name, nslots)
        self.streams.append(st)
        return st

    def sb(self, name, shape, dtype):
        return self.stack.enter_context(self.nc.sbuf_tensor(name, list(shape), dtype))

    def ps(self, name, shape, dtype=F32):
        return self.stack.enter_context(self.nc.psum_tensor(name, list(shape), dtype))

    def _waits(self, eng, deps):
        waits = {}
        kn = self.known[eng]
        own = self.esems[eng]
        for (s, v) in deps:
            if kn.get(s, 0) >= v:
                continue
            if eng == "pe" and s in own:
                continue
            _merge(waits, s, v)
        for s, v in waits.items():
            kn[s] = v
        return list(waits.items())

    def _deps(self, reads, writes, join):
        deps = []
        for r in reads:
            deps.extend(r.writers.items())
        for r in writes:
            if join and not r.readers:
                continue
            deps.extend(r.writers.items())
            deps.extend(r.readers.items())
        return deps

    def _post(self, ev, reads, writes, join):
        s, v = ev
        for r in reads:
            _merge(r.readers, s, v)
        for r in writes:
            if join and not r.readers:
                _merge(r.writers, s, v)
            else:
                r.writers = {s: v}
                r.readers = {}

    def op(self, eng, fn, reads=(), writes=()):
        ex = [r for r in reads if r.excl]
        if ex:
            reads = [r for r in reads if not r.excl]
            writes = list(writes) + ex
        waits = self._waits(eng, self._deps(reads, writes, False))
        if self.esem[eng] is None or self.ecnt[eng] >= SEM_LIMIT:
            self.esem[eng] = self.new_sem(eng)
            self.esems[eng].add(self.esem[eng])
            self.ecnt[eng] = 0
        self.ecnt[eng] += 1
        ev = (self.esem[eng], self.ecnt[eng])
        self.ops[eng].append((waits, fn, self.esem[eng], 1))
        self._post(ev, reads, writes, False)
        self.nops += 1
        return ev

    def dma(self, eng, fn, stream, reads=(), writes=(), join=False):
        slot = stream.slots[stream.i]
        stream.i = (stream.i + 1) % len(stream.slots)
        deps = self._deps(reads, writes, join)
        if slot[0] is not None:
            deps.append((slot[0], slot[1]))
        waits = self._waits(eng, deps)
        if slot[0] is None or slot[1] + 16 > SEM_LIMIT:
            slot[0] = self.new_sem("d" + stream.name)
            slot[1] = 0
        slot[1] += 16
        ev = (slot[0], slot[1])
        self.ops[eng].append((waits, fn, slot[0], 16))
        self._post(ev, reads, writes, join)
        self.nops += 1
        return ev

    def all_events(self):
        evs = []
        for e in self.ENG:
            if self.esem[e] is not None:
                evs.append((self.esem[e], self.ecnt[e]))
        for st in self.streams:
            for sl in st.slots:
                if sl[0] is not None:
                    evs.append((sl[0], sl[1]))
        return evs

    def barrier(self, engines=None):
        evs = self.all_events()
        for e in (engines or self.ENG):
            w = self._waits(e, evs)
            if w:
                self.ops[e].append((w, None, None, 0))

    def emit(self):
        nc = self.nc
        block = self.stack.enter_context(nc.Block())
        ops = self.ops

        def run(e, name):
            for (waits, fn, sem, inc) in ops[name]:
                for (s, v) in waits:
                    e.wait_ge(s, v)
                if fn is not None:
                    fn(e).then_inc(sem, inc)

        @block.tensor
        def _(e):
            run(e, "pe")

        @block.scalar
        def _(e):
            run(e, "act")

        @block.vector
        def _(e):
            run(e, "dve")

        @block.gpsimd
        def _(e):
            run(e, "pool")

        @block.sync
        def _(e):
            run(e, "sp")


class Arena:
    def __init__(self, tensor, n):
        self.t = tensor
        self.n = n
        self.off = 0

    def reset(self):
        self.off = 0

    def take(self, *shape):
        n = int(np.prod(shape))
        assert self.off + n <= self.n, (self.off, n, self.n)
        ap = self.t[:, self.off:self.off + n]
        self.off += n
        if len(shape) == 2:
            ap = ap.rearrange("p (a b) -> p a b", a=shape[0])
        elif len(shape) == 3:
            ap = ap.rearrange("p (a b c) -> p a b c", a=shape[0], b=shape[1])
        return ap


class Builder:
    def __init__(self, NL=4, dbg=False, stop=None):
        self.NL = NL
        self.dbg = dbg
        self.stop = stop

    def mm(self, out, lhsT, rhs, st, sp, rd, wr):
        self.P.op("pe", lambda e: e.matmul(out, lhsT=lhsT, rhs=rhs, start=st, stop=sp), rd, wr)

    def act(self, out, in_, func, rd, wr, bias=None, scale=None):
        kw = {}
        if bias is not None:
            kw["bias"] = bias
        if scale is not None:
            kw["scale"] = scale
        self.P.op("act", lambda e: e.activation(out=out, in_=in_, func=func, **kw), rd, wr)

    def tt(self, eng, out, in0, in1, op, rd, wr):
        self.P.op(eng, lambda e: e.tensor_tensor(out=out, in0=in0, in1=in1, op=op), rd, wr)

    def ts(self, eng, out, in0, s1, s2, op0, op1, rd, wr):
        if op1 is None:
            self.P.op(eng, lambda e: e.tensor_scalar(out=out, in0=in0, scalar1=s1, scalar2=None, op0=op0), rd, wr)
        else:
            self.P.op(eng, lambda e: e.tensor_scalar(out=out, in0=in0, scalar1=s1, scalar2=s2, op0=op0, op1=op1), rd, wr)

    def stt(self, out, in0, scalar, in1, op0, op1, rd, wr):
        self.P.op("dve", lambda e: e.scalar_tensor_tensor(out=out, in0=in0, scalar=scalar, in1=in1, op0=op0, op1=op1), rd, wr)

    def cp(self, eng, out, in_, rd, wr):
        if eng == "act":
            self.P.op("act", lambda e: e.activation(out=out, in_=in_, func=AF.Copy), rd, wr)
        else:
            self.P.op(eng, lambda e: e.tensor_copy(out=out, in_=in_), rd, wr)

    def dma(self, eng, out, in_, stream, rd, wr, join=False, slow=False):
        if slow:
            self.P.dma(eng, lambda e: e.dma_start(out=out, in_=in_, allow_slow_non_contiguous=True), stream, rd, wr, join)
        else:
            self.P.dma(eng, lambda e: e.dma_start(out=out, in_=in_), stream, rd, wr, join)

    def PS(self, b, c0, c1, rows=128):
        ap = self.pb[b][0:rows, c0:c1]
        return ap, [self.pres[b]]

    def wload(self, eng, src, src_res, nelem):
        i = self.wi
        self.wi = (i + 1) % len(self.wbuf)
        buf = self.wbuf[i]
        r = self.wres[i]
        self.dma(eng, buf[:, 0:nelem], src, self.st_w, src_res, [r])
        return buf, r

    def wstream(self, reqs, depth=1):
        loaded = []
        for i in range(len(reqs)):
            while len(loaded) < min(len(reqs), i + 1 + depth):
                loaded.append(self.wload(*reqs[len(loaded)]))
            yield loaded[i]

    def build(self):
        NL = self.NL
        nc = bass.Bass("TRN2", target_bir_lowering=False)
        self.nc = nc

        def EI(n, s, d=F32):
            return nc.dram_tensor(n, list(s), d, kind="ExternalInput").ap()

        def SC(n, s, d=F32):
            return nc.dram_tensor(n, list(s), d, kind="Internal").ap()

        self.xT0 = EI("xT0", [KC, 128, U])
        self.cT_d = EI("cT", [128, KC, 2])
        self.w_ada = EI("w_ada", [NL, D, 6 * D])
        self.w_in = EI("w_in", [NL, D, 5120])
        self.w_out = EI("w_out", [NL, D, D])
        self.w_up = EI("w_up", [NL, D, 2 * DFF])
        self.w_down = EI("w_down", [NL, DFF, D])
        self.badaT_d = EI("badaT", [128, 4, 96])
        self.n1g_d = EI("n1g", [128, 4, KC])
        self.n2g_d = EI("n2g", [128, 4, KC])
        self.fng_d = EI("fng", [128, KC])
        self.qkg_d = EI("qkg", [128, 4, 2])
        self.hgg_d = EI("hgg", [128, 4])
        self.lbp_d = EI("lbp", [128, 2, 4, 4])
        self.sgg_d = EI("sgg", [4, 1, 512])
        self.sgb_d = EI("sgb", [4, 1, 512])
        self.sgwT_d = EI("sgwT", [4, 128, 4, 128])
        self.convw_d = EI("convw", [128, 4, 3, 88])
        self.convb_d = EI("convb", [128, 4, 88])
        self.rope_d = EI("ropeT", [2, 128, T])
        self.rotT_d = EI("rotT", [128, 128])
        self.ident_d = EI("ident", [128, 128], BF16)
        self.identf_d = EI("identf", [128, 128])
        self.masks_d = EI("masks", [128, 6, 128])
        self.seg_d = EI("seg", [128, 3, 128])
        self.outT = nc.dram_tensor("outT", [KC, 128, T], F32, kind="ExternalOutput").ap()
        if self.dbg:
            self.dbg1 = nc.dram_tensor("dbg1", [KC, 128, U], F32, kind="ExternalOutput").ap()
            self.dbg2 = nc.dram_tensor("dbg2", [KC, 128, U], F32, kind="ExternalOutput").ap()
            self.dbgm = nc.dram_tensor("dbgm", [KC, 128, U], F32, kind="ExternalOutput").ap()
            self.dbgs = nc.dram_tensor("dbgs", [128, 6600], F32, kind="ExternalOutput").ap()

        self.xTs = SC("xTs", [KC, 128, U])
        self.xMs = SC("xMs", [KC, 128, U])
        self.obT = SC("obT", [4, 128, U])
        self.hTd = SC("hTd", [KC, 128, U], BF16)
        self.modd = SC("modd", [2, 6 * D])
        self.win_s = [SC(f"win_s{l}", [10, 128, KC * 512], BF16) for l in range(NL)]
        self.wout_s = [SC(f"wout_s{l}", [4, 128, KC * 512], BF16) for l in range(NL)]
        self.wup_s = [SC(f"wup_s{l}", [22, 128, KC * 512], BF16) for l in range(NL)]
        self.wdn_s = [SC(f"wdn_s{l}", [16, 128, FC * 128], BF16) for l in range(NL)]

        with contextlib.ExitStack() as stack:
            P = Prog(nc, stack)
            self.P = P
            self.alloc()
            self.setup()
            for l in range(NL):
                self.layer(l)
            P.barrier()
            P.emit()
        return nc

    def alloc(self):
        P = self.P
        self.st_w = P.stream("w", 2)
        self.st_ld = P.stream("ld", 4)
        self.st_st = P.stream("st", 4)
        self.st_cast = P.stream("cast", 4)
        self.st_x0 = P.stream("x0", 4)
        self.st_h = P.stream("hst", 2)
        self.pb = [P.ps(f"pb{i}", [128, 512]) for i in range(7)]
        self.pres = [P.res(f"pb{i}", excl=True) for i in range(7)]
        self.pb7 = P.ps("pb7", [128, 1024], BF16)
        self.p7res = [P.res("pb7", excl=True)] * 8
        self.wbuf = [P.sb(f"wbuf{i}", [128, 8192], BF16) for i in range(2)]
        self.wres = [P.res(f"wbuf{i}") for i in range(2)]
        self.wi = 0
        self.A32 = Arena(P.sb("A32", [128, 18400], F32), 18400)
        self.A16 = Arena(P.sb("A16", [128, 37500], BF16), 37500)
        self.C32t = P.sb("C32", [128, 6600], F32)
        C32 = Arena(self.C32t, 6600)
        C16 = Arena(P.sb("C16", [128, 1400], BF16), 1400)
        self.masks = C32.take(6, 128)
        self.seg = C32.take(3, 128)
        self.ones_f = C32.take(128)
        self.rotT = C32.take(128)
        self.ident_f = C32.take(128)
        self.sggB = C32.take(512)
        self.sgbB = C32.take(512)
        self.n1g = C32.take(4, KC)
        self.n2g = C32.take(4, KC)
        self.fng = C32.take(KC)
        self.qkg = C32.take(4, 2)
        self.hgg = C32.take(4)
        self.lbe = C32.take(2, 4, 4)
        self.lb = C32.take(2, 4, 4)
        self.oml = C32.take(2, 4, 4)
        self.lbs = C32.take(2, 4)
        self.convw = C32.take(4, 3, 88)
        self.convb = C32.take(4, 88)
        self.badaT = C32.take(4, 96)
        self.cT = C32.take(KC, 2)
        self.modT = C32.take(2, 96)
        self.G1 = C32.take(2, KC)
        self.G2 = C32.take(2, KC)
        self.modrow = [C32.take(512), C32.take(512)]
        self.state_f = C32.take(4, 128)
        self.dec = C32.take(8)
        self.ones_b = C16.take(128)
        self.ident = C16.take(128)
        self.scT = C16.take(KC, 2)
        self.sgwT = C16.take(4, 128)
        self.state_b = C16.take(4, 128)
        R = P.res
        self.r_const = R("const")
        self.r_par = R("par")
        self.r_sg = R("sgpar")
        self.r_lb = R("lb")
        self.r_mod = R("mod")
        self.r_modrow = [R("modrow0"), R("modrow1")]
        self.r_modd = R("modd")
        self.r_state = [R(f"state{h}") for h in range(4)]
        self.r_dec = [R(f"dec{h}") for h in range(8)]
        self.itc = 0
        self.acnt = 0
        self.r_xT = [R(f"xT{j}") for j in range(U // 128)]
        self.r_xM = [R(f"xM{j}") for j in range(U // 128)]
        self.r_ob = [R(f"ob{j}") for j in range(U // 128)]
        self.r_hTd = [R(f"hTd{j}") for j in range(U // 128)]
        self.r_wsc = [{m: R(f"wsc{l}{m}") for m in ("in", "out", "up", "dn")} for l in range(self.NL)]
        self.r_out = R("out")
        self.r_dbg = R("dbg")

    def setup(self):
        P = self.P
        ld = self.st_ld
        rc = [self.r_const]
        rp = [self.r_par]
        for (dst, src) in [(self.masks, self.masks_d), (self.seg, self.seg_d), (self.rotT, self.rotT_d), (self.ident_f, self.identf_d),
                           (self.ident, self.ident_d)]:
            self.dma("sp", dst, src, ld, [], rc, join=True)
        for (dst, src) in [(self.n1g, self.n1g_d), (self.n2g, self.n2g_d), (self.fng, self.fng_d),
                           (self.qkg, self.qkg_d), (self.hgg, self.hgg_d), (self.lbe, self.lbp_d),
                           (self.convw, self.convw_d), (self.convb, self.convb_d), (self.badaT, self.badaT_d),
                           (self.cT, self.cT_d)]:
            self.dma("sp", dst, src, ld, [], rp, join=True)
        P.op("dve", lambda e: e.memset(self.ones_f, 1.0), [], rc)
        P.op("dve", lambda e: e.memset(self.ones_b, 1.0), [], rc)
        self.act(self.scT, self.cT, AF.Silu, rp, [self.r_lb])
        lbe, lb, oml, lbs = self.lbe, self.lb, self.oml, self.lbs
        rl = [self.r_lb]
        self.act(lbe, lbe, AF.Exp, rp + rl, rp)
        self.tt("dve", lbs, lbe[:, :, 0, :], lbe[:, :, 1, :], ALU.add, rp, rl)
        self.tt("dve", lbs, lbs, lbe[:, :, 2, :], ALU.add, rp + rl, rl)
        self.tt("dve", lbs, lbs, lbe[:, :, 3, :], ALU.add, rp + rl, rl)
        P.op("dve", lambda e: e.reciprocal(out=lbs, in_=lbs), rl, rl)
        P.op("dve", lambda e: e.memset(lb[:, :, 0, :], 0.0), [], rl)
        self.tt("dve", lb[:, :, 1, :], lbe[:, :, 1, :], lbs, ALU.mult, rp + rl, rl)
        for j in (2, 3):
            self.tt("dve", lb[:, :, j, :], lbe[:, :, j, :], lbs, ALU.mult, rp + rl, rl)
            self.tt("dve", lb[:, :, j, :], lb[:, :, j, :], lb[:, :, j - 1, :], ALU.add, rl, rl)
        self.ts("dve", oml, lb, -1.0, 1.0, ALU.mult, ALU.add, rl, rl)
        for kc in range(KC):
            self.dma("sp", self.xTs[kc], self.xT0[kc], self.st_x0, [], self.r_xT, join=True)
        for l in range(self.NL):
            self.cast_weights(l)

    def cast_weights(self, l):
        cs = self.st_cast
        rw = self.r_wsc[l]
        for g in range(10):
            self.dma("pool", self.win_s[l][g].rearrange("p (k j) -> p k j", k=KC),
                     self.w_in[l][:, g * 512:(g + 1) * 512].rearrange("(k p) j -> p k j", p=128),
                     cs, [], [rw["in"]], join=True)
        for g in range(4):
            self.dma("pool", self.wout_s[l][g].rearrange("p (k j) -> p k j", k=KC),
                     self.w_out[l][:, g * 512:(g + 1) * 512].rearrange("(k p) j -> p k j", p=128),
                     cs, [], [rw["out"]], join=True)
        for t in range(22):
            dst = self.wup_s[l][t].rearrange("p (k j) -> p k j", k=KC)
            self.dma("pool", dst[:, :, 0:256],
                     self.w_up[l][:, t * 256:(t + 1) * 256].rearrange("(k p) j -> p k j", p=128),
                     cs, [], [rw["up"]], join=True)
            self.dma("pool", dst[:, :, 256:512],
                     self.w_up[l][:, DFF + t * 256:DFF + (t + 1) * 256].rearrange("(k p) j -> p k j", p=128),
                     cs, [], [rw["up"]], join=True)
        for m in range(16):
            self.dma("pool", self.wdn_s[l][m].rearrange("p (k j) -> p k j", k=FC),
                     self.w_down[l][:, m * 128:(m + 1) * 128].rearrange("(k p) j -> p k j", p=128),
                     cs, [], [rw["dn"]], join=True)

    def ada(self, l):
        P = self.P
        reqs = []
        for n in range(24):
            reqs.append(("pool", self.w_ada[l][:, n * 512:(n + 1) * 512].rearrange("(k p) j -> p k j", p=128), [], 8192))
        n = 0
        for (buf, r) in self.wstream_ada(reqs):
            wt = buf[:, 0:8192].rearrange("p (k j) -> p k j", k=KC)
            ps, pr = self.PS(n % 2, 0, 512, rows=2)
            for kc in range(KC):
                self.mm(ps, self.scT[:, kc, :], wt[:, kc, :], kc == 0, kc == KC - 1, [r, self.r_lb], pr)
            mr = self.modrow[n % 2]
            rr = self.r_modrow[n % 2]
            self.cp("act", mr[0:2, :], ps, pr, [rr])
            for cc in range(4):
                c = n * 4 + cc
                pm, pmr = self.PS(2, 0, 192)
                self.mm(pm[:, c * 2:c * 2 + 2], mr[0:2, cc * 128:(cc + 1) * 128], self.ident_f[0:2, 0:2], True, True,
                        [rr, self.r_const], pmr)
            n += 1
        rm = [self.r_mod]
        pm, pmr = self.PS(2, 0, 192)
        pm3 = pm.rearrange("p (c r) -> p c r", r=2)
        for r_ in range(2):
            self.cp("dve", self.modT[:, r_, :], pm3[:, :, r_], pmr, rm)
        for r_ in range(2):
            self.tt("dve", self.modT[:, r_, :], self.modT[:, r_, :], self.badaT[:, l, :], ALU.add, rm + [self.r_par], rm)
        for r_ in range(2):
            self.stt(self.G1[:, r_, :], self.modT[:, r_, 16:32], 1.0, self.n1g[:, l, :], ALU.add, ALU.mult, rm + [self.r_par], rm)
            self.stt(self.G2[:, r_, :], self.modT[:, r_, 64:80], 1.0, self.n2g[:, l, :], ALU.add, ALU.mult, rm + [self.r_par], rm)

    def wstream_ada(self, reqs):
        loaded = []
        for i in range(len(reqs)):
            while len(loaded) < min(len(reqs), i + 2):
                eng, src, sres, nelem = reqs[len(loaded)]
                k = self.wi
                self.wi = (k + 1) % len(self.wbuf)
                buf = self.wbuf[k]
                r = self.wres[k]
                self.dma(eng, buf[:, 0:nelem].rearrange("p (k j) -> p k j", k=KC), src, self.st_w, sres, [r])
                loaded.append((buf, r))
            yield loaded[i]

    def norm(self, x, rx, n, G, S, rg, dst, rdst, tmp):
        sq, rsq, tm, rtm, rstd, rrstd = tmp
        pn, pnr = self.PS(2, 0, n)
        for kc in range(KC):
            i = kc % 2
            self.act(sq[i][:, 0:n], x[:, kc, 0:n], AF.Square, [rx], [rsq[i]])
            self.mm(pn, self.ones_f, sq[i][:, 0:n], kc == 0, kc == KC - 1, [rsq[i], self.r_const], pnr)
        self.act(rstd[:, 0:n], pn, AF.Sqrt, pnr, [rrstd], bias=EPS, scale=1.0 / D)
        self.P.op("dve", lambda e: e.reciprocal(out=rstd[:, 0:n], in_=rstd[:, 0:n]), [rrstd], [rrstd])
        for kc in range(KC):
            i = kc % 2
            self.tt("dve", tm[i][:, 0:n], x[:, kc, 0:n], rstd[:, 0:n], ALU.mult, [rx, rrstd], [rtm[i]])
            if S is not None:
                self.act(dst(kc), tm[i][:, 0:n], AF.Identity, [rtm[i]] + rg, rdst, bias=S[:, kc:kc + 1], scale=G[:, kc:kc + 1])
            else:
                self.act(dst(kc), tm[i][:, 0:n], AF.Identity, [rtm[i]] + rg, rdst, scale=G[:, kc:kc + 1])

    def proj_fm(self, wt, rw, kcn, mo, rhs, rrhs, n, bank):
        ps, pr = self.PS(bank, 0, n)
        for kc in range(kcn):
            self.mm(ps, wt[:, kc, mo:mo + 128], rhs(kc), kc == 0, kc == kcn - 1, [rw] + rrhs, pr)
        return ps, pr

    def proj_tm(self, wt, rw, c0, w, hT, rh, t0, bank):
        ps, pr = self.PS(bank, 0, w)
        for kc in range(KC):
            self.mm(ps, hT[:, kc, t0:t0 + 128], wt[:, kc, c0:c0 + w], kc == 0, kc == KC - 1, [rw, rh], pr)
        return ps, pr

    def alloc_p12(self):
        A32, A16, R = self.A32, self.A16, self.P.res
        A32.reset()
        A16.reset()
        n = NB
        self.xsb = A32.take(KC, n); self.r_x = R("xsb")
        self.ntmp = ([A32.take(n), A32.take(n)], [R("sq0"), R("sq1")], [A32.take(n), A32.take(n)], [R("tm0"), R("tm1")],
                     A32.take(n), R("rstd"))
        self.qS = A32.take(4, n); self.lfS = A32.take(4, n); self.kkS = A32.take(4, n)
        self.r_qS = [R(f"qS{h}") for h in range(4)]
        self.r_lfS = [R(f"lfS{h}") for h in range(4)]
        self.r_kkS = [R(f"kkS{h}") for h in range(4)]
        self.t1 = A32.take(n); self.t2 = A32.take(n); self.r_t1 = R("t1"); self.r_t2 = R("t2")
        self.rope = A32.take(2, n); self.r_rope = R("rope")
        self.hg = {}
        for nm in ["Sf", "tB", "tC", "rso"]:
            self.hg[nm] = (A32.take(128), R("hg_" + nm))
        self.hgo = [{nm: (A32.take(128), R(f"hgo{i}_" + nm)) for nm in ["oS", "sqo"]} for i in range(2)]
        self.hgs = [{nm: (A32.take(128), R(f"hgs{i}_" + nm)) for nm in
                     ["P32", "P64", "P128", "U32", "U64", "U128", "EA", "nW32", "nW64", "XA", "YA", "XB", "YB", "XC", "YC", "XD", "YD"]}
                    for i in range(2)]
        self.gv = A32.take(512); self.sqv = A32.take(512); self.vn = A32.take(512)
        self.ssum = A32.take(4); self.tsg = A32.take(128)
        self.r_gv = R("gv"); self.r_sqv = R("sqv"); self.r_vn = R("vn"); self.r_ssum = R("ssum"); self.r_tsg = R("tsg")
        self.rD = A32.take(n); self.r_rD = R("rD")
        self.obst = A32.take(4, 128); self.r_obst = R("obst")
        self.obld = A32.take(4, 128); self.r_obld = R("obld")
        self.sgm = A32.take(n); self.fS = A32.take(n); self.r_sgm = R("sgm"); self.r_fS = R("fS")
        self.hT = A16.take(KC, n); self.r_h = R("hT")
        self.KT = A16.take(2, U); self.r_KT = R("KT")
        self.V = A16.take(U // 128, 256); self.r_V = R("V")
        self.QT = A16.take(8, n); self.r_QT = [R(f"QT{h}") for h in range(8)]
        self.mixT = A16.take(KC, n); self.r_mix = [R(f"mix{c}") for c in range(KC)]
        self.pT = [A16.take(n) for _ in range(4)]; self.r_pT = [R(f"pT{i}") for i in range(4)]
        self.vS = A16.take(n // 128, 512); self.r_vS = [R(f"vS{i}") for i in range(n // 128)]
        self.vnb = A16.take(512); self.r_vnb = R("vnb")
        self.gateS = A16.take(4, n); self.uS = A16.take(4, n)
        self.r_gateS = [R(f"gateS{h}") for h in range(4)]
        self.r_uS = [R(f"uS{h}") for h in range(4)]
        self.hop = [{nm: (A16.take(128), R(f"hop{i}_{nm}")) for nm in ["qA", "kA", "qB", "kB", "qC", "kC", "qD", "kD", "Sb", "kDt"]}
                    for i in range(4)]

    def alloc_p3(self):
        A32, A16, R = self.A32, self.A16, self.P.res
        A32.reset()
        A16.reset()
        n = FW + 8
        self.xsb = A32.take(KC, n); self.r_x = R("xsb3")
        self.ntmp = ([A32.take(n), A32.take(n)], [R("sq0"), R("sq1")], [A32.take(n), A32.take(n)], [R("tm0"), R("tm1")],
                     A32.take(n), R("rstd"))
        self.accg = [A32.take(n) for _ in range(2)]; self.accv = [A32.take(n) for _ in range(2)]
        self.sil = [A32.take(n) for _ in range(2)]
        self.r_accg = [R("accg0"), R("accg1")]; self.r_accv = [R("accv0"), R("accv1")]; self.r_sil = [R("sil0"), R("sil1")]
        self.hT = A16.take(KC, n); self.r_h = R("hT3")
        self.aT = A16.take(FC, n); self.r_aT = [R(f"aT{j}") for j in range(FC)]

    def layer(self, l):
        P = self.P
        last = (l == 3)
        if self.stop == "setup":
            self.dump_c32()
            return
        self.ada(l)
        P.barrier()
        if self.stop == "ada":
            self.dump_c32()
            return
        self.alloc_p12()
        self.dma("sp", self.sggB, self.sgg_d[l].partition_broadcast(128), self.st_ld, [], [self.r_sg], join=True)
        self.dma("sp", self.sgbB, self.sgb_d[l].partition_broadcast(128), self.st_ld, [], [self.r_sg], join=True)
        self.dma("pool", self.sgwT, self.sgwT_d[l], self.st_ld, [], [self.r_sg], join=True)
        blocks = [(0, LC, True)] + [(LC + j * NB, NB, False) for j in range(T // NB)]
        self.reset_state()
        p1b = [blocks[0]] + blocks[:0:-1]
        if self.stop and self.stop.startswith("p1b"):
            p1b = p1b[:int(self.stop[3:])]
        self.pass1_norm(l, p1b[0])
        for bi, blk in enumerate(p1b):
            self.pass1_block(l, blk, p1b[bi + 1] if bi + 1 < len(p1b) else None)
        if self.stop and self.stop.startswith("p1"):
            return
        self.reset_state()
        p2b = blocks
        if self.stop and self.stop.startswith("p2b"):
            p2b = p2b[:int(self.stop[3:])]
        for blk in p2b:
            self.pass2_block(l, blk, last)
        if self.dbg and l == 0:
            for kc in range(KC):
                self.dma("sp", self.dbg1[kc], self.xMs[kc], self.st_x0, self.r_xM, [self.r_dbg], join=True)
        P.barrier()
        if self.stop and self.stop.startswith("p2"):
            return
        self.alloc_p3()
        wins = []
        if not last:
            wins.append((0, LC, 0, LC))
        o = 0
        while o < T:
            n = min(FW, T - o)
            wins.append((LC + o, LC + o + n, LC, U))
            o += n
        for w in wins:
            self.ffn_window(l, w, last)
        if self.dbg and l == 0:
            for kc in range(KC):
                self.dma("sp", self.dbg2[kc], self.xTs[kc], self.st_x0, self.r_xT, [self.r_dbg], join=True)
        P.barrier()

    def dump_c32(self):
        self.P.barrier()
        self.dma("sp", self.dbgs, self.C32t[:, :], self.st_st, [], [self.r_dbg])

    def reset_state(self):
        for h in range(4):
            self.P.op("dve", lambda e, h=h: e.memset(self.state_f[:, h, :], 0.0), [], [self.r_state[h]])
            self.P.op("pool", lambda e, h=h: e.memset(self.state_b[:, h, :], 0.0), [], [self.r_state[h]])

    def load_x(self, u0, n):
        rs = self.r_xT[u0 // 128:(u0 + n + 127) // 128]
        self.dma("sp", self.xsb[:, :, 0:n], self.xTs[:, :, u0:u0 + n].rearrange("k p n -> p k n"), self.st_ld, rs, [self.r_x])
        return rs

    def block_norm1(self, l, blk):
        u0, nb, isctx = blk
        r_ = 1 if isctx else 0
        self.load_x(u0, nb)
        self.norm(self.xsb, self.r_x, nb, self.G1[:, r_, :], self.modT[:, r_, 0:16], [self.r_mod],
                  lambda kc: self.hT[:, kc, 0:nb], [self.r_h], self.ntmp)

    def win_req(self, l, g):
        return ("sp", self.win_s[l][g], [self.r_wsc[l]["in"]], 8192)

    def qk_head(self, l, ps, pr, n, gcol, rope, dst, rdst):
        sq, rsq, tm, rtm, rstd, rrstd = self.ntmp
        kraw, r_kraw = tm[0], rtm[0]
        kn, r_kn = tm[1], rtm[1]
        self.cp("act", kraw[:, 0:n], ps, pr, [r_kraw])
        self.tt("pool", sq[0][:, 0:n], kraw[:, 0:n], kraw[:, 0:n], ALU.mult, [r_kraw], [rsq[0]])
        pn, pnr = self.PS(2, 0, n)
        self.mm(pn, self.ones_f, sq[0][:, 0:n], True, True, [rsq[0], self.r_const], pnr)
        self.rstd_lnexp(sq[1][:, 0:n], pn, pnr, [rsq[1]], 1.0 / 128)
        self.stt(kn[:, 0:n], kraw[:, 0:n], gcol, sq[1][:, 0:n], ALU.mult, ALU.mult, [r_kraw, rsq[1], self.r_par], [r_kn])
        if rope:
            pro, prr = self.PS(3, 256, 256 + n)
            self.mm(pro, self.rotT, kn[:, 0:n], True, True, [r_kn, self.r_const], prr)
            self.tt("dve", self.t1[:, 0:n], kn[:, 0:n], self.rope[:, 0, 0:n], ALU.mult, [r_kn, self.r_rope], [self.r_t1])
            self.tt("dve", self.t2[:, 0:n], pro, self.rope[:, 1, 0:n], ALU.mult, prr + [self.r_rope], [self.r_t2])
            self.tt("pool", dst, self.t1[:, 0:n], self.t2[:, 0:n], ALU.add, [self.r_t1, self.r_t2], rdst)
        else:
            self.cp("act", dst, kn[:, 0:n], [r_kn], rdst)

    def load_rope(self, u0, nb):
        t0 = u0 - LC
        self.dma("sp", self.rope[:, :, 0:nb], self.rope_d[:, :, t0:t0 + nb].rearrange("a p n -> p a n"), self.st_ld,
                 [], [self.r_rope])

    def hgrn_q(self, wq, rwq, nb):
        rhs = lambda kc: self.hT[:, kc, 0:nb]
        for hd in range(4):
            ps, pr = self.proj_fm(wq, rwq, KC, hd * 128, rhs, [self.r_h], nb, hd % 2)
            self.act(self.qS[:, hd, 0:nb], ps, AF.Silu, pr, [self.r_qS[hd]])

    def hgrn_f(self, l, d, wf, rwf, nb):
        rhs = lambda kc: self.hT[:, kc, 0:nb]
        for hd in range(4):
            ps, pr = self.proj_fm(wf, rwf, KC, hd * 128, rhs, [self.r_h], nb, hd % 2)
            self.act(self.sgm[:, 0:nb], ps, AF.Sigmoid, pr, [self.r_sgm])
            self.ts("dve", self.fS[:, 0:nb], self.sgm[:, 0:nb], self.oml[:, d, l, hd:hd + 1], self.lb[:, d, l, hd:hd + 1],
                    ALU.mult, ALU.add, [self.r_sgm, self.r_lb], [self.r_fS])
            self.ts("pool", self.fS[:, 0:nb], self.fS[:, 0:nb], F_MIN, None, ALU.max, None, [self.r_fS], [self.r_fS])
            self.act(self.lfS[:, hd, 0:nb], self.fS[:, 0:nb], AF.Ln, [self.r_fS], [self.r_lfS[hd]])
            self.ts("pool", self.kkS[:, hd, 0:nb], self.fS[:, 0:nb], -1.0, 1.0, ALU.mult, ALU.add, [self.r_fS], [self.r_kkS[hd]])

    def hi_prep(self, wt, rw, nb):
        for ti in range(nb // 128):
            ps, pr = self.proj_tm(wt, rw, 0, 512, self.hT, self.r_h, ti * 128, ti % 2)
            self.cp("act", self.vS[:, ti, :], ps, pr, [self.r_vS[ti]])

    def rstd_lnexp(self, out, in_, rd, wr, n_inv):
        self.act(out, in_, AF.Ln, rd, wr, bias=EPS, scale=n_inv)
        self.act(out, out, AF.Exp, wr, wr, scale=-0.5)

    def hgrn_front1(self, l, d, hd, ti, it):
        H = self.hgs[it % 2]
        c0 = ti * 128
        lf = self.lfS[:, hd, c0:c0 + 128]; rlf = self.r_lfS[hd]
        rc = self.r_const
        for i, nm in enumerate(["P32", "P64", "P128"]):
            ap, r = H[nm]
            self.P.op("dve", lambda e, ap=ap, i=i: e.tensor_tensor_scan(out=ap, data0=self.seg[:, i, :], data1=lf, initial=0.0,
                                                                       op0=ALU.mult, op1=ALU.add), [rlf, rc], [r])
        if d == 1:
            Us = []
            for nm, pn in [("U32", "P32"), ("U64", "P64"), ("U128", "P128")]:
                ap, r = H[nm]
                self.tt("pool", ap, H[pn][0], lf, ALU.subtract, [H[pn][1], rlf], [r])
                Us.append((ap, r))
        else:
            Us = [H["P32"], H["P64"], H["P128"]]
        (U32, rU32), (U64, rU64), (U128, rU128) = Us
        P32, rP32 = H["P32"]; P64, rP64 = H["P64"]
        mpos = 15 if d == 0 else 16
        U32v = U32.rearrange("p (a b) -> p a b", a=4)
        EA, rEA = H["EA"]
        self.tt("dve", EA.rearrange("p (a b) -> p a b", a=4), U32v, U32v[:, :, mpos:mpos + 1].to_broadcast([128, 4, 32]),
                ALU.subtract, [rU32], [rEA])
        nW32, rnW32 = H["nW32"]
        nW64, rnW64 = H["nW64"]
        P32v = P32.rearrange("p (a b) -> p a b", a=4)
        P64v = P64.rearrange("p (a b) -> p a b", a=2)
        self.tt("dve", nW32.rearrange("p (a b) -> p a b", a=4), U32v, P32v[:, :, 31:32].to_broadcast([128, 4, 32]),
                ALU.subtract, [rU32, rP32], [rnW32])
        self.tt("pool", nW64.rearrange("p (a b) -> p a b", a=2), U64.rearrange("p (a b) -> p a b", a=2),
                P64v[:, :, 63:64].to_broadcast([128, 2, 64]), ALU.subtract, [rU64, rP64], [rnW64])

    def hgrn_front2(self, l, d, hd, ti, it):
        H = self.hgs[it % 2]
        if d == 1:
            (U32, rU32), (U64, rU64), (U128, rU128) = H["U32"], H["U64"], H["U128"]
        else:
            (U32, rU32), (U64, rU64), (U128, rU128) = H["P32"], H["P64"], H["P128"]
        P128, rP128 = H["P128"]
        EA, rEA = H["EA"]; nW32, rnW32 = H["nW32"]; nW64, rnW64 = H["nW64"]
        tot = P128[:, 127:128]
        ex = [("XA", EA, rEA, 1.0, None), ("YA", EA, rEA, -1.0, None),
              ("XB", U32, rU32, 1.0, None), ("YB", nW32, rnW32, -1.0, None),
              ("XC", U64, rU64, 1.0, None), ("YC", nW64, rnW64, -1.0, None),
              ("XD", U128, rU128, 1.0, None), ("YD", U128, rU128, -1.0, tot)]
        for nm, src, rs, sc, b_ in ex:
            ap, r = H[nm]
            if b_ is None:
                self.act(ap, src, AF.Exp, [rs], [r], scale=sc)
            else:
                self.act(ap, src, AF.Exp, [rs, rP128], [r], scale=sc, bias=b_)
        self.act(self.dec[:, it % 8:it % 8 + 1], tot, AF.Exp, [rP128], [self.r_dec[it % 8]])

    def hgrn_front3(self, l, d, hd, ti, it):
        H = self.hgs[it % 2]
        c0 = ti * 128
        q = self.qS[:, hd, c0:c0 + 128]; rq = self.r_qS[hd]
        kk = self.kkS[:, hd, c0:c0 + 128]; rkk = self.r_kkS[hd]
        O = self.hop[it % 4]
        engs = ["dve", "pool"]
        k = 0
        for lv in "ABCD":
            eq, ek = ("X" + lv, "Y" + lv) if d == 0 else ("Y" + lv, "X" + lv)
            self.tt(engs[k % 2], O["q" + lv][0], q, H[eq][0], ALU.mult, [rq, H[eq][1]], [O["q" + lv][1]]); k += 1
            self.tt(engs[k % 2], O["k" + lv][0], kk, H[ek][0], ALU.mult, [rkk, H[ek][1]], [O["k" + lv][1]]); k += 1

    def hgrn_front(self, l, d, hd, ti, it):
        self.hgrn_front1(l, d, hd, ti, it)
        self.hgrn_front2(l, d, hd, ti, it)
        self.hgrn_front3(l, d, hd, ti, it)

    def hgrn_back1(self, d, hd, ti, it):
        H = self.hg
        O = self.hop[it % 4]
        rc = self.r_const
        slots = [(0, 0), (0, 128), (0, 256)]
        sps = []
        for i, lv in enumerate("ABC"):
            ps, pr = self.PS(slots[i][0], slots[i][1], slots[i][1] + 128)
            self.mm(ps, O["k" + lv][0], O["q" + lv][0], True, True, [O["k" + lv][1], O["q" + lv][1]], pr)
            sps.append((ps, pr))
        slot7 = it % 2
        pt = self.pb7[:, slot7 * 128:(slot7 + 1) * 128]
        ptr = [self.p7res[slot7]]
        self.P.op("pe", lambda e: e.transpose(pt, O["kD"][0], self.ident), [O["kD"][1], rc], ptr)
        Sf, rSf = H["Sf"]; tB, rtB = H["tB"]; tC, rtC = H["tC"]
        self.tt("dve", Sf, sps[0][0], self.masks[:, d * 3 + 0, :], ALU.mult, sps[0][1] + [rc], [rSf])
        self.tt("dve", tB, sps[1][0], self.masks[:, d * 3 + 1, :], ALU.mult, sps[1][1] + [rc], [rtB])
        self.tt("dve", tC, sps[2][0], self.masks[:, d * 3 + 2, :], ALU.mult, sps[2][1] + [rc], [rtC])
        self.tt("pool", Sf, Sf, tB, ALU.add, [rSf, rtB], [rSf])
        Sb, rSb = O["Sb"]
        self.tt("pool", Sb, Sf, tC, ALU.add, [rSf, rtC], [rSb])
        kDt, rkDt = O["kDt"]
        self.cp("dve", kDt, pt, ptr, [rkDt])

    def hgrn_back2(self, d, hd, ti, it):
        O = self.hop[it % 4]
        v = self.vS[:, ti, hd * 128:(hd + 1) * 128]; rv = self.r_vS[ti]
        rst = self.r_state[hd]
        po, por = self.PS(1, 0, 128)
        self.mm(po, v, O["Sb"][0], True, False, [rv, O["Sb"][1]], por)
        self.mm(po, self.state_b[:, hd, :], O["qD"][0], False, True, [rst, O["qD"][1]], por)
        pst, pstr = self.PS(1, 128, 256)
        self.mm(pst, O["kDt"][0], v, True, True, [O["kDt"][1], rv], pstr)
        self.stt(self.state_f[:, hd, :], self.state_f[:, hd, :], self.dec[:, it % 8:it % 8 + 1], pst, ALU.mult, ALU.add,
                 pstr + [rst, self.r_dec[it % 8]], [rst])
        self.cp("pool", self.state_b[:, hd, :], self.state_f[:, hd, :], [rst], [rst])
        return po, por

    def pass1_norm(self, l, blk):
        u0, nb, isctx = blk
        self.block_norm1(l, blk)
        if not isctx:
            self.load_rope(u0, nb)
        rs = self.r_hTd[u0 // 128:(u0 + nb) // 128]
        self.dma("act", self.hTd[:, :, u0:u0 + nb].rearrange("k p n -> p k n"), self.hT[:, :, 0:nb], self.st_h, [self.r_h], rs)

    def pass1_block(self, l, blk, nxt_blk):
        u0, nb, isctx = blk
        rhs = lambda kc: self.hT[:, kc, 0:nb]
        reqs = [self.win_req(l, g) for g in (2, 3, 5, 6)]
        tiles = []
        for (buf, r) in self.wstream(reqs):
            tiles.append((buf[:, 0:8192].rearrange("p (k j) -> p k j", k=KC), r))
            gi = len(tiles) - 1
            wt, rw = tiles[gi]
            if gi == 0:
                for kv in range(2):
                    ps, pr = self.proj_fm(wt, rw, KC, kv * 128, rhs, [self.r_h], nb, kv % 2)
                    self.qk_head(l, ps, pr, nb, self.qkg[:, l, 1:2], not isctx, self.KT[:, kv, u0:u0 + nb], [self.r_KT])
                for ti in range(nb // 128):
                    ps, pr = self.proj_tm(wt, rw, 256, 256, self.hT, self.r_h, ti * 128, ti % 2)
                    self.cp("act", self.V[:, u0 // 128 + ti, :], ps, pr, [self.r_V])
            elif gi == 1:
                self.hgrn_q(wt, rw, nb)
            elif gi == 2:
                self.hgrn_f(l, 1, wt, rw, nb)
            elif gi == 3:
                self.hi_prep(wt, rw, nb)
        if nxt_blk is not None:
            self.pass1_norm(l, nxt_blk)
        its = [(ti, hd) for ti in reversed(range(nb // 128)) for hd in range(4)]
        n_it = len(its)
        for s_ in range(n_it + 2):
            if 0 <= s_ - 2 < n_it:
                ti, hd = its[s_ - 2]
                po, por = self.hgrn_back2(1, hd, ti, self.itc + s_ - 2)
                self.cp("act", self.obst[:, hd, :], po, por, [self.r_obst])
                if hd == 3:
                    gt = u0 // 128 + ti
                    self.dma("act", self.obT[:, :, gt * 128:(gt + 1) * 128].rearrange("h p n -> p h n"), self.obst, self.st_st,
                             [self.r_obst], [self.r_ob[gt]])
            if 0 <= s_ - 1 < n_it:
                ti, hd = its[s_ - 1]
                self.hgrn_back1(1, hd, ti, self.itc + s_ - 1)
            if s_ < n_it:
                ti, hd = its[s_]
                self.hgrn_front(l, 1, hd, ti, self.itc + s_)
        self.itc += n_it

    def pass2_block(self, l, blk, last):
        u0, nb, isctx = blk
        skip_out = last and isctx
        ntile = nb // 128
        if not skip_out:
            self.load_x(u0, nb)
        rs_h = self.r_hTd[u0 // 128:(u0 + nb) // 128]
        self.dma("sp", self.hT[:, :, 0:nb], self.hTd[:, :, u0:u0 + nb].rearrange("k p n -> p k n"), self.st_ld, rs_h, [self.r_h])
        if not isctx:
            self.load_rope(u0, nb)
        rhs = lambda kc: self.hT[:, kc, 0:nb]
        gs = [3, 4, 6] if skip_out else [3, 4, 6, 7, 0, 1, 8, 9]
        reqs = [self.win_req(l, g) for g in gs]
        if not skip_out:
            reqs += [("sp", self.wout_s[l][g], [self.r_wsc[l]["out"]], 8192) for g in range(4)]
        ws = self.wstream(reqs)

        def nxt():
            buf, r = next(ws)
            return buf[:, 0:8192].rearrange("p (k j) -> p k j", k=KC), r

        wq, rwq = nxt()
        self.hgrn_q(wq, rwq, nb)
        wf, rwf = nxt()
        self.hgrn_f(l, 0, wf, rwf, nb)
        wt, rw = nxt()
        self.hi_prep(wt, rw, nb)
        oblds = [(self.obld, self.r_obld), (self.obst, self.r_obst)]
        if not skip_out:
            wt, rw = nxt()
            for hd in range(4):
                ps, pr = self.proj_fm(wt, rw, KC, hd * 128, rhs, [self.r_h], nb, hd % 2)
                self.act(self.gateS[:, hd, 0:nb], ps, AF.Silu, pr, [self.r_gateS[hd]])
            for ti in range(ntile):
                gt = u0 // 128 + ti
                self.dma("sp", oblds[ti][0], self.obT[:, :, gt * 128:(gt + 1) * 128].rearrange("h p n -> p h n"), self.st_ld,
                         [self.r_ob[gt]], [oblds[ti][1]])
            for g in (0, 1):
                wt, rw = nxt()
                for hh in range(4):
                    h = g * 4 + hh
                    ps, pr = self.proj_fm(wt, rw, KC, hh * 128, rhs, [self.r_h], nb, hh % 2)
                    self.qk_head(l, ps, pr, nb, self.qkg[:, l, 0:1], not isctx, self.QT[:, h, 0:nb], [self.r_QT[h]])
        H = self.hg
        its = [(ti, hd) for ti in range(ntile) for hd in range(4)]
        n_it = len(its)
        for s_ in range(n_it + 3):
            if s_ < n_it:
                ti, hd = its[s_]
                self.hgrn_front1(l, 0, hd, ti, self.itc + s_)
            if not skip_out and s_ < 8:
                self.attention_head(l, blk, s_)
            if s_ < n_it:
                ti, hd = its[s_]
                self.hgrn_front2(l, 0, hd, ti, self.itc + s_)
                self.hgrn_front3(l, 0, hd, ti, self.itc + s_)
            if 0 <= s_ - 1 < n_it:
                ti, hd = its[s_ - 1]
                self.hgrn_back1(0, hd, ti, self.itc + s_ - 1)
            if not skip_out and 0 <= s_ - 3 < n_it:
                ti, hd = its[s_ - 3]
                c0 = ti * 128
                oS, roS = self.hgo[(s_ - 3) % 2]["oS"]; sqo, rsqo = self.hgo[(s_ - 3) % 2]["sqo"]; rso, rrso = H["rso"]
                pn, pnr = self.PS(6, 256, 384)
                self.mm(pn, self.ones_f, sqo, True, True, [rsqo, self.r_const], pnr)
                self.rstd_lnexp(rso, pn, pnr, [rrso], 1.0 / 128)
                self.stt(oS, oS, self.hgg[:, l:l + 1], rso, ALU.mult, ALU.mult, [roS, rrso, self.r_par], [roS])
                self.tt("pool", self.mixT[:, 8 + hd, c0:c0 + 128], oS, self.gateS[:, hd, c0:c0 + 128], ALU.mult,
                        [roS, self.r_gateS[hd]], [self.r_mix[8 + hd]])
            if 0 <= s_ - 2 < n_it:
                ti, hd = its[s_ - 2]
                po, por = self.hgrn_back2(0, hd, ti, self.itc + s_ - 2)
                if not skip_out:
                    oS, roS = self.hgo[(s_ - 2) % 2]["oS"]; sqo, rsqo = self.hgo[(s_ - 2) % 2]["sqo"]
                    self.tt("dve", oS, po, oblds[ti][0][:, hd, :], ALU.add, por + [oblds[ti][1]], [roS])
                    self.tt("pool", sqo, oS, oS, ALU.mult, [roS], [rsqo])
        self.itc += n_it
        if skip_out:
            return
        wt, rw = nxt()
        for g in range(4):
            ps, pr = self.proj_fm(wt, rw, KC, g * 128, rhs, [self.r_h], nb, g % 2)
            self.act(self.uS[:, g, 0:nb], ps, AF.Gelu_apprx_tanh, pr, [self.r_uS[g]])
        wsv, rwsv = nxt()
        for ti in range(ntile):
            c0 = ti * 128
            ps, pr = self.proj_tm(wsv, rwsv, 0, 512, self.hT, self.r_h, c0, ti % 2)
            self.act(self.gv, ps, AF.Gelu_apprx_tanh, pr, [self.r_gv])
            self.tt("pool", self.sqv, self.gv, self.gv, ALU.mult, [self.r_gv], [self.r_sqv])
            self.P.op("dve", lambda e: e.tensor_reduce(out=self.ssum, in_=self.sqv.rearrange("p (g d) -> p g d", g=4), axis=AX.X,
                                                       op=ALU.add), [self.r_sqv], [self.r_ssum])
            self.rstd_lnexp(self.ssum, self.ssum, [self.r_ssum], [self.r_ssum], 1.0 / 128)
            self.tt("dve", self.vn.rearrange("p (g d) -> p g d", g=4), self.gv.rearrange("p (g d) -> p g d", g=4),
                    self.ssum.unsqueeze(2).to_broadcast([128, 4, 128]), ALU.mult, [self.r_gv, self.r_ssum], [self.r_vn])
            self.tt("pool", self.vnb, self.vn, self.sggB, ALU.mult, [self.r_vn, self.r_sg], [self.r_vnb])
            for g in range(4):
                pg, pgr = self.PS(3 + (g % 2), 256, 384)
                self.mm(pg, self.vnb[:, g * 128:(g + 1) * 128], self.sgwT[:, g, :], True, True, [self.r_vnb, self.r_sg], pgr)
                self.tt("dve", self.tsg, pg, self.sgbB[:, g * 128:(g + 1) * 128], ALU.add, pgr + [self.r_sg], [self.r_tsg])
                self.tt("pool", self.mixT[:, 12 + g, c0:c0 + 128], self.tsg, self.uS[:, g, c0:c0 + 128], ALU.mult,
                        [self.r_tsg, self.r_uS[g]], [self.r_mix[12 + g]])
        if self.dbg and l == 0:
            self.dma("pool", self.dbgm[:, :, u0:u0 + nb].rearrange("k p n -> p k n"), self.mixT[:, :, 0:nb], self.st_st,
                     self.r_mix, [self.r_dbg], join=True)
        r_ = 1 if isctx else 0
        mrhs = lambda kc: self.mixT[:, kc, 0:nb]
        for g in range(4):
            wt, rw = nxt()
            for mm_ in range(4):
                m = g * 4 + mm_
                ps, pr = self.proj_fm(wt, rw, KC, mm_ * 128, mrhs, self.r_mix, nb, mm_ % 2)
                self.stt(self.xsb[:, m, 0:nb], ps, self.modT[:, r_, 32 + m:33 + m], self.xsb[:, m, 0:nb], ALU.mult, ALU.add,
                         pr + [self.r_x, self.r_mod], [self.r_x])
        rs = self.r_xM[u0 // 128:(u0 + nb) // 128]
        self.dma("act", self.xMs[:, :, u0:u0 + nb].rearrange("k p n -> p k n"), self.xsb[:, :, 0:nb], self.st_st, [self.r_x], rs)

    def attention_head(self, l, blk, h):
        u0, nb, isctx = blk
        keyt = [0, 1] if isctx else list(range(U // 128))
        scale = 128 ** -0.5
        kv = h // 4
        n = len(keyt)
        LA = 2
        banks = [2, 3, 4]
        po, por = self.PS(5, 0, nb)
        pd, pdr = self.PS(6, 0, nb)
        sts = {}
        for ji in range(n + LA):
            if ji < n:
                j = keyt[ji]
                cnt = self.acnt
                self.acnt += 1
                ps, pr = self.PS(banks[cnt % 3], 0, nb)
                self.mm(ps, self.KT[:, kv, j * 128:(j + 1) * 128], self.QT[:, h, 0:nb], True, True, [self.r_KT, self.r_QT[h]], pr)
                pT = self.pT[cnt % 4][:, 0:nb]
                rpT = self.r_pT[cnt % 4]
                self.act(pT, ps, AF.Exp, pr, [rpT], scale=scale)
                sts[ji] = (pT, rpT)
            jv = ji - LA
            if jv >= 0:
                pT, rpT = sts.pop(jv)
                j = keyt[jv]
                self.mm(po, self.V[:, j, kv * 128:(kv + 1) * 128], pT, jv == 0, jv == n - 1, [self.r_V, rpT], por)
                self.mm(pd, self.ones_b, pT, jv == 0, jv == n - 1, [self.r_const, rpT], pdr)
        self.P.op("dve", lambda e, pd=pd: e.reciprocal(out=self.rD[:, 0:nb], in_=pd), pdr, [self.r_rD])
        self.tt("dve", self.mixT[:, h, 0:nb], po, self.rD[:, 0:nb], ALU.mult, por + [self.r_rD], [self.r_mix[h]])

    def ffn_window(self, l, win, last):
        o0, o1, s0, s1 = win
        isctx = (s0 == 0)
        r_ = 1 if isctx else 0
        c0 = max(o0 - 1, s0)
        c1 = min(o1 + 1, s1)
        n = c1 - c0
        a = o0 - c0
        b = o1 - c0
        no = o1 - o0
        rxs = self.r_xM[c0 // 128:(c1 + 127) // 128]
        self.dma("sp", self.xsb[:, :, 0:n], self.xMs[:, :, c0:c1].rearrange("k p n -> p k n"), self.st_ld, rxs, [self.r_x])
        self.norm(self.xsb, self.r_x, n, self.G2[:, r_, :], self.modT[:, r_, 48:64], [self.r_mod],
                  lambda kc: self.hT[:, kc, 0:n], [self.r_h], self.ntmp)
        rhs = lambda kc: self.hT[:, kc, 0:n]
        reqs = [("sp", self.wup_s[l][t], [self.r_wsc[l]["up"]], 8192) for t in range(22)]
        reqs += [("sp", self.wdn_s[l][m], [self.r_wsc[l]["dn"]], FC * 128) for m in range(16)]
        ws = self.wstream(reqs)
        cw = self.convw
        cb = self.convb
        rp = [self.r_par]
        for t in range(22):
            buf, rw = next(ws)
            wt = buf[:, 0:8192].rearrange("p (k j) -> p k j", k=KC)
            for i in range(2):
                j = 2 * t + i
                pg, pgr = self.proj_fm(wt, rw, KC, i * 128, rhs, [self.r_h], n, 0)
                pv, pvr = self.proj_fm(wt, rw, KC, 256 + i * 128, rhs, [self.r_h], n, 1)
                bi = j % 2
                for (ps, pr, acc, racc, fc) in [(pg, pgr, self.accg[bi], self.r_accg[bi], j), (pv, pvr, self.accv[bi], self.r_accv[bi], FC + j)]:
                    self.act(acc[:, a:b], ps[:, a:b], AF.Identity, pr + rp, [racc], bias=cb[:, l, fc:fc + 1], scale=cw[:, l, 1, fc:fc + 1])
                    lo = max(a, 1)
                    self.stt(acc[:, lo:b], ps[:, lo - 1:b - 1], cw[:, l, 0, fc:fc + 1], acc[:, lo:b], ALU.mult, ALU.add, pr + rp + [racc], [racc])
                    hi = min(b, n - 1)
                    self.stt(acc[:, a:hi], ps[:, a + 1:hi + 1], cw[:, l, 2, fc:fc + 1], acc[:, a:hi], ALU.mult, ALU.add, pr + rp + [racc], [racc])
                self.act(self.sil[bi][:, a:b], self.accg[bi][:, a:b], AF.Silu, [self.r_accg[bi]], [self.r_sil[bi]])
                self.tt("pool", self.aT[:, j, 0:no], self.sil[bi][:, a:b], self.accv[bi][:, a:b], ALU.mult,
                        [self.r_sil[bi], self.r_accv[bi]], [self.r_aT[j]])
        for m in range(16):
            buf, rw = next(ws)
            wt = buf[:, 0:FC * 128].rearrange("p (k j) -> p k j", k=FC)
            ps, pr = self.PS(m % 2, 0, no)
            for j in range(FC):
                self.mm(ps, wt[:, j, :], self.aT[:, j, 0:no], j == 0, j == FC - 1, [rw, self.r_aT[j]], pr)
            self.stt(self.xsb[:, m, a:b], ps, self.modT[:, r_, 80 + m:81 + m], self.xsb[:, m, a:b], ALU.mult, ALU.add,
                     pr + [self.r_x, self.r_mod], [self.r_x])
        ros = self.r_xT[o0 // 128:(o1 + 127) // 128]
        if not last:
            self.dma("act", self.xTs[:, :, o0:o1].rearrange("k p n -> p k n"), self.xsb[:, :, a:b], self.st_st, [self.r_x], ros)
        else:
            sq, rsq, tm, rtm, rstd, rrstd = self.ntmp
            pn, pnr = self.PS(2, 0, no)
            for kc in range(KC):
                i = kc % 2
                self.act(sq[i][:, 0:no], self.xsb[:, kc, a:b], AF.Square, [self.r_x], [rsq[i]])
                self.mm(pn, self.ones_f, sq[i][:, 0:no], kc == 0, kc == KC - 1, [rsq[i], self.r_const], pnr)
            self.act(rstd[:, 0:no], pn, AF.Sqrt, pnr, [rrstd], bias=EPS, scale=1.0 / D)
            self.P.op("dve", lambda e: e.reciprocal(out=rstd[:, 0:no], in_=rstd[:, 0:no]), [rrstd], [rrstd])
            for kc in range(KC):
                self.stt(self.xsb[:, kc, a:b], self.xsb[:, kc, a:b], self.fng[:, kc:kc + 1], rstd[:, 0:no], ALU.mult, ALU.mult,
                         [self.r_x, rrstd, self.r_par], [self.r_x])
            self.dma("act", self.outT[:, :, o0 - LC:o1 - LC].rearrange("k p n -> p k n"), self.xsb[:, :, a:b], self.st_st,
                     [self.r_x], [self.r_out], join=True)


def _consts():
    n_freq = 32
    inv = (np.float32(10000.0) ** (-np.arange(n_freq, dtype=np.float32) / np.float32(n_freq))).astype(np.float32)
    t = np.arange(T)
    row = (t // 64).astype(np.float32)
    col = (t % 64).astype(np.float32)
    ang = np.concatenate([row[:, None] * inv[None, :], col[:, None] * inv[None, :]], axis=-1).astype(np.float32)
    cos = np.cos(ang).astype(np.float32)
    sin = np.sin(ang).astype(np.float32)
    ropeT = np.stack([np.repeat(cos, 2, axis=1).T, np.repeat(sin, 2, axis=1).T]).astype(np.float32)
    rotT = np.zeros((128, 128), np.float32)
    for i in range(64):
        rotT[2 * i + 1, 2 * i] = -1.0
        rotT[2 * i, 2 * i + 1] = 1.0
    ident = np.eye(128).astype(ml_dtypes.bfloat16)
    s = np.arange(128)[:, None]
    tt = np.arange(128)[None, :]
    mA = ((s // 32 == tt // 32) & (s <= tt))
    mB = ((s // 64 == tt // 64) & (s % 64 < 32) & (tt % 64 >= 32))
    mC = ((s < 64) & (tt >= 64))
    masks = np.stack([mA, mB, mC, mA.T, mB.T, mC.T], axis=1).astype(np.float32)
    seg = np.ones((128, 3, 128), np.float32)
    seg[:, 0, ::32] = 0
    seg[:, 1, ::64] = 0
    seg[:, 2, 0] = 0
    return dict(ropeT=np.ascontiguousarray(ropeT), rotT=rotT, ident=ident, identf=np.eye(128, dtype=np.float32), masks=np.ascontiguousarray(masks), seg=seg)


def _layout(inputs, b, NL=4):
    f = lambda a: np.ascontiguousarray(a, dtype=np.float32)
    x = inputs["x"][b]
    ctx = inputs["ctx"][b]
    xc = np.concatenate([ctx, x], axis=0)
    m = {}
    m["xT0"] = f(xc.T.reshape(KC, 128, U))
    cc = np.stack([inputs["c"][b], inputs["c_ctx"]], axis=-1)
    m["cT"] = f(cc.reshape(KC, 128, 2).transpose(1, 0, 2))
    for k in ("w_ada", "w_in", "w_out", "w_up", "w_down"):
        m[k] = f(inputs[k][:NL])
    m["badaT"] = f(inputs["b_ada"].reshape(4, 96, 128).transpose(2, 0, 1))
    m["n1g"] = f(inputs["norm1_g"].reshape(4, KC, 128).transpose(2, 0, 1))
    m["n2g"] = f(inputs["norm2_g"].reshape(4, KC, 128).transpose(2, 0, 1))
    m["fng"] = f(inputs["final_norm_g"].reshape(KC, 128).T)
    m["qkg"] = f(np.stack([inputs["q_norm_g"], inputs["k_norm_g"]], axis=-1).transpose(1, 0, 2))
    m["hgg"] = f(inputs["hg_norm_g"].T)
    m["lbp"] = f(inputs["hg_lower_bounds"].reshape(2, 4, 4, 128).transpose(3, 0, 1, 2))
    m["sgg"] = f(inputs["sg_norm_g"].reshape(4, 1, 512))
    m["sgb"] = f(inputs["sg_b"].reshape(4, 1, 512))
    m["sgwT"] = np.ascontiguousarray(inputs["sg_w"].transpose(0, 3, 1, 2), dtype=np.float32)
    m["convw"] = f(inputs["conv_w"].reshape(4, 3, 88, 128).transpose(3, 0, 1, 2))
    m["convb"] = f(inputs["conv_b"].reshape(4, 88, 128).transpose(2, 0, 1))
    return m


_NC_CACHE = {}


def kernel(**inputs):
    inputs = {k: np.asarray(v) for k, v in inputs.items()}
    if "nc" not in _NC_CACHE:
        _NC_CACHE["nc"] = Builder(4, False).build()
    nc = _NC_CACHE["nc"]
    consts = _consts()
    in_maps = []
    for b in range(N_CORES):
        m = _layout(inputs, b)
        m.update(consts)
        in_maps.append(m)
    res = run_bass_kernel_spmd(nc, in_maps, core_ids=list(range(N_CORES)))
    out = np.empty((4, T, D), np.float32)
    for b in range(4):
        out[b] = res.results[b]["outT"].reshape(D, T).T
    return out
```

```python
import contextlib
import numpy as np
import ml_dtypes
import concourse.bass as bass
import concourse.mybir as mybir
from concourse.bass_utils import run_bass_kernel_spmd

F32 = mybir.dt.float32
BF16 = mybir.dt.bfloat16
AF = mybir.ActivationFunctionType
ALU = mybir.AluOpType
AX = mybir.AxisListType

import os
SEM_LIMIT = int(os.environ.get("SEM_LIMIT", "16000"))
D = 2048
KC = 16
T = 4096
LC = 256
U = T + LC
DFF = 5632
FC = 44
EPS = 1e-6
F_MIN = 1e-30
NB = 256
FW = 456
N_CORES = 4


class Res:
    __slots__ = ("name", "writers", "readers", "excl")

    def __init__(self, name, excl=False):
        self.name = name
        self.writers = {}
        self.readers = {}
        self.excl = excl


class Stream:
    def __init__(self, name, nslots):
        self.name = name
        self.slots = [[None, 0] for _ in range(nslots)]
        self.i = 0


def _merge(d, s, v):
    if d.get(s, 0) < v:
        d[s] = v


class Prog:
    ENG = ("pe", "act", "dve", "pool", "sp")

    def __init__(self, nc, stack):
        self.nc = nc
        self.stack = stack
        self.ops = {e: [] for e in self.ENG}
        self.esem = {e: None for e in self.ENG}
        self.ecnt = {e: 0 for e in self.ENG}
        self.known = {e: {} for e in self.ENG}
        self.nsem = 0
        self.nres = 0
        self.esems = {e: set() for e in self.ENG}
        self.streams = []
        self.nops = 0

    def new_sem(self, name):
        self.nsem += 1
        return self.stack.enter_context(self.nc.semaphore(f"{name}_{self.nsem}"))

    def res(self, name=None, excl=False):
        self.nres += 1
        return Res(name or f"r{self.nres}", excl)

    def stream(self, name, nslots):
        st = Stream(name, nslots)
        self.streams.append(st)
        return st

    def sb(self, name, shape, dtype):
        return self.stack.enter_context(self.nc.sbuf_tensor(name, list(shape), dtype))

    def ps(self, name, shape, dtype=F32):
        return self.stack.enter_context(self.nc.psum_tensor(name, list(shape), dtype))

    def _waits(self, eng, deps):
        waits = {}
        kn = self.known[eng]
        own = self.esems[eng]
        for (s, v) in deps:
            if kn.get(s, 0) >= v:
                continue
            if eng == "pe" and s in own:
                continue
            _merge(waits, s, v)
        for s, v in waits.items():
            kn[s] = v
        return list(waits.items())

    def _deps(self, reads, writes, join):
        deps = []
        for r in reads:
            deps.extend(r.writers.items())
        for r in writes:
            if join and not r.readers:
                continue
            deps.extend(r.writers.items())
            deps.extend(r.readers.items())
        return deps

    def _post(self, ev, reads, writes, join):
        s, v = ev
        for r in reads:
            _merge(r.readers, s, v)
        for r in writes:
            if join and not r.readers:
                _merge(r.writers, s, v)
            else:
                r.writers = {s: v}
                r.readers = {}

    def op(self, eng, fn, reads=(), writes=()):
        ex = [r for r in reads if r.excl]
        if ex:
            reads = [r for r in reads if not r.excl]
            writes = list(writes) + ex
        waits = self._waits(eng, self._deps(reads, writes, False))
        if self.esem[eng] is None or self.ecnt[eng] >= SEM_LIMIT:
            self.esem[eng] = self.new_sem(eng)
            self.esems[eng].add(self.esem[eng])
            self.ecnt[eng] = 0
        self.ecnt[eng] += 1
        ev = (self.esem[eng], self.ecnt[eng])
        self.ops[eng].append((waits, fn, self.esem[eng], 1))
        self._post(ev, reads, writes, False)
        self.nops += 1
        return ev

    def dma(self, eng, fn, stream, reads=(), writes=(), join=False):
        slot = stream.slots[stream.i]
        stream.i = (stream.i + 1) % len(stream.slots)
        deps = self._deps(reads, writes, join)
        if slot[0] is not None:
            deps.append((slot[0], slot[1]))
        waits = self._waits(eng, deps)
        if slot[0] is None or slot[1] + 16 > SEM_LIMIT:
            slot[0] = self.new_sem("d" + stream.name)
            slot[1] = 0
        slot[1] += 16
        ev = (slot[0], slot[1])
        self.ops[eng].append((waits, fn, slot[0], 16))
        self._post(ev, reads, writes, join)
        self.nops += 1
        return ev

    def all_events(self):
        evs = []
        for e in self.ENG:
            if self.esem[e] is not None:
                evs.append((self.esem[e], self.ecnt[e]))
        for st in self.streams:
            for sl in st.slots:
                if sl[0] is not None:
                    evs.append((sl[0], sl[1]))
        return evs

    def barrier(self, engines=None):
        evs = self.all_events()
        for e in (engines or self.ENG):
            w = self._waits(e, evs)
            if w:
                self.ops[e].append((w, None, None, 0))

    def emit(self):
        nc = self.nc
        block = self.stack.enter_context(nc.Block())
        ops = self.ops

        def run(e, name):
            for (waits, fn, sem, inc) in ops[name]:
                for (s, v) in waits:
                    e.wait_ge(s, v)
                if fn is not None:
                    fn(e).then_inc(sem, inc)

        @block.tensor
        def _(e):
            run(e, "pe")

        @block.scalar
        def _(e):
            run(e, "act")

        @block.vector
        def _(e):
            run(e, "dve")

        @block.gpsimd
        def _(e):
            run(e, "pool")

        @block.sync
        def _(e):
            run(e, "sp")


class Arena:
    def __init__(self, tensor, n):
        self.t = tensor
        self.n = n
        self.off = 0

    def reset(self):
        self.off = 0

    def take(self, *shape):
        n = int(np.prod(shape))
        assert self.off + n <= self.n, (self.off, n, self.n)
        ap = self.t[:, self.off:self.off + n]
        self.off += n
        if len(shape) == 2:
            ap = ap.rearrange("p (a b) -> p a b", a=shape[0])
        elif len(shape) == 3:
            ap = ap.rearrange("p (a b c) -> p a b c", a=shape[0], b=shape[1])
        return ap


class Builder:
    def __init__(self, NL=4, dbg=False, stop=None):
        self.NL = NL
        self.dbg = dbg
        self.stop = stop

    def mm(self, out, lhsT, rhs, st, sp, rd, wr):
        self.P.op("pe", lambda e: e.matmul(out, lhsT=lhsT, rhs=rhs, start=st, stop=sp), rd, wr)

    def act(self, out, in_, func, rd, wr, bias=None, scale=None):
        kw = {}
        if bias is not None:
            kw["bias"] = bias
        if scale is not None:
            kw["scale"] = scale
        self.P.op("act", lambda e: e.activation(out=out, in_=in_, func=func, **kw), rd, wr)

    def tt(self, eng, out, in0, in1, op, rd, wr):
        self.P.op(eng, lambda e: e.tensor_tensor(out=out, in0=in0, in1=in1, op=op), rd, wr)

    def ts(self, eng, out, in0, s1, s2, op0, op1, rd, wr):
        if op1 is None:
            self.P.op(eng, lambda e: e.tensor_scalar(out=out, in0=in0, scalar1=s1, scalar2=None, op0=op0), rd, wr)
        else:
            self.P.op(eng, lambda e: e.tensor_scalar(out=out, in0=in0, scalar1=s1, scalar2=s2, op0=op0, op1=op1), rd, wr)

    def stt(self, out, in0, scalar, in1, op0, op1, rd, wr):
        self.P.op("dve", lambda e: e.scalar_tensor_tensor(out=out, in0=in0, scalar=scalar, in1=in1, op0=op0, op1=op1), rd, wr)

    def cp(self, eng, out, in_, rd, wr):
        if eng == "act":
            self.P.op("act", lambda e: e.activation(out=out, in_=in_, func=AF.Copy), rd, wr)
        else:
            self.P.op(eng, lambda e: e.tensor_copy(out=out, in_=in_), rd, wr)

    def dma(self, eng, out, in_, stream, rd, wr, join=False, slow=False):
        if slow:
            self.P.dma(eng, lambda e: e.dma_start(out=out, in_=in_, allow_slow_non_contiguous=True), stream, rd, wr, join)
        else:
            self.P.dma(eng, lambda e: e.dma_start(out=out, in_=in_), stream, rd, wr, join)

    def PS(self, b, c0, c1, rows=128):
        ap = self.pb[b][0:rows, c0:c1]
        return ap, [self.pres[b]]

    def wload(self, eng, src, src_res, nelem):
        i = self.wi
        self.wi = (i + 1) % len(self.wbuf)
        buf = self.wbuf[i]
        r = self.wres[i]
        self.dma(eng, buf[:, 0:nelem], src, self.st_w, src_res, [r])
        return buf, r

    def wstream(self, reqs, depth=1):
        loaded = []
        for i in range(len(reqs)):
            while len(loaded) < min(len(reqs), i + 1 + depth):
                loaded.append(self.wload(*reqs[len(loaded)]))
            yield loaded[i]

    def build(self):
        NL = self.NL
        nc = bass.Bass("TRN2", target_bir_lowering=False)
        self.nc = nc

        def EI(n, s, d=F32):
            return nc.dram_tensor(n, list(s), d, kind="ExternalInput").ap()

        def SC(n, s, d=F32):
            return nc.dram_tensor(n, list(s), d, kind="Internal").ap()

        self.xT0 = EI("xT0", [KC, 128, U])
        self.cT_d = EI("cT", [128, KC, 2])
        self.w_ada = EI("w_ada", [NL, D, 6 * D])
        self.w_in = EI("w_in", [NL, D, 5120])
        self.w_out = EI("w_out", [NL, D, D])
        self.w_up = EI("w_up", [NL, D, 2 * DFF])
        self.w_down = EI("w_down", [NL, DFF, D])
        self.badaT_d = EI("badaT", [128, 4, 96])
        self.n1g_d = EI("n1g", [128, 4, KC])
        self.n2g_d = EI("n2g", [128, 4, KC])
        self.fng_d = EI("fng", [128, KC])
        self.qkg_d = EI("qkg", [128, 4, 2])
        self.hgg_d = EI("hgg", [128, 4])
        self.lbp_d = EI("lbp", [128, 2, 4, 4])
        self.sgg_d = EI("sgg", [4, 1, 512])
        self.sgb_d = EI("sgb", [4, 1, 512])
        self.sgwT_d = EI("sgwT", [4, 128, 4, 128])
        self.convw_d = EI("convw", [128, 4, 3, 88])
        self.convb_d = EI("convb", [128, 4, 88])
        self.rope_d = EI("ropeT", [2, 128, T])
        self.rotT_d = EI("rotT", [128, 128])
        self.ident_d = EI("ident", [128, 128], BF16)
        self.identf_d = EI("identf", [128, 128])
        self.masks_d = EI("masks", [128, 6, 128])
        self.seg_d = EI("seg", [128, 3, 128])
        self.outT = nc.dram_tensor("outT", [KC, 128, T], F32, kind="ExternalOutput").ap()
        if self.dbg:
            self.dbg1 = nc.dram_tensor("dbg1", [KC, 128, U], F32, kind="ExternalOutput").ap()
            self.dbg2 = nc.dram_tensor("dbg2", [KC, 128, U], F32, kind="ExternalOutput").ap()
            self.dbgm = nc.dram_tensor("dbgm", [KC, 128, U], F32, kind="ExternalOutput").ap()
            self.dbgs = nc.dram_tensor("dbgs", [128, 6600], F32, kind="ExternalOutput").ap()

        self.xTs = SC("xTs", [KC, 128, U])
        self.xMs = SC("xMs", [KC, 128, U])
        self.obT = SC("obT", [4, 128, U])
        self.hTd = SC("hTd", [KC, 128, U], BF16)
        self.modd = SC("modd", [2, 6 * D])
        self.win_s = [SC(f"win_s{l}", [10, 128, KC * 512], BF16) for l in range(NL)]
        self.wout_s = [SC(f"wout_s{l}", [4, 128, KC * 512], BF16) for l in range(NL)]
        self.wup_s = [SC(f"wup_s{l}", [22, 128, KC * 512], BF16) for l in range(NL)]
        self.wdn_s = [SC(f"wdn_s{l}", [16, 128, FC * 128], BF16) for l in range(NL)]

        with contextlib.ExitStack() as stack:
            P = Prog(nc, stack)
            self.P = P
            self.alloc()
            self.setup()
            for l in range(NL):
                self.layer(l)
            P.barrier()
            P.emit()
        return nc

    def alloc(self):
        P = self.P
        self.st_w = P.stream("w", 2)
        self.st_ld = P.stream("ld", 4)
        self.st_st = P.stream("st", 4)
        self.st_cast = P.stream("cast", 4)
        self.st_x0 = P.stream("x0", 4)
        self.st_h = P.stream("hst", 2)
        self.pb = [P.ps(f"pb{i}", [128, 512]) for i in range(7)]
        self.pres = [P.res(f"pb{i}", excl=True) for i in range(7)]
        self.pb7 = P.ps("pb7", [128, 1024], BF16)
        self.p7res = [P.res("pb7", excl=True)] * 8
        self.wbuf = [P.sb(f"wbuf{i}", [128, 8192], BF16) for i in range(2)]
        self.wres = [P.res(f"wbuf{i}") for i in range(2)]
        self.wi = 0
        self.A32 = Arena(P.sb("A32", [128, 18400], F32), 18400)
        self.A16 = Arena(P.sb("A16", [128, 38000], BF16), 38000)
        self.C32t = P.sb("C32", [128, 6600], F32)
        C32 = Arena(self.C32t, 6600)
        C16 = Arena(P.sb("C16", [128, 1400], BF16), 1400)
        self.masks = C32.take(6, 128)
        self.seg = C32.take(3, 128)
        self.ones_f = C32.take(128)
        self.rotT = C32.take(128)
        self.ident_f = C32.take(128)
        self.sggB = C32.take(512)
        self.sgbB = C32.take(512)
        self.n1g = C32.take(4, KC)
        self.n2g = C32.take(4, KC)
        self.fng = C32.take(KC)
        self.qkg = C32.take(4, 2)
        self.hgg = C32.take(4)
        self.lbe = C32.take(2, 4, 4)
        self.lb = C32.take(2, 4, 4)
        self.oml = C32.take(2, 4, 4)
        self.lbs = C32.take(2, 4)
        self.convw = C32.take(4, 3, 88)
        self.convb = C32.take(4, 88)
        self.badaT = C32.take(4, 96)
        self.cT = C32.take(KC, 2)
        self.modT = C32.take(2, 96)
        self.G1 = C32.take(2, KC)
        self.G2 = C32.take(2, KC)
        self.modrow = [C32.take(512), C32.take(512)]
        self.state_f = C32.take(4, 128)
        self.dec = C32.take(8)
        self.ones_b = C16.take(128)
        self.ident = C16.take(128)
        self.scT = C16.take(KC, 2)
        self.sgwT = C16.take(4, 128)
        self.state_b = C16.take(4, 128)
        R = P.res
        self.r_const = R("const")
        self.r_par = R("par")
        self.r_sg = R("sgpar")
        self.r_lb = R("lb")
        self.r_mod = R("mod")
        self.r_modrow = [R("modrow0"), R("modrow1")]
        self.r_modd = R("modd")
        self.r_state = [R(f"state{h}") for h in range(4)]
        self.r_dec = [R(f"dec{h}") for h in range(8)]
        self.itc = 0
        self.acnt = 0
        self.r_xT = [R(f"xT{j}") for j in range(U // 128)]
        self.r_xM = [R(f"xM{j}") for j in range(U // 128)]
        self.r_ob = [R(f"ob{j}") for j in range(U // 128)]
        self.r_hTd = [R(f"hTd{j}") for j in range(U // 128)]
        self.r_wsc = [{m: R(f"wsc{l}{m}") for m in ("in", "out", "up", "dn")} for l in range(self.NL)]
        self.r_out = R("out")
        self.r_dbg = R("dbg")

    def setup(self):
        P = self.P
        ld = self.st_ld
        rc = [self.r_const]
        rp = [self.r_par]
        for (dst, src) in [(self.masks, self.masks_d), (self.seg, self.seg_d), (self.rotT, self.rotT_d), (self.ident_f, self.identf_d),
                           (self.ident, self.ident_d)]:
            self.dma("sp", dst, src, ld, [], rc, join=True)
        for (dst, src) in [(self.n1g, self.n1g_d), (self.n2g, self.n2g_d), (self.fng, self.fng_d),
                           (self.qkg, self.qkg_d), (self.hgg, self.hgg_d), (self.lbe, self.lbp_d),
                           (self.convw, self.convw_d), (self.convb, self.convb_d), (self.badaT, self.badaT_d),
                           (self.cT, self.cT_d)]:
            self.dma("sp", dst, src, ld, [], rp, join=True)
        P.op("dve", lambda e: e.memset(self.ones_f, 1.0), [], rc)
        P.op("dve", lambda e: e.memset(self.ones_b, 1.0), [], rc)
        self.act(self.scT, self.cT, AF.Silu, rp, [self.r_lb])
        lbe, lb, oml, lbs = self.lbe, self.lb, self.oml, self.lbs
        rl = [self.r_lb]
        self.act(lbe, lbe, AF.Exp, rp + rl, rp)
        self.tt("dve", lbs, lbe[:, :, 0, :], lbe[:, :, 1, :], ALU.add, rp, rl)
        self.tt("dve", lbs, lbs, lbe[:, :, 2, :], ALU.add, rp + rl, rl)
        self.tt("dve", lbs, lbs, lbe[:, :, 3, :], ALU.add, rp + rl, rl)
        P.op("dve", lambda e: e.reciprocal(out=lbs, in_=lbs), rl, rl)
        P.op("dve", lambda e: e.memset(lb[:, :, 0, :], 0.0), [], rl)
        self.tt("dve", lb[:, :, 1, :], lbe[:, :, 1, :], lbs, ALU.mult, rp + rl, rl)
        for j in (2, 3):
            self.tt("dve", lb[:, :, j, :], lbe[:, :, j, :], lbs, ALU.mult, rp + rl, rl)
            self.tt("dve", lb[:, :, j, :], lb[:, :, j, :], lb[:, :, j - 1, :], ALU.add, rl, rl)
        self.ts("dve", oml, lb, -1.0, 1.0, ALU.mult, ALU.add, rl, rl)
        for kc in range(KC):
            self.dma("sp", self.xTs[kc], self.xT0[kc], self.st_x0, [], self.r_xT, join=True)
        for l in range(self.NL):
            self.cast_weights(l)

    def cast_weights(self, l):
        cs = self.st_cast
        rw = self.r_wsc[l]
        for g in range(10):
            self.dma("pool", self.win_s[l][g].rearrange("p (k j) -> p k j", k=KC),
                     self.w_in[l][:, g * 512:(g + 1) * 512].rearrange("(k p) j -> p k j", p=128),
                     cs, [], [rw["in"]], join=True)
        for g in range(4):
            self.dma("pool", self.wout_s[l][g].rearrange("p (k j) -> p k j", k=KC),
                     self.w_out[l][:, g * 512:(g + 1) * 512].rearrange("(k p) j -> p k j", p=128),
                     cs, [], [rw["out"]], join=True)
        for t in range(22):
            dst = self.wup_s[l][t].rearrange("p (k j) -> p k j", k=KC)
            self.dma("pool", dst[:, :, 0:256],
                     self.w_up[l][:, t * 256:(t + 1) * 256].rearrange("(k p) j -> p k j", p=128),
                     cs, [], [rw["up"]], join=True)
            self.dma("pool", dst[:, :, 256:512],
                     self.w_up[l][:, DFF + t * 256:DFF + (t + 1) * 256].rearrange("(k p) j -> p k j", p=128),
                     cs, [], [rw["up"]], join=True)
        for m in range(16):
            self.dma("pool", self.wdn_s[l][m].rearrange("p (k j) -> p k j", k=FC),
                     self.w_down[l][:, m * 128:(m + 1) * 128].rearrange("(k p) j -> p k j", p=128),
                     cs, [], [rw["dn"]], join=True)

    def ada(self, l):
        P = self.P
        reqs = []
        for n in range(24):
            reqs.append(("pool", self.w_ada[l][:, n * 512:(n + 1) * 512].rearrange("(k p) j -> p k j", p=128), [], 8192))
        n = 0
        for (buf, r) in self.wstream_ada(reqs):
            wt = buf[:, 0:8192].rearrange("p (k j) -> p k j", k=KC)
            ps, pr = self.PS(n % 2, 0, 512, rows=2)
            for kc in range(KC):
                self.mm(ps, self.scT[:, kc, :], wt[:, kc, :], kc == 0, kc == KC - 1, [r, self.r_lb], pr)
            mr = self.modrow[n % 2]
            rr = self.r_modrow[n % 2]
            self.cp("act", mr[0:2, :], ps, pr, [rr])
            for cc in range(4):
                c = n * 4 + cc
                pm, pmr = self.PS(2, 0, 192)
                self.mm(pm[:, c * 2:c * 2 + 2], mr[0:2, cc * 128:(cc + 1) * 128], self.ident_f[0:2, 0:2], True, True,
                        [rr, self.r_const], pmr)
            n += 1
        rm = [self.r_mod]
        pm, pmr = self.PS(2, 0, 192)
        pm3 = pm.rearrange("p (c r) -> p c r", r=2)
        for r_ in range(2):
            self.cp("dve", self.modT[:, r_, :], pm3[:, :, r_], pmr, rm)
        for r_ in range(2):
            self.tt("dve", self.modT[:, r_, :], self.modT[:, r_, :], self.badaT[:, l, :], ALU.add, rm + [self.r_par], rm)
        for r_ in range(2):
            self.stt(self.G1[:, r_, :], self.modT[:, r_, 16:32], 1.0, self.n1g[:, l, :], ALU.add, ALU.mult, rm + [self.r_par], rm)
            self.stt(self.G2[:, r_, :], self.modT[:, r_, 64:80], 1.0, self.n2g[:, l, :], ALU.add, ALU.mult, rm + [self.r_par], rm)

    def wstream_ada(self, reqs):
        loaded = []
        for i in range(len(reqs)):
            while len(loaded) < min(len(reqs), i + 2):
                eng, src, sres, nelem = reqs[len(loaded)]
                k = self.wi
                self.wi = (k + 1) % len(self.wbuf)
                buf = self.wbuf[k]
                r = self.wres[k]
                self.dma(eng, buf[:, 0:nelem].rearrange("p (k j) -> p k j", k=KC), src, self.st_w, sres, [r])
                loaded.append((buf, r))
            yield loaded[i]

    def norm(self, x, rx, n, G, S, rg, dst, rdst, tmp):
        sq, rsq, tm, rtm, rstd, rrstd = tmp
        pn, pnr = self.PS(2, 0, n)
        for kc in range(KC):
            i = kc % 2
            self.act(sq[i][:, 0:n], x[:, kc, 0:n], AF.Square, [rx], [rsq[i]])
            self.mm(pn, self.ones_f, sq[i][:, 0:n], kc == 0, kc == KC - 1, [rsq[i], self.r_const], pnr)
        self.act(rstd[:, 0:n], pn, AF.Sqrt, pnr, [rrstd], bias=EPS, scale=1.0 / D)
        self.P.op("dve", lambda e: e.reciprocal(out=rstd[:, 0:n], in_=rstd[:, 0:n]), [rrstd], [rrstd])
        for kc in range(KC):
            i = kc % 2
            self.tt("dve", tm[i][:, 0:n], x[:, kc, 0:n], rstd[:, 0:n], ALU.mult, [rx, rrstd], [rtm[i]])
            if S is not None:
                self.act(dst(kc), tm[i][:, 0:n], AF.Identity, [rtm[i]] + rg, rdst, bias=S[:, kc:kc + 1], scale=G[:, kc:kc + 1])
            else:
                self.act(dst(kc), tm[i][:, 0:n], AF.Identity, [rtm[i]] + rg, rdst, scale=G[:, kc:kc + 1])

    def proj_fm(self, wt, rw, kcn, mo, rhs, rrhs, n, bank):
        ps, pr = self.PS(bank, 0, n)
        for kc in range(kcn):
            self.mm(ps, wt[:, kc, mo:mo + 128], rhs(kc), kc == 0, kc == kcn - 1, [rw] + rrhs, pr)
        return ps, pr

    def proj_tm(self, wt, rw, c0, w, hT, rh, t0, bank):
        ps, pr = self.PS(bank, 0, w)
        for kc in range(KC):
            self.mm(ps, hT[:, kc, t0:t0 + 128], wt[:, kc, c0:c0 + w], kc == 0, kc == KC - 1, [rw, rh], pr)
        return ps, pr

    def alloc_p12(self):
        A32, A16, R = self.A32, self.A16, self.P.res
        A32.reset()
        A16.reset()
        n = NB
        self.xsb = A32.take(KC, n); self.r_x = R("xsb")
        self.ntmp = ([A32.take(n), A32.take(n)], [R("sq0"), R("sq1")], [A32.take(n), A32.take(n)], [R("tm0"), R("tm1")],
                     A32.take(n), R("rstd"))
        self.qS = A32.take(4, n); self.lfS = A32.take(4, n); self.kkS = A32.take(4, n)
        self.r_qS = [R(f"qS{h}") for h in range(4)]
        self.r_lfS = [R(f"lfS{h}") for h in range(4)]
        self.r_kkS = [R(f"kkS{h}") for h in range(4)]
        self.t1 = A32.take(n); self.t2 = A32.take(n); self.r_t1 = R("t1"); self.r_t2 = R("t2")
        self.rope = A32.take(2, n); self.r_rope = R("rope")
        self.hg = {}
        for nm in ["Sf", "tB", "tC", "rso"]:
            self.hg[nm] = (A32.take(128), R("hg_" + nm))
        self.hgo = [{nm: (A32.take(128), R(f"hgo{i}_" + nm)) for nm in ["oS", "sqo"]} for i in range(2)]
        self.hgs = [{nm: (A32.take(128), R(f"hgs{i}_" + nm)) for nm in
                     ["P32", "P64", "P128", "U32", "U64", "U128", "EA", "nW32", "nW64", "XA", "YA", "XB", "YB", "XC", "YC", "XD", "YD"]}
                    for i in range(2)]
        self.gv = A32.take(512); self.sqv = A32.take(512); self.vn = A32.take(512)
        self.ssum = A32.take(4); self.tsg = A32.take(128)
        self.r_gv = R("gv"); self.r_sqv = R("sqv"); self.r_vn = R("vn"); self.r_ssum = R("ssum"); self.r_tsg = R("tsg")
        self.rD = A32.take(n); self.r_rD = R("rD")
        self.obst = A32.take(4, 128); self.r_obst = R("obst")
        self.obld = A32.take(4, 128); self.r_obld = R("obld")
        self.sgm = A32.take(n); self.fS = A32.take(n); self.r_sgm = R("sgm"); self.r_fS = R("fS")
        self.hT = A16.take(KC, n); self.r_h = R("hT")
        self.KT = A16.take(2, U); self.r_KT = R("KT")
        self.V = A16.take(U // 128, 256); self.r_V = R("V")
        self.QT = A16.take(8, n); self.r_QT = [R(f"QT{h}") for h in range(8)]
        self.mixT = A16.take(KC, n); self.r_mix = [R(f"mix{c}") for c in range(KC)]
        self.pT = [A16.take(2 * n) for _ in range(3)]; self.r_pT = [R(f"pT{i}") for i in range(3)]
        self.vS = A16.take(n // 128, 512); self.r_vS = [R(f"vS{i}") for i in range(n // 128)]
        self.vnb = A16.take(512); self.r_vnb = R("vnb")
        self.gateS = A16.take(4, n); self.uS = A16.take(4, n)
        self.r_gateS = [R(f"gateS{h}") for h in range(4)]
        self.r_uS = [R(f"uS{h}") for h in range(4)]
        self.hop = [{nm: (A16.take(128), R(f"hop{i}_{nm}")) for nm in ["qA", "kA", "qB", "kB", "qC", "kC", "qD", "kD", "Sb", "kDt"]}
                    for i in range(4)]

    def alloc_p3(self):
        A32, A16, R = self.A32, self.A16, self.P.res
        A32.reset()
        A16.reset()
        n = FW + 8
        self.xsb = A32.take(KC, n); self.r_x = R("xsb3")
        self.ntmp = ([A32.take(n), A32.take(n)], [R("sq0"), R("sq1")], [A32.take(n), A32.take(n)], [R("tm0"), R("tm1")],
                     A32.take(n), R("rstd"))
        self.accg = [A32.take(n) for _ in range(2)]; self.accv = [A32.take(n) for _ in range(2)]
        self.sil = [A32.take(n) for _ in range(2)]
        self.r_accg = [R("accg0"), R("accg1")]; self.r_accv = [R("accv0"), R("accv1")]; self.r_sil = [R("sil0"), R("sil1")]
        self.hT = A16.take(KC, n); self.r_h = R("hT3")
        self.aT = A16.take(FC, n); self.r_aT = [R(f"aT{j}") for j in range(FC)]

    def layer(self, l):
        P = self.P
        last = (l == 3)
        if self.stop == "setup":
            self.dump_c32()
            return
        self.ada(l)
        P.barrier()
        if self.stop == "ada":
            self.dump_c32()
            return
        self.alloc_p12()
        self.dma("sp", self.sggB, self.sgg_d[l].partition_broadcast(128), self.st_ld, [], [self.r_sg], join=True)
        self.dma("sp", self.sgbB, self.sgb_d[l].partition_broadcast(128), self.st_ld, [], [self.r_sg], join=True)
        self.dma("pool", self.sgwT, self.sgwT_d[l], self.st_ld, [], [self.r_sg], join=True)
        blocks = [(0, LC, True)] + [(LC + j * NB, NB, False) for j in range(T // NB)]
        self.reset_state()
        p1b = [blocks[0]] + blocks[:0:-1]
        if self.stop and self.stop.startswith("p1b"):
            p1b = p1b[:int(self.stop[3:])]
        self.pass1_norm(l, p1b[0])
        for bi, blk in enumerate(p1b):
            self.pass1_block(l, blk, p1b[bi + 1] if bi + 1 < len(p1b) else None)
        if self.stop and self.stop.startswith("p1"):
            return
        self.reset_state()
        p2b = blocks
        if self.stop and self.stop.startswith("p2b"):
            p2b = p2b[:int(self.stop[3:])]
        for blk in p2b:
            self.pass2_block(l, blk, last)
        if self.dbg and l == 0:
            for kc in range(KC):
                self.dma("sp", self.dbg1[kc], self.xMs[kc], self.st_x0, self.r_xM, [self.r_dbg], join=True)
        P.barrier()
        if self.stop and self.stop.startswith("p2"):
            return
        self.alloc_p3()
        wins = []
        if not last:
            wins.append((0, LC, 0, LC))
        o = 0
        while o < T:
            n = min(FW, T - o)
            wins.append((LC + o, LC + o + n, LC, U))
            o += n
        for w in wins:
            self.ffn_window(l, w, last)
        if self.dbg and l == 0:
            for kc in range(KC):
                self.dma("sp", self.dbg2[kc], self.xTs[kc], self.st_x0, self.r_xT, [self.r_dbg], join=True)
        P.barrier()

    def dump_c32(self):
        self.P.barrier()
        self.dma("sp", self.dbgs, self.C32t[:, :], self.st_st, [], [self.r_dbg])

    def reset_state(self):
        for h in range(4):
            self.P.op("dve", lambda e, h=h: e.memset(self.state_f[:, h, :], 0.0), [], [self.r_state[h]])
            self.P.op("pool", lambda e, h=h: e.memset(self.state_b[:, h, :], 0.0), [], [self.r_state[h]])

    def load_x(self, u0, n):
        rs = self.r_xT[u0 // 128:(u0 + n + 127) // 128]
        self.dma("sp", self.xsb[:, :, 0:n], self.xTs[:, :, u0:u0 + n].rearrange("k p n -> p k n"), self.st_ld, rs, [self.r_x])
        return rs

    def block_norm1(self, l, blk):
        u0, nb, isctx = blk
        r_ = 1 if isctx else 0
        self.load_x(u0, nb)
        self.norm(self.xsb, self.r_x, nb, self.G1[:, r_, :], self.modT[:, r_, 0:16], [self.r_mod],
                  lambda kc: self.hT[:, kc, 0:nb], [self.r_h], self.ntmp)

    def win_req(self, l, g):
        return ("sp", self.win_s[l][g], [self.r_wsc[l]["in"]], 8192)

    def qk_head(self, l, ps, pr, n, gcol, rope, dst, rdst):
        sq, rsq, tm, rtm, rstd, rrstd = self.ntmp
        kraw, r_kraw = tm[0], rtm[0]
        kn, r_kn = tm[1], rtm[1]
        self.cp("act", kraw[:, 0:n], ps, pr, [r_kraw])
        self.tt("pool", sq[0][:, 0:n], kraw[:, 0:n], kraw[:, 0:n], ALU.mult, [r_kraw], [rsq[0]])
        pn, pnr = self.PS(2, 0, n)
        self.mm(pn, self.ones_f, sq[0][:, 0:n], True, True, [rsq[0], self.r_const], pnr)
        self.rstd_lnexp(sq[1][:, 0:n], pn, pnr, [rsq[1]], 1.0 / 128)
        self.stt(kn[:, 0:n], kraw[:, 0:n], gcol, sq[1][:, 0:n], ALU.mult, ALU.mult, [r_kraw, rsq[1], self.r_par], [r_kn])
        if rope:
            pro, prr = self.PS(3, 256, 256 + n)
            self.mm(pro, self.rotT, kn[:, 0:n], True, True, [r_kn, self.r_const], prr)
            self.tt("dve", self.t1[:, 0:n], kn[:, 0:n], self.rope[:, 0, 0:n], ALU.mult, [r_kn, self.r_rope], [self.r_t1])
            self.tt("dve", self.t2[:, 0:n], pro, self.rope[:, 1, 0:n], ALU.mult, prr + [self.r_rope], [self.r_t2])
            self.tt("pool", dst, self.t1[:, 0:n], self.t2[:, 0:n], ALU.add, [self.r_t1, self.r_t2], rdst)
        else:
            self.cp("act", dst, kn[:, 0:n], [r_kn], rdst)

    def load_rope(self, u0, nb):
        t0 = u0 - LC
        self.dma("sp", self.rope[:, :, 0:nb], self.rope_d[:, :, t0:t0 + nb].rearrange("a p n -> p a n"), self.st_ld,
                 [], [self.r_rope])

    def hgrn_q(self, wq, rwq, nb):
        rhs = lambda kc: self.hT[:, kc, 0:nb]
        for hd in range(4):
            ps, pr = self.proj_fm(wq, rwq, KC, hd * 128, rhs, [self.r_h], nb, hd % 2)
            self.act(self.qS[:, hd, 0:nb], ps, AF.Silu, pr, [self.r_qS[hd]])

    def hgrn_f(self, l, d, wf, rwf, nb):
        rhs = lambda kc: self.hT[:, kc, 0:nb]
        for hd in range(4):
            ps, pr = self.proj_fm(wf, rwf, KC, hd * 128, rhs, [self.r_h], nb, hd % 2)
            self.act(self.sgm[:, 0:nb], ps, AF.Sigmoid, pr, [self.r_sgm])
            self.ts("dve", self.fS[:, 0:nb], self.sgm[:, 0:nb], self.oml[:, d, l, hd:hd + 1], self.lb[:, d, l, hd:hd + 1],
                    ALU.mult, ALU.add, [self.r_sgm, self.r_lb], [self.r_fS])
            self.ts("pool", self.fS[:, 0:nb], self.fS[:, 0:nb], F_MIN, None, ALU.max, None, [self.r_fS], [self.r_fS])
            self.act(self.lfS[:, hd, 0:nb], self.fS[:, 0:nb], AF.Ln, [self.r_fS], [self.r_lfS[hd]])
            self.ts("pool", self.kkS[:, hd, 0:nb], self.fS[:, 0:nb], -1.0, 1.0, ALU.mult, ALU.add, [self.r_fS], [self.r_kkS[hd]])

    def hi_prep(self, wt, rw, nb):
        for ti in range(nb // 128):
            ps, pr = self.proj_tm(wt, rw, 0, 512, self.hT, self.r_h, ti * 128, ti % 2)
            self.cp("act", self.vS[:, ti, :], ps, pr, [self.r_vS[ti]])

    def rstd_lnexp(self, out, in_, rd, wr, n_inv):
        self.act(out, in_, AF.Ln, rd, wr, bias=EPS, scale=n_inv)
        self.act(out, out, AF.Exp, wr, wr, scale=-0.5)

    def hgrn_front1(self, l, d, hd, ti, it):
        H = self.hgs[it % 2]
        c0 = ti * 128
        lf = self.lfS[:, hd, c0:c0 + 128]; rlf = self.r_lfS[hd]
        rc = self.r_const
        for i, nm in enumerate(["P32", "P64", "P128"]):
            ap, r = H[nm]
            self.P.op("dve", lambda e, ap=ap, i=i: e.tensor_tensor_scan(out=ap, data0=self.seg[:, i, :], data1=lf, initial=0.0,
                                                                       op0=ALU.mult, op1=ALU.add), [rlf, rc], [r])
        if d == 1:
            Us = []
            for nm, pn in [("U32", "P32"), ("U64", "P64"), ("U128", "P128")]:
                ap, r = H[nm]
                self.tt("pool", ap, H[pn][0], lf, ALU.subtract, [H[pn][1], rlf], [r])
                Us.append((ap, r))
        else:
            Us = [H["P32"], H["P64"], H["P128"]]
        (U32, rU32), (U64, rU64), (U128, rU128) = Us
        P32, rP32 = H["P32"]; P64, rP64 = H["P64"]
        mpos = 15 if d == 0 else 16
        U32v = U32.rearrange("p (a b) -> p a b", a=4)
        EA, rEA = H["EA"]
        self.tt("dve", EA.rearrange("p (a b) -> p a b", a=4), U32v, U32v[:, :, mpos:mpos + 1].to_broadcast([128, 4, 32]),
                ALU.subtract, [rU32], [rEA])
        nW32, rnW32 = H["nW32"]
        nW64, rnW64 = H["nW64"]
        P32v = P32.rearrange("p (a b) -> p a b", a=4)
        P64v = P64.rearrange("p (a b) -> p a b", a=2)
        self.tt("dve", nW32.rearrange("p (a b) -> p a b", a=4), U32v, P32v[:, :, 31:32].to_broadcast([128, 4, 32]),
                ALU.subtract, [rU32, rP32], [rnW32])
        self.tt("pool", nW64.rearrange("p (a b) -> p a b", a=2), U64.rearrange("p (a b) -> p a b", a=2),
                P64v[:, :, 63:64].to_broadcast([128, 2, 64]), ALU.subtract, [rU64, rP64], [rnW64])

    def hgrn_front2(self, l, d, hd, ti, it):
        H = self.hgs[it % 2]
        if d == 1:
            (U32, rU32), (U64, rU64), (U128, rU128) = H["U32"], H["U64"], H["U128"]
        else:
            (U32, rU32), (U64, rU64), (U128, rU128) = H["P32"], H["P64"], H["P128"]
        P128, rP128 = H["P128"]
        EA, rEA = H["EA"]; nW32, rnW32 = H["nW32"]; nW64, rnW64 = H["nW64"]
        tot = P128[:, 127:128]
        ex = [("XA", EA, rEA, 1.0, None), ("YA", EA, rEA, -1.0, None),
              ("XB", U32, rU32, 1.0, None), ("YB", nW32, rnW32, -1.0, None),
              ("XC", U64, rU64, 1.0, None), ("YC", nW64, rnW64, -1.0, None),
              ("XD", U128, rU128, 1.0, None), ("YD", U128, rU128, -1.0, tot)]
        for nm, src, rs, sc, b_ in ex:
            ap, r = H[nm]
            if b_ is None:
                self.act(ap, src, AF.Exp, [rs], [r], scale=sc)
            else:
                self.act(ap, src, AF.Exp, [rs, rP128], [r], scale=sc, bias=b_)
        self.act(self.dec[:, it % 8:it % 8 + 1], tot, AF.Exp, [rP128], [self.r_dec[it % 8]])

    def hgrn_front3(self, l, d, hd, ti, it):
        H = self.hgs[it % 2]
        c0 = ti * 128
        q = self.qS[:, hd, c0:c0 + 128]; rq = self.r_qS[hd]
        kk = self.kkS[:, hd, c0:c0 + 128]; rkk = self.r_kkS[hd]
        O = self.hop[it % 4]
        engs = ["dve", "pool"]
        k = 0
        for lv in "ABCD":
            eq, ek = ("X" + lv, "Y" + lv) if d == 0 else ("Y" + lv, "X" + lv)
            self.tt(engs[k % 2], O["q" + lv][0], q, H[eq][0], ALU.mult, [rq, H[eq][1]], [O["q" + lv][1]]); k += 1
            self.tt(engs[k % 2], O["k" + lv][0], kk, H[ek][0], ALU.mult, [rkk, H[ek][1]], [O["k" + lv][1]]); k += 1

    def hgrn_front(self, l, d, hd, ti, it):
        self.hgrn_front1(l, d, hd, ti, it)
        self.hgrn_front2(l, d, hd, ti, it)
        self.hgrn_front3(l, d, hd, ti, it)

    def hgrn_back1(self, d, hd, ti, it):
        H = self.hg
        O = self.hop[it % 4]
        rc = self.r_const
        slots = [(0, 0), (0, 128), (0, 256)]
        sps = []
        for i, lv in enumerate("ABC"):
            ps, pr = self.PS(slots[i][0], slots[i][1], slots[i][1] + 128)
            self.mm(ps, O["k" + lv][0], O["q" + lv][0], True, True, [O["k" + lv][1], O["q" + lv][1]], pr)
            sps.append((ps, pr))
        slot7 = it % 2
        pt = self.pb7[:, slot7 * 128:(slot7 + 1) * 128]
        ptr = [self.p7res[slot7]]
        self.P.op("pe", lambda e: e.transpose(pt, O["kD"][0], self.ident), [O["kD"][1], rc], ptr)
        Sf, rSf = H["Sf"]; tB, rtB = H["tB"]; tC, rtC = H["tC"]
        self.tt("dve", Sf, sps[0][0], self.masks[:, d * 3 + 0, :], ALU.mult, sps[0][1] + [rc], [rSf])
        self.tt("dve", tB, sps[1][0], self.masks[:, d * 3 + 1, :], ALU.mult, sps[1][1] + [rc], [rtB])
        self.tt("dve", tC, sps[2][0], self.masks[:, d * 3 + 2, :], ALU.mult, sps[2][1] + [rc], [rtC])
        self.tt("pool", Sf, Sf, tB, ALU.add, [rSf, rtB], [rSf])
        Sb, rSb = O["Sb"]
        self.tt("pool", Sb, Sf, tC, ALU.add, [rSf, rtC], [rSb])
        kDt, rkDt = O["kDt"]
        self.cp("dve", kDt, pt, ptr, [rkDt])

    def hgrn_back2(self, d, hd, ti, it):
        O = self.hop[it % 4]
        v = self.vS[:, ti, hd * 128:(hd + 1) * 128]; rv = self.r_vS[ti]
        rst = self.r_state[hd]
        po, por = self.PS(1, 0, 128)
        self.mm(po, v, O["Sb"][0], True, False, [rv, O["Sb"][1]], por)
        self.mm(po, self.state_b[:, hd, :], O["qD"][0], False, True, [rst, O["qD"][1]], por)
        pst, pstr = self.PS(1, 128, 256)
        self.mm(pst, O["kDt"][0], v, True, True, [O["kDt"][1], rv], pstr)
        self.stt(self.state_f[:, hd, :], self.state_f[:, hd, :], self.dec[:, it % 8:it % 8 + 1], pst, ALU.mult, ALU.add,
                 pstr + [rst, self.r_dec[it % 8]], [rst])
        self.cp("pool", self.state_b[:, hd, :], self.state_f[:, hd, :], [rst], [rst])
        return po, por

    def pass1_norm(self, l, blk):
        u0, nb, isctx = blk
        self.block_norm1(l, blk)
        if not isctx:
            self.load_rope(u0, nb)
        rs = self.r_hTd[u0 // 128:(u0 + nb) // 128]
        self.dma("act", self.hTd[:, :, u0:u0 + nb].rearrange("k p n -> p k n"), self.hT[:, :, 0:nb], self.st_h, [self.r_h], rs)

    def pass1_block(self, l, blk, nxt_blk):
        u0, nb, isctx = blk
        rhs = lambda kc: self.hT[:, kc, 0:nb]
        reqs = [self.win_req(l, g) for g in (2, 3, 5, 6)]
        tiles = []
        for (buf, r) in self.wstream(reqs):
            tiles.append((buf[:, 0:8192].rearrange("p (k j) -> p k j", k=KC), r))
            gi = len(tiles) - 1
            wt, rw = tiles[gi]
            if gi == 0:
                for kv in range(2):
                    ps, pr = self.proj_fm(wt, rw, KC, kv * 128, rhs, [self.r_h], nb, kv % 2)
                    self.qk_head(l, ps, pr, nb, self.qkg[:, l, 1:2], not isctx, self.KT[:, kv, u0:u0 + nb], [self.r_KT])
                for ti in range(nb // 128):
                    ps, pr = self.proj_tm(wt, rw, 256, 256, self.hT, self.r_h, ti * 128, ti % 2)
                    self.cp("act", self.V[:, u0 // 128 + ti, :], ps, pr, [self.r_V])
            elif gi == 1:
                self.hgrn_q(wt, rw, nb)
            elif gi == 2:
                self.hgrn_f(l, 1, wt, rw, nb)
            elif gi == 3:
                self.hi_prep(wt, rw, nb)
        if nxt_blk is not None:
            self.pass1_norm(l, nxt_blk)
        its = [(ti, hd) for ti in reversed(range(nb // 128)) for hd in range(4)]
        n_it = len(its)
        for s_ in range(n_it + 2):
            if 0 <= s_ - 2 < n_it:
                ti, hd = its[s_ - 2]
                po, por = self.hgrn_back2(1, hd, ti, self.itc + s_ - 2)
                self.cp("act", self.obst[:, hd, :], po, por, [self.r_obst])
                if hd == 3:
                    gt = u0 // 128 + ti
                    self.dma("act", self.obT[:, :, gt * 128:(gt + 1) * 128].rearrange("h p n -> p h n"), self.obst, self.st_st,
                             [self.r_obst], [self.r_ob[gt]])
            if 0 <= s_ - 1 < n_it:
                ti, hd = its[s_ - 1]
                self.hgrn_back1(1, hd, ti, self.itc + s_ - 1)
            if s_ < n_it:
                ti, hd = its[s_]
                self.hgrn_front(l, 1, hd, ti, self.itc + s_)
        self.itc += n_it

    def pass2_block(self, l, blk, last):
        u0, nb, isctx = blk
        skip_out = last and isctx
        ntile = nb // 128
        if not skip_out:
            self.load_x(u0, nb)
        rs_h = self.r_hTd[u0 // 128:(u0 + nb) // 128]
        self.dma("sp", self.hT[:, :, 0:nb], self.hTd[:, :, u0:u0 + nb].rearrange("k p n -> p k n"), self.st_ld, rs_h, [self.r_h])
        if not isctx:
            self.load_rope(u0, nb)
        rhs = lambda kc: self.hT[:, kc, 0:nb]
        gs = [3, 4, 6] if skip_out else [3, 4, 6, 7, 0, 1, 8, 9]
        reqs = [self.win_req(l, g) for g in gs]
        if not skip_out:
            reqs += [("sp", self.wout_s[l][g], [self.r_wsc[l]["out"]], 8192) for g in range(4)]
        ws = self.wstream(reqs)

        def nxt():
            buf, r = next(ws)
            return buf[:, 0:8192].rearrange("p (k j) -> p k j", k=KC), r

        wq, rwq = nxt()
        self.hgrn_q(wq, rwq, nb)
        wf, rwf = nxt()
        self.hgrn_f(l, 0, wf, rwf, nb)
        wt, rw = nxt()
        self.hi_prep(wt, rw, nb)
        oblds = [(self.obld, self.r_obld), (self.obst, self.r_obst)]
        if not skip_out:
            wt, rw = nxt()
            for hd in range(4):
                ps, pr = self.proj_fm(wt, rw, KC, hd * 128, rhs, [self.r_h], nb, hd % 2)
                self.act(self.gateS[:, hd, 0:nb], ps, AF.Silu, pr, [self.r_gateS[hd]])
            for ti in range(ntile):
                gt = u0 // 128 + ti
                self.dma("sp", oblds[ti][0], self.obT[:, :, gt * 128:(gt + 1) * 128].rearrange("h p n -> p h n"), self.st_ld,
                         [self.r_ob[gt]], [oblds[ti][1]])
            for g in (0, 1):
                wt, rw = nxt()
                for hh in range(4):
                    h = g * 4 + hh
                    ps, pr = self.proj_fm(wt, rw, KC, hh * 128, rhs, [self.r_h], nb, hh % 2)
                    self.qk_head(l, ps, pr, nb, self.qkg[:, l, 0:1], not isctx, self.QT[:, h, 0:nb], [self.r_QT[h]])
        H = self.hg
        its = [(ti, hd) for ti in range(ntile) for hd in range(4)]
        n_it = len(its)
        for s_ in range(n_it + 3):
            if s_ < n_it:
                ti, hd = its[s_]
                self.hgrn_front1(l, 0, hd, ti, self.itc + s_)
            if not skip_out and s_ < 8:
                self.attention_pair(l, blk, s_ // 2, s_ % 2)
            if s_ < n_it:
                ti, hd = its[s_]
                self.hgrn_front2(l, 0, hd, ti, self.itc + s_)
                self.hgrn_front3(l, 0, hd, ti, self.itc + s_)
            if 0 <= s_ - 1 < n_it:
                ti, hd = its[s_ - 1]
                self.hgrn_back1(0, hd, ti, self.itc + s_ - 1)
            if not skip_out and 0 <= s_ - 3 < n_it:
                ti, hd = its[s_ - 3]
                c0 = ti * 128
                oS, roS = self.hgo[(s_ - 3) % 2]["oS"]; sqo, rsqo = self.hgo[(s_ - 3) % 2]["sqo"]; rso, rrso = H["rso"]
                pn, pnr = self.PS(1, 256, 384)
                self.mm(pn, self.ones_f, sqo, True, True, [rsqo, self.r_const], pnr)
                self.rstd_lnexp(rso, pn, pnr, [rrso], 1.0 / 128)
                self.stt(oS, oS, self.hgg[:, l:l + 1], rso, ALU.mult, ALU.mult, [roS, rrso, self.r_par], [roS])
                self.tt("pool", self.mixT[:, 8 + hd, c0:c0 + 128], oS, self.gateS[:, hd, c0:c0 + 128], ALU.mult,
                        [roS, self.r_gateS[hd]], [self.r_mix[8 + hd]])
            if 0 <= s_ - 2 < n_it:
                ti, hd = its[s_ - 2]
                po, por = self.hgrn_back2(0, hd, ti, self.itc + s_ - 2)
                if not skip_out:
                    oS, roS = self.hgo[(s_ - 2) % 2]["oS"]; sqo, rsqo = self.hgo[(s_ - 2) % 2]["sqo"]
                    self.tt("dve", oS, po, oblds[ti][0][:, hd, :], ALU.add, por + [oblds[ti][1]], [roS])
                    self.tt("pool", sqo, oS, oS, ALU.mult, [roS], [rsqo])
        self.itc += n_it
        if skip_out:
            return
        wt, rw = nxt()
        for g in range(4):
            ps, pr = self.proj_fm(wt, rw, KC, g * 128, rhs, [self.r_h], nb, g % 2)
            self.act(self.uS[:, g, 0:nb], ps, AF.Gelu_apprx_tanh, pr, [self.r_uS[g]])
        wsv, rwsv = nxt()
        for ti in range(ntile):
            c0 = ti * 128
            ps, pr = self.proj_tm(wsv, rwsv, 0, 512, self.hT, self.r_h, c0, ti % 2)
            self.act(self.gv, ps, AF.Gelu_apprx_tanh, pr, [self.r_gv])
            self.tt("pool", self.sqv, self.gv, self.gv, ALU.mult, [self.r_gv], [self.r_sqv])
            self.P.op("dve", lambda e: e.tensor_reduce(out=self.ssum, in_=self.sqv.rearrange("p (g d) -> p g d", g=4), axis=AX.X,
                                                       op=ALU.add), [self.r_sqv], [self.r_ssum])
            self.rstd_lnexp(self.ssum, self.ssum, [self.r_ssum], [self.r_ssum], 1.0 / 128)
            self.tt("dve", self.vn.rearrange("p (g d) -> p g d", g=4), self.gv.rearrange("p (g d) -> p g d", g=4),
                    self.ssum.unsqueeze(2).to_broadcast([128, 4, 128]), ALU.mult, [self.r_gv, self.r_ssum], [self.r_vn])
            self.tt("pool", self.vnb, self.vn, self.sggB, ALU.mult, [self.r_vn, self.r_sg], [self.r_vnb])
            for g in range(4):
                pg, pgr = self.PS(3 + (g % 2), 256, 384)
                self.mm(pg, self.vnb[:, g * 128:(g + 1) * 128], self.sgwT[:, g, :], True, True, [self.r_vnb, self.r_sg], pgr)
                self.tt("dve", self.tsg, pg, self.sgbB[:, g * 128:(g + 1) * 128], ALU.add, pgr + [self.r_sg], [self.r_tsg])
                self.tt("pool", self.mixT[:, 12 + g, c0:c0 + 128], self.tsg, self.uS[:, g, c0:c0 + 128], ALU.mult,
                        [self.r_tsg, self.r_uS[g]], [self.r_mix[12 + g]])
        if self.dbg and l == 0:
            self.dma("pool", self.dbgm[:, :, u0:u0 + nb].rearrange("k p n -> p k n"), self.mixT[:, :, 0:nb], self.st_st,
                     self.r_mix, [self.r_dbg], join=True)
        r_ = 1 if isctx else 0
        mrhs = lambda kc: self.mixT[:, kc, 0:nb]
        for g in range(4):
            wt, rw = nxt()
            for mm_ in range(4):
                m = g * 4 + mm_
                ps, pr = self.proj_fm(wt, rw, KC, mm_ * 128, mrhs, self.r_mix, nb, mm_ % 2)
                self.stt(self.xsb[:, m, 0:nb], ps, self.modT[:, r_, 32 + m:33 + m], self.xsb[:, m, 0:nb], ALU.mult, ALU.add,
                         pr + [self.r_x, self.r_mod], [self.r_x])
        rs = self.r_xM[u0 // 128:(u0 + nb) // 128]
        self.dma("act", self.xMs[:, :, u0:u0 + nb].rearrange("k p n -> p k n"), self.xsb[:, :, 0:nb], self.st_st, [self.r_x], rs)

    def attention_pair(self, l, blk, hp, half):
        u0, nb, isctx = blk
        keyt_all = [0, 1] if isctx else list(range(U // 128))
        ntot = len(keyt_all)
        hsz = ntot // 2
        keyt = keyt_all[half * hsz:(half + 1) * hsz]
        kbase = half * hsz
        scale = 128 ** -0.5
        h0, h1 = 2 * hp, 2 * hp + 1
        kv = hp // 2
        n = len(keyt)
        N2 = 2 * nb
        LA = 2
        banks = [2, 3, 4]
        Q2 = self.QT[:, h0:h0 + 2, :].rearrange("p a b -> p (a b)")
        M2 = self.mixT[:, h0:h0 + 2, :].rearrange("p a b -> p (a b)")
        rq = [self.r_QT[h0], self.r_QT[h1]]
        po, por = self.PS(5, 0, N2)
        pd, pdr = self.PS(6, 0, N2)
        sts = {}
        for ji in range(n + LA):
            if ji < n:
                j = keyt[ji]
                cnt = self.acnt
                self.acnt += 1
                ps, pr = self.PS(banks[cnt % 3], 0, N2)
                self.mm(ps, self.KT[:, kv, j * 128:(j + 1) * 128], Q2, True, True, [self.r_KT] + rq, pr)
                pT = self.pT[cnt % 3][:, 0:N2]
                rpT = self.r_pT[cnt % 3]
                self.act(pT, ps, AF.Exp, pr, [rpT], scale=scale)
                sts[ji] = (pT, rpT)
            jv = ji - LA
            if jv >= 0:
                pT, rpT = sts.pop(jv)
                j = keyt[jv]
                g0 = (kbase + jv == 0)
                g1 = (kbase + jv == ntot - 1)
                self.mm(po, self.V[:, j, kv * 128:(kv + 1) * 128], pT, g0, g1, [self.r_V, rpT], por)
                self.mm(pd, self.ones_b, pT, g0, g1, [self.r_const, rpT], pdr)
        if half == 0:
            return
        rD = self.gv[:, 0:N2]
        self.P.op("dve", lambda e, pd=pd: e.reciprocal(out=rD, in_=pd), pdr, [self.r_gv])
        self.tt("dve", M2, po, rD, ALU.mult, por + [self.r_gv], [self.r_mix[h0], self.r_mix[h1]])

    def ffn_window(self, l, win, last):
        o0, o1, s0, s1 = win
        isctx = (s0 == 0)
        r_ = 1 if isctx else 0
        c0 = max(o0 - 1, s0)
        c1 = min(o1 + 1, s1)
        n = c1 - c0
        a = o0 - c0
        b = o1 - c0
        no = o1 - o0
        rxs = self.r_xM[c0 // 128:(c1 + 127) // 128]
        self.dma("sp", self.xsb[:, :, 0:n], self.xMs[:, :, c0:c1].rearrange("k p n -> p k n"), self.st_ld, rxs, [self.r_x])
        self.norm(self.xsb, self.r_x, n, self.G2[:, r_, :], self.modT[:, r_, 48:64], [self.r_mod],
                  lambda kc: self.hT[:, kc, 0:n], [self.r_h], self.ntmp)
        rhs = lambda kc: self.hT[:, kc, 0:n]
        reqs = [("sp", self.wup_s[l][t], [self.r_wsc[l]["up"]], 8192) for t in range(22)]
        reqs += [("sp", self.wdn_s[l][m], [self.r_wsc[l]["dn"]], FC * 128) for m in range(16)]
        ws = self.wstream(reqs)
        cw = self.convw
        cb = self.convb
        rp = [self.r_par]
        for t in range(22):
            buf, rw = next(ws)
            wt = buf[:, 0:8192].rearrange("p (k j) -> p k j", k=KC)
            for i in range(2):
                j = 2 * t + i
                pg, pgr = self.proj_fm(wt, rw, KC, i * 128, rhs, [self.r_h], n, 0)
                pv, pvr = self.proj_fm(wt, rw, KC, 256 + i * 128, rhs, [self.r_h], n, 1)
                bi = j % 2
                for (ps, pr, acc, racc, fc) in [(pg, pgr, self.accg[bi], self.r_accg[bi], j), (pv, pvr, self.accv[bi], self.r_accv[bi], FC + j)]:
                    self.act(acc[:, a:b], ps[:, a:b], AF.Identity, pr + rp, [racc], bias=cb[:, l, fc:fc + 1], scale=cw[:, l, 1, fc:fc + 1])
                    lo = max(a, 1)
                    self.stt(acc[:, lo:b], ps[:, lo - 1:b - 1], cw[:, l, 0, fc:fc + 1], acc[:, lo:b], ALU.mult, ALU.add, pr + rp + [racc], [racc])
                    hi = min(b, n - 1)
                    self.stt(acc[:, a:hi], ps[:, a + 1:hi + 1], cw[:, l, 2, fc:fc + 1], acc[:, a:hi], ALU.mult, ALU.add, pr + rp + [racc], [racc])
                self.act(self.sil[bi][:, a:b], self.accg[bi][:, a:b], AF.Silu, [self.r_accg[bi]], [self.r_sil[bi]])
                self.tt("pool", self.aT[:, j, 0:no], self.sil[bi][:, a:b], self.accv[bi][:, a:b], ALU.mult,
                        [self.r_sil[bi], self.r_accv[bi]], [self.r_aT[j]])
        for m in range(16):
            buf, rw = next(ws)
            wt = buf[:, 0:FC * 128].rearrange("p (k j) -> p k j", k=FC)
            ps, pr = self.PS(m % 2, 0, no)
            for j in range(FC):
                self.mm(ps, wt[:, j, :], self.aT[:, j, 0:no], j == 0, j == FC - 1, [rw, self.r_aT[j]], pr)
            self.stt(self.xsb[:, m, a:b], ps, self.modT[:, r_, 80 + m:81 + m], self.xsb[:, m, a:b], ALU.mult, ALU.add,
                     pr + [self.r_x, self.r_mod], [self.r_x])
        ros = self.r_xT[o0 // 128:(o1 + 127) // 128]
        if not last:
            self.dma("act", self.xTs[:, :, o0:o1].rearrange("k p n -> p k n"), self.xsb[:, :, a:b], self.st_st, [self.r_x], ros)
        else:
            sq, rsq, tm, rtm, rstd, rrstd = self.ntmp
            pn, pnr = self.PS(2, 0, no)
            for kc in range(KC):
                i = kc % 2
                self.act(sq[i][:, 0:no], self.xsb[:, kc, a:b], AF.Square, [self.r_x], [rsq[i]])
                self.mm(pn, self.ones_f, sq[i][:, 0:no], kc == 0, kc == KC - 1, [rsq[i], self.r_const], pnr)
            self.act(rstd[:, 0:no], pn, AF.Sqrt, pnr, [rrstd], bias=EPS, scale=1.0 / D)
            self.P.op("dve", lambda e: e.reciprocal(out=rstd[:, 0:no], in_=rstd[:, 0:no]), [rrstd], [rrstd])
            for kc in range(KC):
                self.stt(self.xsb[:, kc, a:b], self.xsb[:, kc, a:b], self.fng[:, kc:kc + 1], rstd[:, 0:no], ALU.mult, ALU.mult,
                         [self.r_x, rrstd, self.r_par], [self.r_x])
            self.dma("act", self.outT[:, :, o0 - LC:o1 - LC].rearrange("k p n -> p k n"), self.xsb[:, :, a:b], self.st_st,
                     [self.r_x], [self.r_out], join=True)


def _consts():
    n_freq = 32
    inv = (np.float32(10000.0) ** (-np.arange(n_freq, dtype=np.float32) / np.float32(n_freq))).astype(np.float32)
    t = np.arange(T)
    row = (t // 64).astype(np.float32)
    col = (t % 64).astype(np.float32)
    ang = np.concatenate([row[:, None] * inv[None, :], col[:, None] * inv[None, :]], axis=-1).astype(np.float32)
    cos = np.cos(ang).astype(np.float32)
    sin = np.sin(ang).astype(np.float32)
    ropeT = np.stack([np.repeat(cos, 2, axis=1).T, np.repeat(sin, 2, axis=1).T]).astype(np.float32)
    rotT = np.zeros((128, 128), np.float32)
    for i in range(64):
        rotT[2 * i + 1, 2 * i] = -1.0
        rotT[2 * i, 2 * i + 1] = 1.0
    ident = np.eye(128).astype(ml_dtypes.bfloat16)
    s = np.arange(128)[:, None]
    tt = np.arange(128)[None, :]
    mA = ((s // 32 == tt // 32) & (s <= tt))
    mB = ((s // 64 == tt // 64) & (s % 64 < 32) & (tt % 64 >= 32))
    mC = ((s < 64) & (tt >= 64))
    masks = np.stack([mA, mB, mC, mA.T, mB.T, mC.T], axis=1).astype(np.float32)
    seg = np.ones((128, 3, 128), np.float32)
    seg[:, 0, ::32] = 0
    seg[:, 1, ::64] = 0
    seg[:, 2, 0] = 0
    return dict(ropeT=np.ascontiguousarray(ropeT), rotT=rotT, ident=ident, identf=np.eye(128, dtype=np.float32), masks=np.ascontiguousarray(masks), seg=seg)


def _layout(inputs, b, NL=4):
    f = lambda a: np.ascontiguousarray(a, dtype=np.float32)
    x = inputs["x"][b]
    ctx = inputs["ctx"][b]
    xc = np.concatenate([ctx, x], axis=0)
    m = {}
    m["xT0"] = f(xc.T.reshape(KC, 128, U))
    cc = np.stack([inputs["c"][b], inputs["c_ctx"]], axis=-1)
    m["cT"] = f(cc.reshape(KC, 128, 2).transpose(1, 0, 2))
    for k in ("w_ada", "w_in", "w_out", "w_up", "w_down"):
        m[k] = f(inputs[k][:NL])
    m["badaT"] = f(inputs["b_ada"].reshape(4, 96, 128).transpose(2, 0, 1))
    m["n1g"] = f(inputs["norm1_g"].reshape(4, KC, 128).transpose(2, 0, 1))
    m["n2g"] = f(inputs["norm2_g"].reshape(4, KC, 128).transpose(2, 0, 1))
    m["fng"] = f(inputs["final_norm_g"].reshape(KC, 128).T)
    m["qkg"] = f(np.stack([inputs["q_norm_g"], inputs["k_norm_g"]], axis=-1).transpose(1, 0, 2))
    m["hgg"] = f(inputs["hg_norm_g"].T)
    m["lbp"] = f(inputs["hg_lower_bounds"].reshape(2, 4, 4, 128).transpose(3, 0, 1, 2))
    m["sgg"] = f(inputs["sg_norm_g"].reshape(4, 1, 512))
    m["sgb"] = f(inputs["sg_b"].reshape(4, 1, 512))
    m["sgwT"] = np.ascontiguousarray(inputs["sg_w"].transpose(0, 3, 1, 2), dtype=np.float32)
    m["convw"] = f(inputs["conv_w"].reshape(4, 3, 88, 128).transpose(3, 0, 1, 2))
    m["convb"] = f(inputs["conv_b"].reshape(4, 88, 128).transpose(2, 0, 1))
    return m


_NC_CACHE = {}


def kernel(**inputs):
    inputs = {k: np.asarray(v) for k, v in inputs.items()}
    if "nc" not in _NC_CACHE:
        _NC_CACHE["nc"] = Builder(4, False).build()
    nc = _NC_CACHE["nc"]
    consts = _consts()
    in_maps = []
    for b in range(N_CORES):
        m = _layout(inputs, b)
        m.update(consts)
        in_maps.append(m)
    res = run_bass_kernel_spmd(nc, in_maps, core_ids=list(range(N_CORES)))
    out = np.empty((4, T, D), np.float32)
    for b in range(4):
        out[b] = res.results[b]["outT"].reshape(D, T).T
    return out
```
